# Optimizing a Trainium2 kernel written in Bass

```python
import jax, jax.numpy as jnp
from jax import lax
import numpy as np

D_MODEL = 2048
BATCH = 4
SEQ = 2048
DEPTH = 4

N_MIXERS = 3
N_A = len(range(0, DEPTH, N_MIXERS))
N_B = len(range(1, DEPTH, N_MIXERS))
N_C = len(range(2, DEPTH, N_MIXERS))
EPS = 1e-6
N_MOD = 9
D_FF = 128 * ((8 * D_MODEL // 3 + 127) // 128)
CONV_WIDTH = 31
GLA_HEADS = 4
GLA_DK = D_MODEL // 2
GLA_DV = D_MODEL
GLA_HK = GLA_DK // GLA_HEADS
GLA_HV = GLA_DV // GLA_HEADS
GLA_GATE_RANK = 16
GLA_GATE_TEMP = 16.0
GLA_CHUNK = 64
POOL_WINDOWS = (2, 4, 8, 16)
POOL_GROUP = D_MODEL // len(POOL_WINDOWS)

kernel_name = "hybrid_conv_gla_pool_macaron_encoder"


def rmsnorm(x, g):
    xf = x.astype(jnp.float32)
    y = xf * lax.rsqrt(jnp.mean(xf * xf, axis=-1, keepdims=True) + EPS)
    return (y * g.astype(jnp.float32)).astype(x.dtype)


def layernorm(x, g, b):
    xf = x.astype(jnp.float32)
    mu = jnp.mean(xf, axis=-1, keepdims=True)
    var = jnp.mean(jnp.square(xf - mu), axis=-1, keepdims=True)
    y = (xf - mu) * lax.rsqrt(var + EPS)
    return (y * g.astype(jnp.float32) + b.astype(jnp.float32)).astype(x.dtype)


def modulate(x, g, shift, scale):
    return rmsnorm(x, g) * (1 + scale) + shift


def swiglu(h, w_gate, w_up, w_down):
    return (jax.nn.silu(h @ w_gate) * (h @ w_up)) @ w_down


def conformer_conv(h, w_in, b_in, w_dw, b_dw, ln_g, ln_b, w_out, b_out):
    u = h @ w_in + b_in
    a, gte = jnp.split(u, 2, axis=-1)
    u = a * jax.nn.sigmoid(gte)
    pad = CONV_WIDTH // 2
    u = lax.conv_general_dilated(u, w_dw[:, None, :], (1,), [(pad, pad)],
                                 dimension_numbers=("NWC", "WIO", "NWC"),
                                 feature_group_count=D_MODEL) + b_dw
    u = jax.nn.silu(layernorm(u, ln_g, ln_b))
    return u @ w_out + b_out


def _gla_scan(q, k, v, logg, strict):
    bn, nh, t_len, dk = q.shape
    dv = v.shape[-1]
    n_chunks = t_len // GLA_CHUNK

    def to_chunks(a):
        return jnp.moveaxis(a.reshape(bn, nh, n_chunks, GLA_CHUNK, a.shape[-1]), 2, 0)

    idx = jnp.arange(GLA_CHUNK)
    mask = (idx[None, :] < idx[:, None]) if strict else (idx[None, :] <= idx[:, None])

    def step(state, inp):
        qc, kc, vc, gc = inp
        b = jnp.cumsum(gc, axis=-2)
        inter = jnp.einsum("bhtd,bhde->bhte", qc * jnp.exp(b), state)
        diff = jnp.where(mask[:, :, None], b[..., :, None, :] - b[..., None, :, :], -jnp.inf)
        scores = jnp.einsum("bhtsd,bhsd->bhts", qc[..., :, None, :] * jnp.exp(diff), kc)
        intra = jnp.einsum("bhts,bhse->bhte", scores, vc)
        b_last = b[..., -1:, :]
        state = (jnp.exp(b_last[..., 0, :])[..., None] * state
                 + jnp.einsum("bhsd,bhse->bhde", kc * jnp.exp(b_last - b), vc))
        return state, inter + intra

    s0 = jnp.zeros((bn, nh, dk, dv), jnp.float32)
    _, out = lax.scan(step, s0, (to_chunks(q), to_chunks(k), to_chunks(v), to_chunks(logg)))
    return jnp.moveaxis(out, 0, 2).reshape(bn, nh, t_len, dv)


def gla_mixer(h, w_in, wa1, wa2, ba, norm_g, w_out):
    bn, t_len, _ = h.shape
    proj = h @ w_in
    q, k, v, g = jnp.split(proj, [GLA_DK, 2 * GLA_DK, 2 * GLA_DK + GLA_DV], axis=-1)

    def heads(a, dh):
        return a.reshape(bn, t_len, GLA_HEADS, dh).transpose(0, 2, 1, 3).astype(jnp.float32)

    qh = heads(q, GLA_HK) * (GLA_HK ** -0.5)
    kh = heads(k, GLA_HK)
    vh = heads(v, GLA_HV)

    def log_decay(d):
        z = ((h @ wa1[d]) @ wa2[d] + ba[d]).astype(jnp.float32)
        return heads(jax.nn.log_sigmoid(z) / GLA_GATE_TEMP, GLA_HK)

    flip = lambda a: jnp.flip(a, axis=2)
    o_fwd = _gla_scan(qh, kh, vh, log_decay(0), strict=False)
    o_bwd = flip(_gla_scan(flip(qh), flip(kh), flip(vh), flip(log_decay(1)), strict=True))
    o = (o_fwd + o_bwd).transpose(0, 2, 1, 3)
    o = rmsnorm(o, norm_g).reshape(bn, t_len, GLA_DV).astype(h.dtype)
    return (o * jax.nn.silu(g)) @ w_out


def pool_mixer(h, w_grp, b_grp, scale):
    bn, t_len, _ = h.shape
    hf = h.astype(jnp.float32)
    prefix = jnp.concatenate([jnp.zeros((bn, 1, D_MODEL), jnp.float32), jnp.cumsum(hf, axis=1)], axis=1)
    t = jnp.arange(t_len)
    outs = []
    for gi, w in enumerate(POOL_WINDOWS):
        sl = slice(gi * POOL_GROUP, (gi + 1) * POOL_GROUP)
        lo = jnp.maximum(t - w // 2, 0)
        hi = jnp.minimum(t + (w - w // 2) - 1, t_len - 1)
        pg = prefix[..., sl]
        mean = (jnp.take(pg, hi + 1, axis=1) - jnp.take(pg, lo, axis=1)) / (hi - lo + 1).astype(jnp.float32)[None, :, None]
        d = (mean - hf[..., sl]).astype(h.dtype)
        outs.append(d @ w_grp[gi] + b_grp[gi])
    return jnp.concatenate(outs, axis=-1) * scale


def setup_inputs(seed: int = 0) -> dict:
    key = jax.random.key(seed)
    ks = iter(jax.random.split(key, 40))

    def nrm(shape, std):
        return jax.random.normal(next(ks), shape, jnp.float32) * std

    D, F = D_MODEL, D_FF
    gla_in = 2 * GLA_DK + 2 * GLA_DV
    return {
        "x": nrm((BATCH, SEQ, D), 1.0),
        "c": nrm((BATCH, D), 1.0),
        "ada_w": nrm((DEPTH, D, N_MOD * D), 0.5 * D ** -0.5),
        "ada_b": nrm((DEPTH, N_MOD * D), 0.02),
        "norm_g": 1.0 + nrm((DEPTH, 3, D), 0.02),
        "ffn_w_gate": nrm((DEPTH, 2, D, F), D ** -0.5),
        "ffn_w_up": nrm((DEPTH, 2, D, F), D ** -0.5),
        "ffn_w_down": nrm((DEPTH, 2, F, D), F ** -0.5),
        "conv_w_in": nrm((N_A, D, 2 * D), D ** -0.5),
        "conv_b_in": nrm((N_A, 2 * D), 0.02),
        "conv_w_dw": nrm((N_A, CONV_WIDTH, D), CONV_WIDTH ** -0.5),
        "conv_b_dw": nrm((N_A, D), 0.02),
        "conv_ln_g": 1.0 + nrm((N_A, D), 0.02),
        "conv_ln_b": nrm((N_A, D), 0.02),
        "conv_w_out": nrm((N_A, D, D), D ** -0.5),
        "conv_b_out": nrm((N_A, D), 0.02),
        "gla_w_in": nrm((N_B, D, gla_in), D ** -0.5),
        "gla_wa1": nrm((N_B, 2, D, GLA_GATE_RANK), D ** -0.5),
        "gla_wa2": nrm((N_B, 2, GLA_GATE_RANK, GLA_DK), GLA_GATE_RANK ** -0.5),
        "gla_ba": nrm((N_B, 2, GLA_DK), 0.1),
        "gla_norm_g": 1.0 + nrm((N_B, GLA_HV), 0.02),
        "gla_w_out": nrm((N_B, GLA_DV, D), GLA_DV ** -0.5),
        "pool_w": nrm((N_C, len(POOL_WINDOWS), POOL_GROUP, POOL_GROUP), POOL_GROUP ** -0.5),
        "pool_b": nrm((N_C, len(POOL_WINDOWS), POOL_GROUP), 0.02),
        "pool_scale": 1.0 + nrm((N_C, D), 0.1),
        "final_g": 1.0 + nrm((D,), 0.02),
    }


def reference(x, c, ada_w, ada_b, norm_g, ffn_w_gate, ffn_w_up, ffn_w_down,
              conv_w_in, conv_b_in, conv_w_dw, conv_b_dw, conv_ln_g, conv_ln_b, conv_w_out, conv_b_out,
              gla_w_in, gla_wa1, gla_wa2, gla_ba, gla_norm_g, gla_w_out,
              pool_w, pool_b, pool_scale, final_g):
    c_act = jax.nn.silu(c)
    for i in range(DEPTH):
        mods = (c_act @ ada_w[i] + ada_b[i])[:, None, :]
        sh1, sc1, gt1, sh2, sc2, gt2, sh3, sc3, gt3 = jnp.split(mods, N_MOD, axis=-1)

        h = modulate(x, norm_g[i, 0], sh1, sc1)
        x = x + 0.5 * gt1 * swiglu(h, ffn_w_gate[i, 0], ffn_w_up[i, 0], ffn_w_down[i, 0])

        h = modulate(x, norm_g[i, 1], sh2, sc2)
        kind, j = i % N_MIXERS, i // N_MIXERS
        if kind == 0:
            y = conformer_conv(h, conv_w_in[j], conv_b_in[j], conv_w_dw[j], conv_b_dw[j],
                               conv_ln_g[j], conv_ln_b[j], conv_w_out[j], conv_b_out[j])
        elif kind == 1:
            y = gla_mixer(h, gla_w_in[j], gla_wa1[j], gla_wa2[j], gla_ba[j], gla_norm_g[j], gla_w_out[j])
        else:
            y = pool_mixer(h, pool_w[j], pool_b[j], pool_scale[j])
        x = x + gt2 * y

        h = modulate(x, norm_g[i, 2], sh3, sc3)
        x = x + 0.5 * gt3 * swiglu(h, ffn_w_gate[i, 1], ffn_w_up[i, 1], ffn_w_down[i, 1])
    return rmsnorm(x, final_g)
```

```python
import contextlib
import numpy as np
import concourse.bass as bass
import concourse.mybir as mybir
from concourse.bass_utils import run_bass_kernel_spmd

F32 = mybir.dt.float32
BF16 = mybir.dt.bfloat16
AF = mybir.ActivationFunctionType
ALU = mybir.AluOpType
AX = mybir.AxisListType

NCORES = 8
D = 2048
KC = 16
T = 1024
SEQ = 2048
DEPTH = 4
FF = 5504
FC = 43
EPS = 1e-6
CONVW = 31
SEM_LIMIT = 12000
ADA_SPLIT = True
FFN_ON = True
GLA_STOP = 0
PAIRS = [[0, 1], [2, 3], [4, 5], [6, 7]]


class Buf:
    __slots__ = ("w", "r", "name")

    def __init__(self, name=""):
        self.w = None
        self.r = []
        self.name = name


class DmaSem:
    def __init__(self, name):
        self.name = name
        self.sem = None
        self.n = 0
        self.k = 0

    def advance(self, P):
        if self.sem is None or self.n >= SEM_LIMIT * 4:
            self.sem = P._alloc_sem(f"d_{self.name}_{self.k}")
            self.k += 1
            self.n = 0
        self.n += 16


class Prog:
    ENGS = ("sp", "act", "pe", "dve", "pool")

    def __init__(self, nc):
        self.nc = nc
        self.streams = {e: [] for e in self.ENGS}
        self.cur_sem = {}
        self.cnt = {}
        self.seen = {e: {} for e in self.ENGS}
        self.nsem = 0
        self._sem_cms = []
        self.last_tok = {}
        self.dma_last = {}
        for e in self.ENGS:
            self._new_eng_sem(e)

    def _alloc_sem(self, name):
        cm = self.nc.semaphore(name)
        s = cm.__enter__()
        self._sem_cms.append(cm)
        self.nsem += 1
        return s

    def _new_eng_sem(self, e):
        self.cur_sem[e] = self._alloc_sem(f"s_{e}_{self.nsem}")
        self.cnt[e] = 0

    def close(self):
        for cm in reversed(self._sem_cms):
            cm.__exit__(None, None, None)

    def _waits(self, eng, toks):
        out = []
        seen = self.seen[eng]
        for t in toks:
            if t is None:
                continue
            sem, val, src = t
            if src == "pe" and eng == "pe":
                continue
            key = id(sem)
            if seen.get(key, (None, 0))[1] >= val:
                continue
            seen[key] = (sem, val)
            out.append((sem, val))
        return out

    def _deps(self, rd, wr, waits):
        toks = list(waits)
        for b in rd:
            toks.append(b.w)
        for b in wr:
            toks.append(b.w)
            toks.extend(b.r)
        return toks

    def _commit(self, tok, rd, wr):
        for b in rd:
            b.r.append(tok)
        for b in wr:
            b.w = tok
            b.r = []

    def op(self, eng, fn, rd=(), wr=(), waits=(), sig=True):
        ws = self._waits(eng, self._deps(rd, wr, waits))
        tok = None
        if sig:
            if self.cnt[eng] >= SEM_LIMIT:
                self._new_eng_sem(eng)
            self.cnt[eng] += 1
            tok = (self.cur_sem[eng], self.cnt[eng], eng)
            self.last_tok[eng] = tok
        self.streams[eng].append((ws, fn, (tok[0], 1) if tok else None))
        if tok is not None:
            self._commit(tok, rd, wr)
        return tok

    def dma(self, eng, out, in_, dsem, rd=(), wr=(), waits=(), **kw):
        ws = self._waits(eng, self._deps(rd, wr, waits))
        dsem.advance(self)
        tok = (dsem.sem, dsem.n, "dma")
        self.dma_last[id(dsem.sem)] = tok
        self.streams[eng].append((ws, (lambda e: e.dma_start(out=out, in_=in_, **kw)), (dsem.sem, 16)))
        self._commit(tok, rd, wr)
        return tok

    def wait_only(self, eng, toks):
        ws = self._waits(eng, toks)
        if ws:
            self.streams[eng].append((ws, None, None))

    def barrier(self):
        toks = list(self.last_tok.values()) + list(self.dma_last.values())
        for e in self.ENGS:
            ws = self._waits(e, [t for t in toks if not (t[2] == e and e != "pe")] +
                             [t for t in toks if t[2] == e])
            if ws:
                self.streams[e].append((ws, None, None))

    def run(self, eng, h):
        for ws, fn, inc in self.streams[eng]:
            for sem, val in ws:
                h.wait_ge(sem, val)
            if fn is not None:
                ins = fn(h)
                if inc is not None:
                    ins.then_inc(inc[0], inc[1])


def emit_block(nc, P):
    with nc.Block() as block:
        @block.sync
        def _(e):
            P.run("sp", e)

        @block.scalar
        def _(e):
            P.run("act", e)

        @block.tensor
        def _(e):
            P.run("pe", e)

        @block.vector
        def _(e):
            P.run("dve", e)

        @block.gpsimd
        def _(e):
            P.run("pool", e)


def smalls_layout():
    off = {}
    n = 0

    def add(name, cols):
        nonlocal n
        off[name] = (n, cols)
        n += cols

    add("c", 16)
    for L in range(DEPTH):
        for s in range(3):
            add(f"ng{L}_{s}", 16)
        add(f"adab{L}", 144)
    add("final_g", 16)
    for j in range(2):
        add(f"cbin{j}", 32)
        add(f"cwdw{j}", 16 * CONVW)
        add(f"cbdw{j}", 16)
        add(f"clng{j}", 16)
        add(f"clnb{j}", 16)
        add(f"cbout{j}", 16)
    add("gba", 16)
    add("gng", 4)
    add("pb", 16)
    add("pscale", 16)
    add("ident", 128)
    add("mask_f", 64)
    add("mask_b", 64)
    add("is_odd", 1)
    add("is_even", 1)
    return off, n


def col(v):
    v = np.asarray(v, np.float32).reshape(-1, 128)
    return np.ascontiguousarray(v.T)


def pack_smalls(inp, core):
    off, n = smalls_layout()
    b = core // 2
    S = np.zeros((128, n), np.float32)

    def put(name, arr):
        o, c = off[name]
        assert arr.shape == (128, c), (name, arr.shape, c)
        S[:, o:o + c] = arr

    put("c", col(inp["c"][b]))
    for L in range(DEPTH):
        for s in range(3):
            put(f"ng{L}_{s}", col(inp["norm_g"][L, s]))
        put(f"adab{L}", col(inp["ada_b"][L]))
    put("final_g", col(inp["final_g"]))
    for j in range(2):
        put(f"cbin{j}", col(inp["conv_b_in"][j]))
        w = np.asarray(inp["conv_w_dw"][j], np.float32)
        w = w.reshape(CONVW, 16, 128).transpose(2, 1, 0).reshape(128, 16 * CONVW)
        put(f"cwdw{j}", w)
        put(f"cbdw{j}", col(inp["conv_b_dw"][j]))
        put(f"clng{j}", col(inp["conv_ln_g"][j]))
        put(f"clnb{j}", col(inp["conv_ln_b"][j]))
        put(f"cbout{j}", col(inp["conv_b_out"][j]))
    put("gba", col(inp["gla_ba"][0].reshape(-1)))
    put("gng", col(inp["gla_norm_g"][0]))
    put("pb", col(inp["pool_b"][0].reshape(-1)))
    put("pscale", col(inp["pool_scale"][0]))
    put("ident", np.eye(128, dtype=np.float32))
    mf = np.zeros((128, 64), np.float32)
    mb = np.zeros((128, 64), np.float32)
    si = np.arange(64)[:, None]
    ti = np.arange(64)[None, :]
    mf[:64] = (si <= ti)
    mb[:64] = (si > ti)
    put("mask_f", mf)
    put("mask_b", mb)
    S[:, off["is_odd"][0]] = float(core % 2)
    S[:, off["is_even"][0]] = float(1 - core % 2)
    return S


class K:
    pass


def build_program(layers, first, last, mixers=True):
    nc = bass.Bass("TRN2", target_bir_lowering=False)
    soff, NS = smalls_layout()
    k = K()
    k.nc = nc
    k.soff = soff
    dt = nc.dram_tensor
    k.xin = dt("xT_in", [D, T], F32, kind="ExternalInput").ap()
    k.smalls_d = dt("smalls", [128, NS], F32, kind="ExternalInput").ap()
    nl = len(layers)
    k.li = {L: i for i, L in enumerate(layers)}
    k.ada_w = dt("ada_w", [nl, D, 9216 if ADA_SPLIT else 9 * D], F32, kind="ExternalInput").ap()
    k.wg = dt("ffn_w_gate", [nl, 2, D, FF], F32, kind="ExternalInput").ap()
    k.wu = dt("ffn_w_up", [nl, 2, D, FF], F32, kind="ExternalInput").ap()
    k.wd = dt("ffn_w_down", [nl, 2, FF, D], F32, kind="ExternalInput").ap()
    convL = [L for L in layers if L % 3 == 0 and mixers]
    k.ci = {L: i for i, L in enumerate(convL)}
    if convL:
        k.conv_w_in = dt("conv_w_in", [len(convL), D, 2 * D], F32, kind="ExternalInput").ap()
        k.conv_w_out = dt("conv_w_out", [len(convL), D, D], F32, kind="ExternalInput").ap()
    if 1 in layers and mixers:
        k.gla_w_in = dt("gla_w_in", [1, D, 6144], F32, kind="ExternalInput").ap()
        k.gla_wa1 = dt("gla_wa1", [1, 2, D, 16], F32, kind="ExternalInput").ap()
        k.gla_wa2 = dt("gla_wa2", [1, 2, 16, 1024], F32, kind="ExternalInput").ap()
        k.gla_w_out = dt("gla_w_out", [1, D, D], F32, kind="ExternalInput").ap()
    if 2 in layers and mixers:
        k.pool_w = dt("pool_w", [1, 4, 512, 512], F32, kind="ExternalInput").ap()
    k.out = dt("outT", [D, T], F32, kind="ExternalOutput").ap()

    P = Prog(nc)
    k.P = P
    with contextlib.ExitStack() as es:
        def sb(name, shape, dtype):
            return es.enter_context(nc.sbuf_tensor(name, shape, dtype))

        k.xT = sb("xT", [128, KC, T], F32)
        k.smalls = sb("smalls_sb", [128, NS], F32)
        k.mods = sb("mods", [128, DEPTH, 144], F32)
        k.modA = sb("modA", [128, DEPTH, 3, KC], F32)
        k.modG = sb("modG", [128, DEPTH, 3, KC], F32)
        k.ones = sb("ones", [128, 128], F32)
        k.epsc = sb("epsc", [128, 1], F32)
        k.zeroc = sb("zeroc", [128, 1], F32)
        k.cact = sb("cact", [128, KC], BF16)
        k.identb = sb("identb", [128, 128], BF16)
        k.ctmp = sb("ctmp", [128, KC], F32)
        ARENA_WORDS = (nc.sbuf_bytes_remaining - 128) // 4
        k.arena = sb("arena", [128, ARENA_WORDS], F32)
        k.arena_bytes = ARENA_WORDS * 4
        k.pb = [es.enter_context(nc.psum_tensor(f"pb{i}", [128, 512], F32)) for i in range(8)]
        k.bpb = [Buf(f"pb{i}") for i in range(8)]
        k.bx = Buf("x")
        k.bsm = Buf("smalls")
        k.bmods = Buf("mods")
        k.bconst = Buf("const")
        k.dsem = {}

        prologue(k, layers)
        for L in layers:
            if FFN_ON:
                ffn_stage(k, L, 0)
            if mixers:
                kind = L % 3
                if kind == 0:
                    conv_stage(k, L, L // 3)
                elif kind == 1:
                    gla_stage(k, L)
                else:
                    pool_stage(k, L)
            if FFN_ON:
                ffn_stage(k, L, 1)
        epilogue(k, last)
        emit_block(nc, P)
    P.close()
    return nc


def carve(k, byte_off, shape, dtype):
    n = int(np.prod(shape[1:]))
    esz = 4 if dtype == F32 else 2
    nbytes = n * esz
    assert byte_off % 4 == 0 and nbytes % 4 == 0
    assert byte_off + nbytes <= k.arena_bytes, (byte_off, nbytes, k.arena_bytes)
    ap = k.arena[:, byte_off // 4:(byte_off + nbytes) // 4]
    if dtype != F32:
        ap = ap.bitcast(dtype)
    if len(shape) == 3:
        ap = ap.rearrange("p (a b) -> p a b", a=shape[1])
    elif len(shape) == 4:
        ap = ap.rearrange("p (a b c) -> p a b c", a=shape[1], b=shape[2])
    return ap


def sm(k, name, j0=0, n=None):
    o, c = k.soff[name]
    if n is None:
        n = c - j0
    return k.smalls[:, o + j0:o + j0 + n]


def getds(k, name):
    if name not in k.dsem:
        k.dsem[name] = DmaSem(name)
    return k.dsem[name]


def prologue(k, layers):
    P, nc = k.P, k.nc
    P.dma("sp", k.smalls[:, :], k.smalls_d[:, :], getds(k, "sm"), wr=[k.bsm])
    xv = k.xin.rearrange("(kc p) t -> p kc t", p=128)
    for q in range(4):
        P.dma("sp", k.xT[:, q * 4:(q + 1) * 4, :], xv[:, q * 4:(q + 1) * 4, :], getds(k, "xin"), wr=[k.bx])
    P.op("dve", lambda e: e.memset(k.ones[:, :], 1.0), wr=[k.bconst])
    P.op("dve", lambda e: e.memset(k.epsc[:, :], EPS), wr=[k.bconst])
    P.op("dve", lambda e: e.memset(k.zeroc[:, :], 0.0), wr=[k.bconst])
    P.op("dve", lambda e: e.tensor_copy(out=k.identb[:, :], in_=sm(k, "ident")), rd=[k.bsm], wr=[k.bconst])
    P.op("act", lambda e: e.activation(out=k.ctmp[:, :], in_=sm(k, "c"), func=AF.Silu), rd=[k.bsm], wr=[k.bconst])
    P.op("dve", lambda e: e.tensor_copy(out=k.cact[:, :], in_=k.ctmp[:, :]), rd=[k.bconst], wr=[k.bconst])
    if ADA_SPLIT:
        mods_split(k, layers)
    else:
        mods_local(k, layers)


def mods_finish(k, L):
    P = k.P
    for s3 in range(3):
        P.op("dve", (lambda e, L=L, s3=s3: e.scalar_tensor_tensor(
            out=k.modA[:, L, s3, :], in0=k.mods[:, L, (3 * s3 + 1) * 16:(3 * s3 + 2) * 16], scalar=1.0,
            in1=sm(k, f"ng{L}_{s3}"), op0=ALU.add, op1=ALU.mult)), rd=[k.bmods, k.bsm], wr=[k.bmods])
        P.op("dve", (lambda e, L=L, s3=s3: e.tensor_scalar(
            out=k.modG[:, L, s3, :], in0=k.mods[:, L, (3 * s3 + 2) * 16:(3 * s3 + 3) * 16],
            scalar1=(1.0 if s3 == 1 else 0.5), scalar2=None, op0=ALU.mult)), rd=[k.bmods], wr=[k.bmods])


def mods_split(k, layers):
    P, nc = k.P, k.nc
    nl = len(layers)
    MS = carve(k, 0, [128, nl * 72], F32)
    WB0 = 4096
    wslots = [carve(k, WB0 + s * 16384, [128, KC, 512], BF16) for s in range(2)]
    bw = [Buf(), Buf()]
    dsw = [getds(k, "adaw0"), getds(k, "adaw1")]
    bMS, bcin, bcout = Buf(), Buf(), Buf()
    NBLK = 18
    blocks = [(li, b) for li in range(nl) for b in range(NBLK)]
    psm = k.pb[0]

    def load(i):
        li, b = blocks[i]
        s = i % 2
        src = k.ada_w[li].rearrange("(kc p) n -> p kc n", p=128)[:, :, b * 512:(b + 1) * 512]
        P.dma("pool", wslots[s][:, :, :], src, dsw[s], wr=[bw[s]])

    for i in range(min(2, len(blocks))):
        load(i)
    for i, (li, b) in enumerate(blocks):
        s = i % 2
        for dc in range(4):
            c0 = li * 72 + b * 4 + dc
            for kc in range(KC):
                lastk = kc == KC - 1
                P.op("pe", (lambda e, s=s, dc=dc, kc=kc, c0=c0: e.matmul(
                    psm[:, c0:c0 + 1], lhsT=wslots[s][:, kc, dc * 128:(dc + 1) * 128],
                    rhs=k.cact[:, kc:kc + 1], start=(kc == 0), stop=(kc == KC - 1))),
                    rd=[bw[s], k.bconst] if kc == 0 or lastk else [],
                    wr=[k.bpb[0]] if (lastk and dc == 3) else [],
                    waits=(([k.bpb[0].w] + k.bpb[0].r) if (kc == 0 and dc == 0 and i == 0) else []),
                    sig=(lastk and dc == 3))
        if i + 2 < len(blocks):
            load(i + 2)
    P.op("dve", lambda e: e.tensor_copy(out=MS[:, :], in_=psm[:, 0:nl * 72]), rd=[k.bpb[0]], wr=[bMS])
    cc_in = nc.dram_tensor("ada_cc_in", [128, nl * 72], F32).ap()
    cc_out = nc.dram_tensor("ada_cc_out", [256, nl * 72], F32).ap()
    P.dma("sp", cc_in, MS[:, :], getds(k, "adacc"), rd=[bMS], wr=[bcin])
    P.op("pool", lambda e: e.collective_compute("AllGather", ALU.bypass, replica_groups=PAIRS,
                                                ins=[cc_in.opt()], outs=[cc_out.opt()]), rd=[bcin], wr=[bcout])
    ccv = cc_out.rearrange("(r p) n -> p r n", p=128)
    for li, L in enumerate(layers):
        P.dma("sp", k.mods[:, L, :].rearrange("p (r j) -> p r j", r=2), ccv[:, :, li * 72:(li + 1) * 72],
              getds(k, "adacc"), rd=[bcout], wr=[k.bmods])
    for li, L in enumerate(layers):
        P.op("dve", (lambda e, L=L: e.tensor_tensor(out=k.mods[:, L, :], in0=k.mods[:, L, :],
                                                     in1=sm(k, f"adab{L}"), op=ALU.add)),
             rd=[k.bsm], wr=[k.bmods])
        mods_finish(k, L)


def mods_local(k, layers):
    P, nc = k.P, k.nc
    NBLK = 36
    wslots = [carve(k, s * 16384, [128, KC, 512], BF16) for s in range(2)]
    bw = [Buf("adaw0"), Buf("adaw1")]
    dsw = [getds(k, "adaw0"), getds(k, "adaw1")]
    blocks = [(L, b) for L in layers for b in range(NBLK)]
    psm = k.pb[0]

    def load(i):
        L, b = blocks[i]
        s = i % 2
        src = k.ada_w[k.li[L]].rearrange("(kc p) n -> p kc n", p=128)[:, :, b * 512:(b + 1) * 512]
        P.dma("pool", wslots[s][:, :, :], src, dsw[s], wr=[bw[s]])

    for i in range(min(2, len(blocks))):
        load(i)
    for i, (L, b) in enumerate(blocks):
        s = i % 2
        for dc in range(4):
            colj = (b * 4 + dc)
            for kc in range(KC):
                lastk = kc == KC - 1
                P.op("pe", (lambda e, s=s, dc=dc, kc=kc, colj=colj: e.matmul(
                    psm[:, colj:colj + 1], lhsT=wslots[s][:, kc, dc * 128:(dc + 1) * 128],
                    rhs=k.cact[:, kc:kc + 1], start=(kc == 0), stop=(kc == KC - 1))),
                    rd=[bw[s], k.bconst] if kc == 0 or lastk else [],
                    wr=[k.bpb[0]] if (lastk and dc == 3) else [],
                    waits=(([k.bpb[0].w] + k.bpb[0].r) if (kc == 0 and dc == 0 and b == 0) else []),
                    sig=(lastk and dc == 3))
        if b == NBLK - 1:
            P.op("dve", (lambda e, L=L: e.tensor_tensor(out=k.mods[:, L, :], in0=psm[:, 0:144],
                                                         in1=sm(k, f"adab{L}"), op=ALU.add)),
                 rd=[k.bpb[0], k.bsm], wr=[k.bmods])
            mods_finish(k, L)
        if i + 2 < len(blocks):
            load(i + 2)


def norm_modulate(k, A_fn, S_fn, out_fn, base, out_is_bf16=True, bout=None):
    P = k.P
    sq = [carve(k, base + i * 2048, [128, 512], F32) for i in range(2)]
    rstd = [carve(k, base + 4096 + i * 2048, [128, 512], F32) for i in range(2)]
    tmp = [carve(k, base + 8192 + i * 2048, [128, 512], F32) for i in range(2)]
    bsq = [Buf(), Buf()]
    brs = [Buf(), Buf()]
    btmp = [Buf(), Buf()]
    if bout is None:
        bout = Buf()
    for half in range(2):
        ps = k.pb[6 + half]
        bps = k.bpb[6 + half]
        hs = slice(half * 512, (half + 1) * 512)
        for kc in range(KC):
            i = kc % 2
            if kc % 2 == 0:
                P.op("act", (lambda e, kc=kc, i=i, hs=hs: e.activation(out=sq[i][:, :], in_=k.xT[:, kc, hs], func=AF.Square)),
                     rd=[k.bx], wr=[bsq[i]])
            else:
                P.op("dve", (lambda e, kc=kc, i=i, hs=hs: e.tensor_tensor(out=sq[i][:, :], in0=k.xT[:, kc, hs],
                                                                  in1=k.xT[:, kc, hs], op=ALU.mult)),
                     rd=[k.bx], wr=[bsq[i]])
            P.op("pe", (lambda e, kc=kc, i=i, ps=ps: e.matmul(ps[:, :], lhsT=k.ones[:, :], rhs=sq[i][:, :],
                                                             start=(kc == 0), stop=(kc == KC - 1))),
                 rd=[bsq[i], k.bconst], wr=[bps] if kc == KC - 1 else [],
                 waits=([bps.w] + bps.r) if kc == 0 else [], sig=True)
        P.op("act", (lambda e, half=half, ps=ps: e.activation(out=rstd[half][:, :], in_=ps[:, :], func=AF.Sqrt,
                                                              bias=k.epsc[:, 0:1], scale=1.0 / D)),
             rd=[bps, k.bconst], wr=[brs[half]])
        P.op("dve", (lambda e, half=half: e.reciprocal(out=rstd[half][:, :], in_=rstd[half][:, :])),
             rd=[brs[half]], wr=[brs[half]])
        for kc in range(KC):
            i = kc % 2
            P.op("dve", (lambda e, kc=kc, i=i, half=half, hs=hs: e.tensor_tensor(out=tmp[i][:, :], in0=k.xT[:, kc, hs],
                                                                         in1=rstd[half][:, :], op=ALU.mult)),
                 rd=[k.bx, brs[half]], wr=[btmp[i]])
            if S_fn is not None:
                P.op("act", (lambda e, kc=kc, i=i, half=half: e.activation(
                    out=out_fn(kc, half), in_=tmp[i][:, :], func=AF.Identity, bias=S_fn(kc), scale=A_fn(kc))),
                    rd=[btmp[i], k.bmods, k.bsm], wr=[bout])
            else:
                P.op("act", (lambda e, kc=kc, i=i, half=half: e.activation(
                    out=out_fn(kc, half), in_=tmp[i][:, :], func=AF.Identity, bias=k.zeroc[:, 0:1], scale=A_fn(kc))),
                    rd=[btmp[i], k.bmods, k.bsm], wr=[bout])
    return bout


def ffn_stage(k, L, which):
    P = k.P
    P.barrier()
    s3 = 0 if which == 0 else 2
    hT = carve(k, 0, [128, KC, T], BF16)
    WG = [carve(k, 32768 + s * 8192, [128, KC, 256], BF16) for s in range(2)]
    WU = [carve(k, 49152 + s * 8192, [128, KC, 256], BF16) for s in range(2)]
    WD = [carve(k, 65536 + s * 8192, [128, 8, 512], BF16) for s in range(2)]
    ACT = carve(k, 81920, [128, 8, T], BF16)
    SIL = [carve(k, 98304 + s * 2048, [128, 512], F32) for s in range(3)]
    NB = 104448
    bh = Buf("hT")
    bWG = [Buf(), Buf()]
    bWU = [Buf(), Buf()]
    bWD = [Buf(), Buf()]
    bsil = [Buf(), Buf(), Buf()]
    dWG = [getds(k, "wg0"), getds(k, "wg1")]
    dWU = [getds(k, "wu0"), getds(k, "wu1")]
    dWD = [getds(k, "wd0"), getds(k, "wd1")]
    wgv = k.wg[k.li[L], which].rearrange("(kc p) f -> p kc f", p=128)
    wuv = k.wu[k.li[L], which].rearrange("(kc p) f -> p kc f", p=128)
    wdv = k.wd[k.li[L], which].rearrange("(fc p) d -> p fc d", p=128)

    groups = [(g * 2, min(2, FC - g * 2)) for g in range((FC + 1) // 2)]
    phases = []
    for p0 in range(0, FC, 8):
        phases.append((p0, min(8, FC - p0)))
    gl = {"n": 0}

    def load_group(gi):
        f0, nf = groups[gi]
        s = gi % 2
        P.dma("pool", WG[s][:, :, 0:nf * 128], wgv[:, :, f0 * 128:(f0 + nf) * 128], dWG[s], wr=[bWG[s]])
        P.dma("pool", WU[s][:, :, 0:nf * 128], wuv[:, :, f0 * 128:(f0 + nf) * 128], dWU[s], wr=[bWU[s]])

    dl = {"n": 0}
    dlist = [(pi, dg) for pi in range(len(phases)) for dg in range(4)]

    def load_wd(i):
        pi, dg = dlist[i]
        p0, pn = phases[pi]
        s = i % 2
        P.dma("pool", WD[s][:, 0:pn, :], wdv[:, p0:p0 + pn, dg * 512:(dg + 1) * 512], dWD[s], wr=[bWD[s]])

    load_group(0)
    load_group(1)
    next_group = 2
    load_wd(0)
    load_wd(1)
    next_wd = 2
    norm_modulate(k, lambda kc: k.modA[:, L, s3, kc:kc + 1],
                  lambda kc: k.mods[:, L, (3 * s3) * 16 + kc:(3 * s3) * 16 + kc + 1],
                  lambda kc, half: hT[:, kc, half * 512:(half + 1) * 512], NB, bout=bh)
    unit = 0
    dunit = 0
    bact = Buf("act")
    gi = 0
    for pi, (p0, pn) in enumerate(phases):
        ngr = (pn + 1) // 2
        for _ in range(ngr):
            f0, nf = groups[gi]
            s = gi % 2
            for half in range(2):
                hs = slice(half * 512, (half + 1) * 512)
                for fi in range(nf):
                    f = f0 + fi
                    slot = unit % 3
                    unit += 1
                    pg, pu = k.pb[2 * slot], k.pb[2 * slot + 1]
                    bpg, bpu = k.bpb[2 * slot], k.bpb[2 * slot + 1]
                    for (W, bW, ps, bps) in ((WG[s], bWG[s], pg, bpg), (WU[s], bWU[s], pu, bpu)):
                        for kc in range(KC):
                            P.op("pe", (lambda e, W=W, ps=ps, kc=kc, fi=fi, hs=hs: e.matmul(
                                ps[:, :], lhsT=W[:, kc, fi * 128:(fi + 1) * 128], rhs=hT[:, kc, hs],
                                start=(kc == 0), stop=(kc == KC - 1))),
                                rd=[bW, bh] if kc in (0, KC - 1) else [],
                                wr=[bps] if kc == KC - 1 else [],
                                waits=([bps.w] + bps.r) if kc == 0 else [], sig=(kc == KC - 1))
                    P.op("act", (lambda e, slot=slot, pg=pg: e.activation(out=SIL[slot][:, :], in_=pg[:, :], func=AF.Silu)),
                         rd=[bpg], wr=[bsil[slot]])
                    P.op("dve", (lambda e, slot=slot, pu=pu, f=f, p0=p0, hs=hs: e.tensor_tensor(
                        out=ACT[:, f - p0, hs], in0=SIL[slot][:, :], in1=pu[:, :], op=ALU.mult)),
                        rd=[bsil[slot], bpu], wr=[bact])
            gi += 1
            if next_group < len(groups):
                load_group(next_group)
                next_group += 1
        for dg in range(4):
            i = pi * 4 + dg
            s = i % 2
            for dc4 in range(4):
                dc = dg * 4 + dc4
                for half in range(2):
                    hs = slice(half * 512, (half + 1) * 512)
                    b = 6 + (dunit % 2)
                    dunit += 1
                    ps, bps = k.pb[b], k.bpb[b]
                    for fi in range(pn):
                        P.op("pe", (lambda e, ps=ps, fi=fi, dc4=dc4, hs=hs, s=s, pn=pn: e.matmul(
                            ps[:, :], lhsT=WD[s][:, fi, dc4 * 128:(dc4 + 1) * 128], rhs=ACT[:, fi, hs],
                            start=(fi == 0), stop=(fi == pn - 1))),
                            rd=[bWD[s], bact] if fi in (0, pn - 1) else [],
                            wr=[bps] if fi == pn - 1 else [],
                            waits=([bps.w] + bps.r) if fi == 0 else [], sig=(fi == pn - 1))
                    P.op("dve", (lambda e, ps=ps, dc=dc, hs=hs: e.scalar_tensor_tensor(
                        out=k.xT[:, dc, hs], in0=ps[:, :], scalar=k.modG[:, L, s3, dc:dc + 1],
                        in1=k.xT[:, dc, hs], op0=ALU.mult, op1=ALU.add)),
                        rd=[bps, k.bmods], wr=[k.bx])
            if next_wd < len(dlist):
                load_wd(next_wd)
                next_wd += 1


def conv_stage(k, L, j):
    P, nc = k.P, k.nc
    P.barrier()
    ci = k.ci[L]
    UW = 1056
    hT = carve(k, 0, [128, KC, T], BF16)
    U = carve(k, 32768, [128, KC, UW], BF16)
    CB = 32768 + 33792
    SIL = [carve(k, CB + 12288 + s * 2048, [128, 512], F32) for s in range(3)]
    SBh = carve(k, CB + 20480, [128, KC, 32], BF16)
    RBh = carve(k, CB + 21504, [128, 2, KC, 32], BF16)
    WB = CB + 32768
    WG = [carve(k, WB + s * 8192, [128, KC, 256], BF16) for s in range(2)]
    WU = [carve(k, WB + 16384 + s * 8192, [128, KC, 256], BF16) for s in range(2)]
    bh, bU = Buf("hT"), Buf("U")
    bWG, bWU = [Buf(), Buf()], [Buf(), Buf()]
    bsil = [Buf(), Buf(), Buf()]
    dWG = [getds(k, "wg0"), getds(k, "wg1")]
    dWU = [getds(k, "wu0"), getds(k, "wu1")]
    win = k.conv_w_in[ci].rearrange("(kc p) f -> p kc f", p=128)
    wout = k.conv_w_out[ci].rearrange("(kc p) f -> p kc f", p=128)
    cbin = f"cbin{j}"

    def load_group(gi):
        s = gi % 2
        P.dma("pool", WG[s][:, :, :], win[:, :, D + gi * 256:D + (gi + 1) * 256], dWG[s], wr=[bWG[s]])
        P.dma("pool", WU[s][:, :, :], win[:, :, gi * 256:(gi + 1) * 256], dWU[s], wr=[bWU[s]])

    load_group(0)
    load_group(1)
    norm_modulate(k, lambda kc: k.modA[:, L, 1, kc:kc + 1],
                  lambda kc: k.mods[:, L, 48 + kc:48 + kc + 1],
                  lambda kc, half: hT[:, kc, half * 512:(half + 1) * 512], CB, bout=bh)
    unit = 0
    for gi in range(8):
        s = gi % 2
        for half in range(2):
            hs = slice(half * 512, (half + 1) * 512)
            for fi in range(2):
                dc = gi * 2 + fi
                slot = unit % 3
                unit += 1
                pg, pu = k.pb[2 * slot], k.pb[2 * slot + 1]
                bpg, bpu = k.bpb[2 * slot], k.bpb[2 * slot + 1]
                for (W, bW, ps, bps) in ((WG[s], bWG[s], pg, bpg), (WU[s], bWU[s], pu, bpu)):
                    for kc in range(KC):
                        P.op("pe", (lambda e, W=W, ps=ps, kc=kc, fi=fi, hs=hs: e.matmul(
                            ps[:, :], lhsT=W[:, kc, fi * 128:(fi + 1) * 128], rhs=hT[:, kc, hs],
                            start=(kc == 0), stop=(kc == KC - 1))),
                            rd=[bW, bh] if kc in (0, KC - 1) else [],
                            wr=[bps] if kc == KC - 1 else [],
                            waits=([bps.w] + bps.r) if kc == 0 else [], sig=(kc == KC - 1))
                P.op("act", (lambda e, slot=slot, pg=pg, dc=dc: e.activation(
                    out=SIL[slot][:, :], in_=pg[:, :], func=AF.Sigmoid, bias=sm(k, cbin, 16 + dc, 1), scale=1.0)),
                    rd=[bpg, k.bsm], wr=[bsil[slot]])
                P.op("dve", (lambda e, slot=slot, pu=pu, dc=dc, half=half: e.scalar_tensor_tensor(
                    out=U[:, dc, 16 + half * 512:16 + (half + 1) * 512], in0=pu[:, :], scalar=sm(k, cbin, dc, 1),
                    in1=SIL[slot][:, :], op0=ALU.add, op1=ALU.mult)),
                    rd=[bsil[slot], bpu, k.bsm], wr=[bU])
        if gi + 2 < 8:
            load_group(gi + 2)
    cc_in = nc.dram_tensor(f"cvh_in{j}", [128, 512], BF16)
    cc_out = nc.dram_tensor(f"cvh_out{j}", [256, 512], BF16)
    bSB, bRB, bcin, bcout = Buf(), Buf(), Buf(), Buf()
    P.op("dve", lambda e: e.tensor_copy(out=SBh[:, :, 0:16], in_=U[:, :, 16:32]), rd=[bU], wr=[bSB])
    P.op("dve", lambda e: e.tensor_copy(out=SBh[:, :, 16:32], in_=U[:, :, 1024:1040]), rd=[bU], wr=[bSB])
    P.dma("sp", cc_in.ap(), SBh.rearrange("p a b -> p (a b)"), getds(k, "halo"), rd=[bSB], wr=[bcin])
    P.op("pool", lambda e: e.collective_compute("AllGather", ALU.bypass, replica_groups=PAIRS,
                                                ins=[cc_in.ap().opt()], outs=[cc_out.ap().opt()]),
         rd=[bcin], wr=[bcout])
    P.dma("sp", RBh.rearrange("p r a b -> p r (a b)"), cc_out.ap().rearrange("(r p) n -> p r n", p=128),
          getds(k, "halo"), rd=[bcout], wr=[bRB])
    P.op("dve", lambda e: e.tensor_scalar(out=U[:, :, 1:16], in0=RBh[:, 0, :, 17:32], scalar1=sm(k, "is_odd"),
                                          scalar2=None, op0=ALU.mult), rd=[bRB, k.bsm], wr=[bU])
    P.op("dve", lambda e: e.tensor_scalar(out=U[:, :, 1040:1055], in0=RBh[:, 1, :, 0:15], scalar1=sm(k, "is_even"),
                                          scalar2=None, op0=ALU.mult), rd=[bRB, k.bsm], wr=[bU])
    P.barrier()
    SN = carve(k, 0, [128, KC, 512], BF16)
    DG = [carve(k, 16384 + s * 7936, [128, CONVW, 128], BF16) for s in range(2)]
    C = carve(k, CB, [128, KC, 512], F32)
    WO = [carve(k, WB + s * 8192, [128, KC, 256], BF16) for s in range(2)]
    TB = WB + 16384
    csq = [carve(k, TB + i * 2048, [128, 512], F32) for i in range(2)]
    MEAN = carve(k, TB + 4096, [128, 512], F32)
    T1 = carve(k, TB + 6144, [128, 512], F32)
    RSTD = carve(k, TB + 8192, [128, 512], F32)
    NMR = carve(k, TB + 10240, [128, 512], F32)
    tt = [carve(k, TB + 12288 + i * 2048, [128, 512], F32) for i in range(2)]
    bSN, bC = Buf("SN"), Buf("C")
    bDG = [Buf(), Buf()]
    bWO = [Buf(), Buf()]
    bcsq = [Buf(), Buf()]
    bst, btt = Buf("stats"), [Buf(), Buf()]
    dWO = [getds(k, "wo0"), getds(k, "wo1")]
    ident3 = sm(k, "ident").unsqueeze(1).to_broadcast([128, CONVW, 128])
    wo_loads = [(half, g) for half in range(2) for g in range(8)]

    def load_wo(i):
        half, g = wo_loads[i]
        s = i % 2
        P.dma("pool", WO[s][:, :, :], wout[:, :, g * 256:(g + 1) * 256], dWO[s], wr=[bWO[s]])

    load_wo(0)
    load_wo(1)
    nwo = 2
    cu = 0
    ou = 0
    for half in range(2):
        S1, S2 = k.pb[4], k.pb[5]
        bS1, bS2 = k.bpb[4], k.bpb[5]
        for dc in range(KC):
            s = dc % 2
            w3 = sm(k, f"cwdw{j}", dc * CONVW, CONVW).unsqueeze(2).to_broadcast([128, CONVW, 128])
            P.op("pool", (lambda e, s=s, w3=w3: e.tensor_tensor(out=DG[s][:, :, :], in0=ident3, in1=w3, op=ALU.mult)),
                 rd=[k.bsm], wr=[bDG[s]])
            cps, bcps = k.pb[cu % 3], k.bpb[cu % 3]
            cu += 1
            for t in range(CONVW):
                c0 = 1 + half * 512 + t
                P.op("pe", (lambda e, cps=cps, s=s, t=t, dc=dc, c0=c0: e.matmul(
                    cps[:, :], lhsT=DG[s][:, t, :], rhs=U[:, dc, c0:c0 + 512], start=(t == 0), stop=(t == CONVW - 1))),
                    rd=[bDG[s], bU] if t in (0, CONVW - 1) else [],
                    wr=[bcps] if t == CONVW - 1 else [],
                    waits=([bcps.w] + bcps.r) if t == 0 else [], sig=(t == CONVW - 1))
            P.op("act", (lambda e, cps=cps, dc=dc: e.activation(out=C[:, dc, :], in_=cps[:, :], func=AF.Identity,
                                                               bias=sm(k, f"cbdw{j}", dc, 1), scale=1.0)),
                 rd=[bcps, k.bsm], wr=[bC])
            i = dc % 2
            P.op("dve", (lambda e, dc=dc, i=i: e.tensor_tensor(out=csq[i][:, :], in0=C[:, dc, :], in1=C[:, dc, :],
                                                              op=ALU.mult)), rd=[bC], wr=[bcsq[i]])
            P.op("pe", (lambda e, dc=dc: e.matmul(S1[:, :], lhsT=k.ones[:, :], rhs=C[:, dc, :],
                                                  start=(dc == 0), stop=(dc == KC - 1))),
                 rd=[bC, k.bconst], wr=[bS1] if dc == KC - 1 else [],
                 waits=([bS1.w] + bS1.r) if dc == 0 else [], sig=True)
            P.op("pe", (lambda e, dc=dc, i=i: e.matmul(S2[:, :], lhsT=k.ones[:, :], rhs=csq[i][:, :],
                                                       start=(dc == 0), stop=(dc == KC - 1))),
                 rd=[bcsq[i], k.bconst], wr=[bS2] if dc == KC - 1 else [],
                 waits=([bS2.w] + bS2.r) if dc == 0 else [], sig=True)
        P.op("dve", lambda e: e.tensor_scalar(out=MEAN[:, :], in0=S1[:, :], scalar1=1.0 / D, scalar2=None,
                                              op0=ALU.mult), rd=[bS1], wr=[bst])
        P.op("dve", lambda e: e.tensor_tensor(out=T1[:, :], in0=MEAN[:, :], in1=MEAN[:, :], op=ALU.mult),
             rd=[bst], wr=[bst])
        P.op("dve", lambda e: e.scalar_tensor_tensor(out=T1[:, :], in0=S2[:, :], scalar=1.0 / D, in1=T1[:, :],
                                                     op0=ALU.mult, op1=ALU.subtract), rd=[bS2, bst], wr=[bst])
        P.op("act", lambda e: e.activation(out=RSTD[:, :], in_=T1[:, :], func=AF.Sqrt, bias=k.epsc[:, 0:1], scale=1.0),
             rd=[bst, k.bconst], wr=[bst])
        P.op("dve", lambda e: e.reciprocal(out=RSTD[:, :], in_=RSTD[:, :]), rd=[bst], wr=[bst])
        P.op("dve", lambda e: e.scalar_tensor_tensor(out=NMR[:, :], in0=MEAN[:, :], scalar=-1.0, in1=RSTD[:, :],
                                                     op0=ALU.mult, op1=ALU.mult), rd=[bst], wr=[bst])
        for dc in range(KC):
            i = dc % 2
            P.op("dve", (lambda e, dc=dc, i=i: e.tensor_tensor(out=tt[i][:, :], in0=C[:, dc, :], in1=RSTD[:, :],
                                                              op=ALU.mult)), rd=[bC, bst], wr=[btt[i]])
            P.op("dve", (lambda e, i=i: e.tensor_tensor(out=tt[i][:, :], in0=tt[i][:, :], in1=NMR[:, :], op=ALU.add)),
                 rd=[bst], wr=[btt[i]])
            P.op("act", (lambda e, dc=dc, i=i: e.activation(out=SN[:, dc, :], in_=tt[i][:, :], func=AF.Silu,
                                                           bias=sm(k, f"clnb{j}", dc, 1), scale=sm(k, f"clng{j}", dc, 1))),
                 rd=[btt[i], k.bsm], wr=[bSN])
        hs = slice(half * 512, (half + 1) * 512)
        for g in range(8):
            i = half * 8 + g
            s = i % 2
            for fi in range(2):
                dc = g * 2 + fi
                b = 6 + ou % 2
                ou += 1
                yps, byps = k.pb[b], k.bpb[b]
                for kc in range(KC):
                    P.op("pe", (lambda e, yps=yps, s=s, kc=kc, fi=fi: e.matmul(
                        yps[:, :], lhsT=WO[s][:, kc, fi * 128:(fi + 1) * 128], rhs=SN[:, kc, :],
                        start=(kc == 0), stop=(kc == KC - 1))),
                        rd=[bWO[s], bSN] if kc in (0, KC - 1) else [],
                        wr=[byps] if kc == KC - 1 else [],
                        waits=([byps.w] + byps.r) if kc == 0 else [], sig=(kc == KC - 1))
                ti = ou % 2
                P.op("dve", (lambda e, yps=yps, dc=dc, ti=ti: e.tensor_scalar(
                    out=tt[ti][:, :], in0=yps[:, :], scalar1=sm(k, f"cbout{j}", dc, 1),
                    scalar2=k.modG[:, L, 1, dc:dc + 1], op0=ALU.add, op1=ALU.mult)),
                    rd=[byps, k.bsm, k.bmods], wr=[btt[ti]])
                P.op("dve", (lambda e, dc=dc, ti=ti, hs=hs: e.tensor_tensor(
                    out=k.xT[:, dc, hs], in0=k.xT[:, dc, hs], in1=tt[ti][:, :], op=ALU.add)),
                    rd=[btt[ti]], wr=[k.bx])
            if nwo < len(wo_loads):
                load_wo(nwo)
                nwo += 1


def gla_stage(k, L):
    P, nc = k.P, k.nc
    P.barrier()
    win = k.gla_w_in[0].rearrange("(kc p) f -> p kc f", p=128)
    wout = k.gla_w_out[0].rearrange("(kc p) f -> p kc f", p=128)
    hT = carve(k, 0, [128, KC, T], BF16)
    OH = carve(k, 32768, [128, 4, T], F32)
    WS = [carve(k, 49152 + s * 8192, [128, KC, 256], BF16) for s in range(2)]
    Qb = carve(k, 65536, [128, 2, T], BF16)
    Kb = carve(k, 69632, [128, 2, T], BF16)
    LG = carve(k, 73728, [128, T], F32)
    Bc = carve(k, 77824, [128, T], F32)
    T1 = carve(k, 81920, [128, T], F32)
    QE = carve(k, 86016, [128, 2, T], BF16)
    KE = carve(k, 90112, [128, 2, T], BF16)
    KL = carve(k, 94208, [128, 2, T], BF16)
    QG = carve(k, 98304, [128, 2, T], BF16)
    Vt = carve(k, 102400, [128, 16, 512], BF16)
    S = carve(k, 118784, [128, 2, 512], F32)
    Sb = carve(k, 122880, [128, 2, 512], BF16)
    KLT = [carve(k, 124928 + i * 512, [128, 2, 128], BF16) for i in range(2)]
    ST = [carve(k, 125952 + i * 128, [128, 64], BF16) for i in range(2)]
    Z1 = carve(k, 126208, [128, 2, T], BF16)
    WA1 = carve(k, 130304, [128, 2, KC, 16], BF16)
    WA2 = carve(k, 131328, [128, 2, 256], BF16)
    SMT = carve(k, 132352, [128, 8, 16], F32)
    TOT, INC, PX, EP = SMT[:, 0, :], SMT[:, 1, :], SMT[:, 2, :], SMT[:, 3, :]
    DL = SMT[:, 4:6, :]
    ONE16 = SMT[:, 6, :]
    M01 = carve(k, 118784, [128, T], F32)
    bh, bOH, bQ, bK, bLG, bB, bT1 = Buf("hT"), Buf("OH"), Buf("Q"), Buf("K"), Buf("LG"), Buf("B"), Buf("T1")
    bQE, bKE, bKL, bQG, bV, bS, bSb = Buf("QE"), Buf("KE"), Buf("KL"), Buf("QG"), Buf("Vt"), Buf("S"), Buf("Sb")
    bKLT, bST = [Buf(), Buf()], [Buf(), Buf()]
    bZ1, bWA1, bWA2, bSM = Buf(), Buf(), Buf(), Buf()
    bM01 = bS
    bWS = [Buf(), Buf()]
    dWS = [getds(k, "gws0"), getds(k, "gws1")]
    qg_d = nc.dram_tensor("gla_qg", [4, 2, 128, 2 * T], BF16).ap()
    o_d = nc.dram_tensor("gla_o", [4, 128, 4 * T], F32).ap()
    st_in = [[nc.dram_tensor(f"gla_st_in{h}_{d}", [128, 1024], F32).ap() for d in range(2)] for h in range(4)]
    st_out = [[nc.dram_tensor(f"gla_st_out{h}_{d}", [256, 1024], F32).ap() for d in range(2)] for h in range(4)]
    bsti = [[Buf() for d in range(2)] for h in range(4)]
    bsto = [[Buf() for d in range(2)] for h in range(4)]
    bqgd, bod, bstin, bstout = Buf(), Buf(), Buf(), Buf()
    identb = k.identb
    gba = "gba"

    pieces = []
    for h in range(4):
        pieces += [("q", h, h * 256), ("k", h, 1024 + h * 256), ("v0", h, 2048 + h * 512), ("v1", h, 2048 + h * 512 + 256)]
    pst = {"n": 0}

    def load_piece():
        i = pst["n"]
        if i >= len(pieces):
            return
        pst["n"] += 1
        s = i % 2
        c0 = pieces[i][2]
        P.dma("pool", WS[s][:, :, :], win[:, :, c0:c0 + 256], dWS[s], wr=[bWS[s]])

    load_piece()
    load_piece()
    wa1v = k.gla_wa1[0].rearrange("d (kc p) r -> p d kc r", p=128)
    for d in range(2):
        for q in range(4):
            P.dma("pool", WA1[:, d, q * 4:(q + 1) * 4, :], wa1v[:, d, q * 4:(q + 1) * 4, :], getds(k, "gwa"), wr=[bWA1])
    norm_modulate(k, lambda kc: k.modA[:, L, 1, kc:kc + 1],
                  lambda kc: k.mods[:, L, 48 + kc:48 + kc + 1],
                  lambda kc, half: hT[:, kc, half * 512:(half + 1) * 512], 102400, bout=bh)
    P.op("dve", lambda e: e.memset(SMT[:, 6, :], 1.0), wr=[bSM])
    pc = {"n": 0}

    def nextbank():
        b = pc["n"] % 8
        pc["n"] += 1
        return k.pb[b], k.bpb[b]

    for d in range(2):
        for half in range(2):
            hs = slice(half * 512, (half + 1) * 512)
            ps, bps = nextbank()
            for kc in range(KC):
                P.op("pe", (lambda e, ps=ps, d=d, kc=kc, hs=hs: e.matmul(
                    ps[0:16, :], lhsT=WA1[:, d, kc, :], rhs=hT[:, kc, hs], start=(kc == 0), stop=(kc == KC - 1))),
                    rd=[bWA1, bh] if kc in (0, KC - 1) else [], wr=[bps] if kc == KC - 1 else [],
                    waits=([bps.w] + bps.r) if kc == 0 else [], sig=(kc == KC - 1))
            P.op("act", (lambda e, ps=ps, d=d, hs=hs: e.copy(out=Z1[0:16, d, hs], in_=ps[0:16, :])),
                 rd=[bps], wr=[bZ1])

    def proj_fm(dst, bdst, s, scale):
        for dkc in range(2):
            for half in range(2):
                hs = slice(half * 512, (half + 1) * 512)
                ps, bps = nextbank()
                for kc in range(KC):
                    P.op("pe", (lambda e, ps=ps, s=s, kc=kc, dkc=dkc, hs=hs: e.matmul(
                        ps[:, :], lhsT=WS[s][:, kc, dkc * 128:(dkc + 1) * 128], rhs=hT[:, kc, hs],
                        start=(kc == 0), stop=(kc == KC - 1))),
                        rd=[bWS[s], bh] if kc in (0, KC - 1) else [], wr=[bps] if kc == KC - 1 else [],
                        waits=([bps.w] + bps.r) if kc == 0 else [], sig=(kc == KC - 1))
                P.op("act", (lambda e, ps=ps, dkc=dkc, hs=hs: e.mul(out=dst[:, dkc, hs], in_=ps[:, :], mul=scale)),
                     rd=[bps], wr=[bdst])

    if GLA_STOP == 1:
        return
    for h in range(4):
        s = (h * 4) % 2
        proj_fm(Qb, bQ, (h * 4) % 2, 1.0 / 16.0)
        load_piece()
        proj_fm(Kb, bK, (h * 4 + 1) % 2, 1.0)
        load_piece()
        for vp in range(2):
            s = (h * 4 + 2 + vp) % 2
            for c in range(16):
                ps, bps = nextbank()
                for kc in range(KC):
                    P.op("pe", (lambda e, ps=ps, s=s, kc=kc, c=c: e.matmul(
                        ps[0:64, 0:256], lhsT=hT[:, kc, c * 64:(c + 1) * 64], rhs=WS[s][:, kc, :],
                        start=(kc == 0), stop=(kc == KC - 1))),
                        rd=[bWS[s], bh] if kc in (0, KC - 1) else [], wr=[bps] if kc == KC - 1 else [],
                        waits=([bps.w] + bps.r) if kc == 0 else [], sig=(kc == KC - 1))
                eng = "act" if c % 2 == 0 else "dve"
                if eng == "act":
                    P.op("act", (lambda e, ps=ps, c=c, vp=vp: e.copy(
                        out=Vt[0:64, c, vp * 256:(vp + 1) * 256], in_=ps[0:64, 0:256])),
                        rd=[bps], wr=[bV])
                else:
                    P.op("dve", (lambda e, ps=ps, c=c, vp=vp: e.tensor_copy(
                        out=Vt[0:64, c, vp * 256:(vp + 1) * 256], in_=ps[0:64, 0:256])), rd=[bps], wr=[bV])
            load_piece()
        P.dma("pool", WA2[0:16, :, :], k.gla_wa2[0].rearrange("d r n -> r d n")[:, :, h * 256:(h + 1) * 256],
              getds(k, "gwa"), wr=[bWA2])
        if GLA_STOP == 2:
            return
        first_o = True
        for d in range(2):
            P.op("dve", lambda e: e.memset(M01[:, :], 1.0), wr=[bM01])
            P.op("dve", lambda e: e.memset(M01.rearrange("p (c s) -> p c s", s=64)[:, :, 0:1], 0.0), wr=[bM01])
            for dkc in range(2):
                gcol = d * 8 + h * 2 + dkc
                for half in range(2):
                    hs = slice(half * 512, (half + 1) * 512)
                    ps, bps = nextbank()
                    P.op("pe", (lambda e, ps=ps, d=d, dkc=dkc, hs=hs: e.matmul(
                        ps[:, :], lhsT=WA2[0:16, d, dkc * 128:(dkc + 1) * 128], rhs=Z1[0:16, d, hs],
                        start=True, stop=True)),
                        rd=[bWA2, bZ1], wr=[bps], waits=([bps.w] + bps.r))
                    P.op("act", (lambda e, ps=ps, hs=hs, gcol=gcol: e.activation(
                        out=LG[:, hs], in_=ps[:, :], func=AF.Sigmoid, bias=sm(k, gba, gcol, 1), scale=1.0)),
                        rd=[bps, k.bsm], wr=[bLG])
                P.op("act", lambda e: e.activation(out=LG[:, :], in_=LG[:, :], func=AF.Ln), rd=[bLG], wr=[bLG])
                P.op("dve", lambda e: e.tensor_tensor_scan(out=Bc[:, :], data0=M01[:, :], data1=LG[:, :], initial=0.0,
                                                           op0=ALU.mult, op1=ALU.add), rd=[bM01, bLG], wr=[bB])
                B3 = Bc.rearrange("p (c s) -> p c s", s=64)
                totv = B3[:, :, 63]
                P.op("dve", lambda e, totv=totv: e.tensor_copy(out=TOT, in_=totv), rd=[bB], wr=[bSM])
                tot3 = TOT.unsqueeze(2).to_broadcast([128, 16, 64])
                T13 = T1.rearrange("p (c s) -> p c s", s=64)
                LG3 = LG.rearrange("p (c s) -> p c s", s=64)
                if d == 0:
                    P.op("act", lambda e: e.activation(out=T1[:, :], in_=Bc[:, :], func=AF.Exp, scale=1.0 / 16),
                         rd=[bB], wr=[bT1])
                    P.op("dve", (lambda e, dkc=dkc: e.tensor_tensor(out=QE[:, dkc, :], in0=Qb[:, dkc, :], in1=T1[:, :],
                                                                   op=ALU.mult)), rd=[bQ, bT1], wr=[bQE])
                    P.op("act", lambda e: e.activation(out=T1[:, :], in_=Bc[:, :], func=AF.Exp, scale=-1.0 / 16),
                         rd=[bB], wr=[bT1])
                    P.op("dve", (lambda e, dkc=dkc: e.tensor_tensor(out=KE[:, dkc, :], in0=Kb[:, dkc, :], in1=T1[:, :],
                                                                   op=ALU.mult)), rd=[bK, bT1], wr=[bKE])
                    P.op("dve", (lambda e, tot3=tot3, T13=T13, B3=B3: e.tensor_tensor(out=T13, in0=tot3, in1=B3,
                                                                                     op=ALU.subtract)),
                         rd=[bSM, bB], wr=[bT1])
                    P.op("act", lambda e: e.activation(out=T1[:, :], in_=T1[:, :], func=AF.Exp, scale=1.0 / 16),
                         rd=[bT1], wr=[bT1])
                    P.op("dve", (lambda e, dkc=dkc: e.tensor_tensor(out=KL[:, dkc, :], in0=Kb[:, dkc, :], in1=T1[:, :],
                                                                   op=ALU.mult)), rd=[bK, bT1], wr=[bKL])
                else:
                    P.op("dve", (lambda e, tot3=tot3, T13=T13, B3=B3: e.tensor_tensor(out=T13, in0=tot3, in1=B3,
                                                                                     op=ALU.subtract)),
                         rd=[bSM, bB], wr=[bT1])
                    P.op("dve", lambda e: e.tensor_tensor(out=T1[:, :], in0=T1[:, :], in1=LG[:, :], op=ALU.add),
                         rd=[bLG], wr=[bT1])
                    P.op("dve", lambda e: e.tensor_tensor(out=LG[:, :], in0=Bc[:, :], in1=LG[:, :], op=ALU.subtract),
                         rd=[bB], wr=[bLG])
                    P.op("act", lambda e: e.activation(out=Bc[:, :], in_=T1[:, :], func=AF.Exp, scale=1.0 / 16),
                         rd=[bT1], wr=[bB])
                    P.op("dve", (lambda e, dkc=dkc: e.tensor_tensor(out=QE[:, dkc, :], in0=Qb[:, dkc, :], in1=Bc[:, :],
                                                                   op=ALU.mult)), rd=[bQ, bB], wr=[bQE])
                    P.op("act", lambda e: e.activation(out=Bc[:, :], in_=T1[:, :], func=AF.Exp, scale=-1.0 / 16),
                         rd=[bT1], wr=[bB])
                    P.op("dve", (lambda e, dkc=dkc: e.tensor_tensor(out=KE[:, dkc, :], in0=Kb[:, dkc, :], in1=Bc[:, :],
                                                                   op=ALU.mult)), rd=[bK, bB], wr=[bKE])
                    P.op("act", lambda e: e.activation(out=Bc[:, :], in_=LG[:, :], func=AF.Exp, scale=1.0 / 16),
                         rd=[bLG], wr=[bB])
                    P.op("dve", (lambda e, dkc=dkc: e.tensor_tensor(out=KL[:, dkc, :], in0=Kb[:, dkc, :], in1=Bc[:, :],
                                                                   op=ALU.mult)), rd=[bK, bB], wr=[bKL])
                P.op("act", (lambda e, dkc=dkc: e.activation(out=DL[:, dkc, :], in_=TOT, func=AF.Exp, scale=1.0 / 16)),
                     rd=[bSM], wr=[bSM])
                P.op("dve", lambda e: e.tensor_tensor_scan(out=INC, data0=ONE16, data1=TOT, initial=0.0,
                                                           op0=ALU.mult, op1=ALU.add), rd=[bSM], wr=[bSM])
                if d == 0:
                    P.op("dve", lambda e: e.tensor_tensor(out=PX, in0=INC, in1=TOT, op=ALU.subtract), rd=[bSM], wr=[bSM])
                else:
                    P.op("dve", lambda e: e.tensor_scalar(out=PX, in0=INC, scalar1=SMT[:, 1, 15:16], scalar2=-1.0,
                                                          op0=ALU.subtract, op1=ALU.mult), rd=[bSM], wr=[bSM])
                P.op("act", lambda e: e.activation(out=EP, in_=PX, func=AF.Exp, scale=1.0 / 16), rd=[bSM], wr=[bSM])
                ep3 = EP.unsqueeze(2).to_broadcast([128, 16, 64])
                P.op("dve", (lambda e, dkc=dkc, ep3=ep3: e.tensor_tensor(
                    out=QG[:, dkc, :].rearrange("p (c s) -> p c s", s=64),
                    in0=QE[:, dkc, :].rearrange("p (c s) -> p c s", s=64), in1=ep3, op=ALU.mult)),
                    rd=[bQE, bSM], wr=[bQG])
            P.dma("sp", qg_d[h, d], QG.rearrange("p a t -> p (a t)"), getds(k, "gqg"), rd=[bQG], wr=[bqgd])
            if GLA_STOP == 3:
                return
            P.op("dve", lambda e: e.memset(S[:, :, :], 0.0), wr=[bS])
            P.op("dve", lambda e: e.memset(Sb[:, :, :], 0.0), wr=[bSb])
            order = list(range(16)) if d == 0 else list(range(15, -1, -1))
            mask = sm(k, "mask_f" if d == 0 else "mask_b")
            for ci, c in enumerate(order):
                cs = slice(c * 64, (c + 1) * 64)
                sl = ci % 2
                pss, bpss = k.pb[sl], k.bpb[sl]
                pso, bpso = k.pb[2 + sl], k.bpb[2 + sl]
                psu = [k.pb[4], k.pb[5]]
                bpsu = [k.bpb[4], k.bpb[5]]
                ptp, bptp = k.pb[6 + sl], k.bpb[6 + sl]
                tpv = ptp[0:64, 0:128].bitcast(BF16).rearrange("p (a b) -> p a b", a=2)
                for dkc in range(2):
                    P.op("pe", (lambda e, tpv=tpv, dkc=dkc, cs=cs: e.transpose(tpv[:, dkc, :], KL[:, dkc, cs], identb[:, :])),
                         rd=[bKL, k.bconst], wr=[bptp] if dkc == 1 else [],
                         waits=([bptp.w] + bptp.r) if dkc == 0 else [], sig=(dkc == 1))
                P.op("act", (lambda e, tpv=tpv, sl=sl: e.copy(out=KLT[sl][0:64, :, :], in_=tpv)),
                     rd=[bptp], wr=[bKLT[sl]])
                for dkc in range(2):
                    P.op("pe", (lambda e, dkc=dkc, sl=sl, c=c: e.matmul(
                        psu[dkc][:, :], lhsT=KLT[sl][0:64, dkc, :], rhs=Vt[0:64, c, :], start=True, stop=True)),
                        rd=[bKLT[sl], bV], wr=[bpsu[dkc]], waits=([bpsu[dkc].w] + bpsu[dkc].r))
                for dkc in range(2):
                    P.op("pe", (lambda e, pss=pss, dkc=dkc, cs=cs: e.matmul(
                        pss[0:64, 0:64], lhsT=KE[:, dkc, cs], rhs=QE[:, dkc, cs], start=(dkc == 0), stop=(dkc == 1))),
                        rd=[bKE, bQE], wr=[bpss] if dkc == 1 else [],
                        waits=([bpss.w] + bpss.r) if dkc == 0 else [], sig=(dkc == 1))
                P.op("dve", (lambda e, pss=pss, sl=sl, mask=mask: e.tensor_tensor(
                    out=ST[sl][0:64, :], in0=pss[0:64, 0:64], in1=mask[0:64, :], op=ALU.mult)),
                    rd=[bpss, k.bsm], wr=[bST[sl]])
                pso3 = pso[:, 0:256].rearrange("p (a b) -> p a b", a=4)
                for ec in range(4):
                    es = slice(ec * 128, (ec + 1) * 128)
                    P.op("pe", (lambda e, pso3=pso3, ec=ec, es=es, cs=cs: e.matmul(
                        pso3[:, ec, :], lhsT=Sb[:, 0, es], rhs=QE[:, 0, cs], start=True, stop=False)),
                        rd=[bSb, bQE] if ec == 0 else [], waits=([bpso.w] + bpso.r) if ec == 0 else [], sig=False)
                    P.op("pe", (lambda e, pso3=pso3, ec=ec, es=es, cs=cs: e.matmul(
                        pso3[:, ec, :], lhsT=Sb[:, 1, es], rhs=QE[:, 1, cs], start=False, stop=False)), sig=False)
                    P.op("pe", (lambda e, pso3=pso3, ec=ec, es=es, c=c, sl=sl: e.matmul(
                        pso3[:, ec, :], lhsT=Vt[0:64, c, es], rhs=ST[sl][0:64, :], start=False, stop=True)),
                        rd=[bST[sl], bV, bSb, bQE] if ec in (0, 3) else [], wr=[bpso] if ec == 3 else [], sig=(ec == 3))
                if d == 0:
                    P.op("act", (lambda e, pso3=pso3, cs=cs: e.copy(out=OH[:, :, cs], in_=pso3)),
                         rd=[bpso], wr=[bOH])
                else:
                    P.op("dve", (lambda e, pso3=pso3, cs=cs: e.tensor_tensor(out=OH[:, :, cs], in0=OH[:, :, cs], in1=pso3,
                                                                            op=ALU.add)), rd=[bpso], wr=[bOH])
                for dkc in range(2):
                    P.op("dve", (lambda e, dkc=dkc, c=c: e.scalar_tensor_tensor(
                        out=S[:, dkc, :], in0=S[:, dkc, :], scalar=DL[:, dkc, c:c + 1], in1=psu[dkc][:, :],
                        op0=ALU.mult, op1=ALU.add)), rd=[bpsu[dkc], bSM], wr=[bS])
                P.op("act", lambda e: e.copy(out=Sb[:, :, :], in_=S[:, :, :]), rd=[bS], wr=[bSb])
            P.dma("sp", st_in[h][d], S.rearrange("p a b -> p (a b)"), getds(k, "gst"), rd=[bS], wr=[bsti[h][d]])
            P.op("pool", (lambda e, h=h, d=d: e.collective_compute(
                "AllGather", ALU.bypass, replica_groups=PAIRS, ins=[st_in[h][d].opt()], outs=[st_out[h][d].opt()])),
                rd=[bsti[h][d]], wr=[bsto[h][d]])
            if GLA_STOP == 4:
                return
        P.dma("sp", o_d[h], OH.rearrange("p a t -> p (a t)"), getds(k, "god"), rd=[bOH], wr=[bod])

    if GLA_STOP == 5:
        return
    P.barrier()
    if GLA_STOP == 6:
        return
    ON = carve(k, 65536, [128, KC, T], BF16)
    QGf = carve(k, 98304, [128, 2, T], BF16)
    QGb = carve(k, 102400, [128, 2, T], BF16)
    SST = [carve(k, 106496 + i * 4096, [128, 2, 512], F32) for i in range(2)]
    SIN = [carve(k, 114688 + i * 2048, [128, 2, 512], BF16) for i in range(2)]
    RS = carve(k, 118784, [128, T], F32)
    SQ = [carve(k, 122880 + i * 2048, [128, 512], F32) for i in range(2)]
    SG = [carve(k, 126976 + i * 2048, [128, 512], F32) for i in range(2)]
    TT = SQ[0]
    bON, bQGf, bQGb, bRS, bTT = Buf("ON"), Buf(), Buf(), Buf(), Buf()
    bSST, bSIN, bSQ, bSG = [Buf(), Buf()], [Buf(), Buf()], [Buf(), Buf()], [Buf(), Buf()]
    bTT = bSQ[0]
    gp = {"n": 0}
    gpieces = [(h, gpi) for h in range(4) for gpi in range(2)]

    def load_gpiece():
        i = gp["n"]
        if i >= len(gpieces):
            return
        gp["n"] += 1
        h, gpi = gpieces[i]
        c0 = 4096 + h * 512 + gpi * 256
        P.dma("pool", WS[i % 2][:, :, :], win[:, :, c0:c0 + 256], dWS[i % 2], wr=[bWS[i % 2]])

    load_gpiece()
    load_gpiece()
    for h in range(4):
        P.dma("sp", OH.rearrange("p a t -> p (a t)"), o_d[h], getds(k, "god2"), rd=[bod], wr=[bOH])
        P.dma("sp", QGf.rearrange("p a t -> p (a t)"), qg_d[h, 0], getds(k, "gqg2"), rd=[bqgd], wr=[bQGf])
        P.dma("sp", QGb.rearrange("p a t -> p (a t)"), qg_d[h, 1], getds(k, "gqg3"), rd=[bqgd], wr=[bQGb])
        for d in range(2):
            P.dma("sp", SST[d].rearrange("p a b -> p (a b)"), st_out[h][d][d * 128:(d + 1) * 128, :], getds(k, f"gsi{d}"),
                  rd=[bsto[h][d]], wr=[bSST[d]])
            P.op("dve", (lambda e, d=d: e.tensor_scalar(out=SIN[d][:, :, :], in0=SST[d][:, :, :],
                                                        scalar1=sm(k, "is_odd" if d == 0 else "is_even"), scalar2=None,
                                                        op0=ALU.mult)), rd=[bSST[d], k.bsm], wr=[bSIN[d]])
        for ec in range(4):
            es = slice(ec * 128, (ec + 1) * 128)
            for half in range(2):
                hs = slice(half * 512, (half + 1) * 512)
                ps, bps = nextbank()
                n = 0
                for d, QGx, bQGx in ((0, QGf, bQGf), (1, QGb, bQGb)):
                    for dkc in range(2):
                        P.op("pe", (lambda e, ps=ps, d=d, dkc=dkc, es=es, hs=hs, QGx=QGx, n=n: e.matmul(
                            ps[:, :], lhsT=SIN[d][:, dkc, es], rhs=QGx[:, dkc, hs], start=(n == 0), stop=(n == 3))),
                            rd=[bSIN[d], bQGx], wr=[bps] if n == 3 else [],
                            waits=([bps.w] + bps.r) if n == 0 else [], sig=(n == 3))
                        n += 1
                P.op("dve", (lambda e, ps=ps, ec=ec, hs=hs: e.tensor_tensor(out=OH[:, ec, hs], in0=OH[:, ec, hs],
                                                                           in1=ps[:, :], op=ALU.add)),
                     rd=[bps], wr=[bOH])
        for half in range(2):
            hs = slice(half * 512, (half + 1) * 512)
            ps, bps = nextbank()
            for ec in range(4):
                i = ec % 2
                P.op("dve", (lambda e, ec=ec, i=i, hs=hs: e.tensor_tensor(out=SQ[i][:, :], in0=OH[:, ec, hs],
                                                                         in1=OH[:, ec, hs], op=ALU.mult)),
                     rd=[bOH], wr=[bSQ[i]])
                P.op("pe", (lambda e, ps=ps, ec=ec, i=i: e.matmul(ps[:, :], lhsT=k.ones[:, :], rhs=SQ[i][:, :],
                                                                  start=(ec == 0), stop=(ec == 3))),
                     rd=[bSQ[i], k.bconst], wr=[bps] if ec == 3 else [],
                     waits=([bps.w] + bps.r) if ec == 0 else [], sig=True)
            P.op("act", (lambda e, ps=ps, hs=hs: e.activation(out=RS[:, hs], in_=ps[:, :], func=AF.Sqrt,
                                                              bias=k.epsc[:, 0:1], scale=1.0 / 512)),
                 rd=[bps, k.bconst], wr=[bRS])
            P.op("dve", (lambda e, hs=hs: e.reciprocal(out=RS[:, hs], in_=RS[:, hs])), rd=[bRS], wr=[bRS])
        for gpi in range(2):
            i = h * 2 + gpi
            s = i % 2
            for f in range(2):
                ec = gpi * 2 + f
                for half in range(2):
                    hs = slice(half * 512, (half + 1) * 512)
                    ps, bps = nextbank()
                    for kc in range(KC):
                        P.op("pe", (lambda e, ps=ps, s=s, kc=kc, f=f, hs=hs: e.matmul(
                            ps[:, :], lhsT=WS[s][:, kc, f * 128:(f + 1) * 128], rhs=hT[:, kc, hs],
                            start=(kc == 0), stop=(kc == KC - 1))),
                            rd=[bWS[s], bh] if kc in (0, KC - 1) else [], wr=[bps] if kc == KC - 1 else [],
                            waits=([bps.w] + bps.r) if kc == 0 else [], sig=(kc == KC - 1))
                    gi2 = (ec * 2 + half) % 2
                    P.op("act", (lambda e, ps=ps, gi2=gi2: e.activation(out=SG[gi2][:, :], in_=ps[:, :], func=AF.Silu)),
                         rd=[bps], wr=[bSG[gi2]])
                    P.op("dve", (lambda e, ec=ec, hs=hs: e.tensor_tensor(out=TT[:, :], in0=OH[:, ec, hs], in1=RS[:, hs],
                                                                        op=ALU.mult)), rd=[bOH, bRS], wr=[bTT])
                    P.op("dve", (lambda e, ec=ec, hs=hs, gi2=gi2, h=h: e.scalar_tensor_tensor(
                        out=ON[:, h * 4 + ec, hs], in0=TT[:, :], scalar=sm(k, "gng", ec, 1), in1=SG[gi2][:, :],
                        op0=ALU.mult, op1=ALU.mult)), rd=[bTT, bSG[gi2], k.bsm], wr=[bON])
            load_gpiece()
    P.barrier()
    if GLA_STOP == 7:
        return
    WO = [carve(k, 49152 + s * 8192, [128, KC, 256], BF16) for s in range(2)]
    bWO = [Buf(), Buf()]
    dWO = [getds(k, "wo0"), getds(k, "wo1")]

    def load_wo(g):
        P.dma("pool", WO[g % 2][:, :, :], wout[:, :, g * 256:(g + 1) * 256], dWO[g % 2], wr=[bWO[g % 2]])

    load_wo(0)
    load_wo(1)
    ou = 0
    for g in range(8):
        s = g % 2
        for fi in range(2):
            dc = g * 2 + fi
            for half in range(2):
                hs = slice(half * 512, (half + 1) * 512)
                b = 6 + ou % 2
                ou += 1
                yps, byps = k.pb[b], k.bpb[b]
                for kc in range(KC):
                    P.op("pe", (lambda e, yps=yps, s=s, kc=kc, fi=fi, hs=hs: e.matmul(
                        yps[:, :], lhsT=WO[s][:, kc, fi * 128:(fi + 1) * 128], rhs=ON[:, kc, hs],
                        start=(kc == 0), stop=(kc == KC - 1))),
                        rd=[bWO[s], bON] if kc in (0, KC - 1) else [], wr=[byps] if kc == KC - 1 else [],
                        waits=([byps.w] + byps.r) if kc == 0 else [], sig=(kc == KC - 1))
                P.op("dve", (lambda e, yps=yps, dc=dc, hs=hs: e.scalar_tensor_tensor(
                    out=k.xT[:, dc, hs], in0=yps[:, :], scalar=k.modG[:, L, 1, dc:dc + 1], in1=k.xT[:, dc, hs],
                    op0=ALU.mult, op1=ALU.add)), rd=[byps, k.bmods], wr=[k.bx])
        if g + 2 < 8:
            load_wo(g + 2)


def pool_stage(k, L):
    P, nc = k.P, k.nc
    P.barrier()
    HW = 1040
    H = carve(k, 0, [128, KC, HW], F32)
    B0 = 66560
    A_ = [carve(k, B0 + i * 8448, [128, 4, 528], F32) for i in range(2)]
    B1 = B0 + 2 * 8448
    DBF = [carve(k, B1 + i * 4096, [128, 4, 512], BF16) for i in range(2)]
    B2 = B1 + 8192
    PW = carve(k, B2, [128, 4, 4, 512], BF16)
    B3 = B2 + 16384
    V = carve(k, B3, [128, HW], F32)
    VA = [carve(k, B3 + 4160 + i * 2112, [128, 528], F32) for i in range(2)]
    INV = carve(k, B3 + 4160 + 4224, [128, 512], F32)
    SBh = carve(k, B3 + 10432, [128, KC, 16], F32)
    RBh = carve(k, B3 + 11456, [128, 2, KC, 16], F32)
    TMP = [carve(k, B3 + 13504 + i * 2048, [128, 512], F32) for i in range(2)]
    NB = B0
    bH, bA, bD, bPW, bV, bVA, bINV, bT = Buf("H"), [Buf(), Buf()], [Buf(), Buf()], Buf(), Buf(), [Buf(), Buf()], Buf(), [Buf(), Buf()]
    P.dma("pool", PW.rearrange("p g i o -> p (g i) o"),
          k.pool_w[0].rearrange("g (i p) o -> p (g i) o", p=128), getds(k, "pw"), wr=[bPW])
    norm_modulate(k, lambda kc: k.modA[:, L, 1, kc:kc + 1],
                  lambda kc: k.mods[:, L, 48 + kc:48 + kc + 1],
                  lambda kc, half: H[:, kc, 8 + half * 512:8 + (half + 1) * 512], NB, bout=bH)
    cc_in = nc.dram_tensor("plh_in", [128, 256], F32)
    cc_out = nc.dram_tensor("plh_out", [256, 256], F32)
    bSB, bRB, bcin, bcout = Buf(), Buf(), Buf(), Buf()
    P.op("dve", lambda e: e.tensor_copy(out=SBh[:, :, 0:8], in_=H[:, :, 8:16]), rd=[bH], wr=[bSB])
    P.op("dve", lambda e: e.tensor_copy(out=SBh[:, :, 8:16], in_=H[:, :, 1024:1032]), rd=[bH], wr=[bSB])
    P.dma("sp", cc_in.ap(), SBh.rearrange("p a b -> p (a b)"), getds(k, "halo"), rd=[bSB], wr=[bcin])
    P.op("pool", lambda e: e.collective_compute("AllGather", ALU.bypass, replica_groups=PAIRS,
                                                ins=[cc_in.ap().opt()], outs=[cc_out.ap().opt()]),
         rd=[bcin], wr=[bcout])
    P.dma("sp", RBh.rearrange("p r a b -> p r (a b)"), cc_out.ap().rearrange("(r p) n -> p r n", p=128),
          getds(k, "halo"), rd=[bcout], wr=[bRB])
    P.op("dve", lambda e: e.tensor_scalar(out=H[:, :, 0:8], in0=RBh[:, 0, :, 8:16], scalar1=sm(k, "is_odd"),
                                          scalar2=None, op0=ALU.mult), rd=[bRB, k.bsm], wr=[bH])
    P.op("dve", lambda e: e.tensor_scalar(out=H[:, :, 1032:1040], in0=RBh[:, 1, :, 0:8], scalar1=sm(k, "is_even"),
                                          scalar2=None, op0=ALU.mult), rd=[bRB, k.bsm], wr=[bH])
    P.op("dve", lambda e: e.memset(V[:, :], 1.0), wr=[bV])
    P.op("dve", lambda e: e.tensor_copy(out=V[:, 0:8], in_=sm(k, "is_odd").to_broadcast([128, 8])), rd=[k.bsm], wr=[bV])
    P.op("dve", lambda e: e.tensor_copy(out=V[:, 1032:1040], in_=sm(k, "is_even").to_broadcast([128, 8])),
         rd=[k.bsm], wr=[bV])
    u = 0
    for gi in range(4):
        w = 2 << gi
        cs = slice(gi * 4, gi * 4 + 4)
        for half in range(2):
            b0 = half * 512 + 8 - w // 2
            n = 512 + w - 1
            src, bsrc = H[:, cs, b0:b0 + n], bH
            vsrc, bvsrc = V[:, b0:b0 + n], bV
            st = 1
            lvl = 0
            while st < w:
                dst, bdst = A_[lvl % 2], bA[lvl % 2]
                vdst, bvdst = VA[lvl % 2], bVA[lvl % 2]
                nn = n - st
                P.op("dve", (lambda e, dst=dst, src=src, nn=nn, st=st: e.tensor_tensor(
                    out=dst[:, :, 0:nn], in0=src[:, :, 0:nn], in1=src[:, :, st:st + nn], op=ALU.add)),
                    rd=[bsrc], wr=[bdst])
                P.op("dve", (lambda e, vdst=vdst, vsrc=vsrc, nn=nn, st=st: e.tensor_tensor(
                    out=vdst[:, 0:nn], in0=vsrc[:, 0:nn], in1=vsrc[:, st:st + nn], op=ALU.add)),
                    rd=[bvsrc], wr=[bvdst])
                src, bsrc = dst, bdst
                vsrc, bvsrc = vdst, bvdst
                n = nn
                st *= 2
                lvl += 1
            assert n == 512
            P.op("dve", (lambda e, vsrc=vsrc: e.reciprocal(out=INV[:, :], in_=vsrc[:, 0:512])), rd=[bvsrc], wr=[bINV])
            inv3 = INV[:, :].unsqueeze(1).to_broadcast([128, 4, 512])
            P.op("dve", (lambda e, src=src, inv3=inv3: e.tensor_tensor(out=src[:, :, 0:512], in0=src[:, :, 0:512],
                                                                      in1=inv3, op=ALU.mult)),
                 rd=[bINV], wr=[bsrc])
            di = u % 2
            u += 1
            P.op("dve", (lambda e, src=src, di=di, cs=cs, half=half: e.tensor_tensor(
                out=DBF[di][:, :, :], in0=src[:, :, 0:512], in1=H[:, cs, 8 + half * 512:8 + (half + 1) * 512],
                op=ALU.subtract)), rd=[bsrc, bH], wr=[bD[di]])
            hs = slice(half * 512, (half + 1) * 512)
            for oc in range(4):
                ch = gi * 4 + oc
                b = 6 + (u * 4 + oc) % 2
                yps, byps = k.pb[b], k.bpb[b]
                for ic in range(4):
                    P.op("pe", (lambda e, yps=yps, gi=gi, ic=ic, oc=oc, di=di: e.matmul(
                        yps[:, :], lhsT=PW[:, gi, ic, oc * 128:(oc + 1) * 128], rhs=DBF[di][:, ic, :],
                        start=(ic == 0), stop=(ic == 3))),
                        rd=[bPW, bD[di]] if ic in (0, 3) else [],
                        wr=[byps] if ic == 3 else [],
                        waits=([byps.w] + byps.r) if ic == 0 else [], sig=(ic == 3))
                ti = oc % 2
                P.op("dve", (lambda e, yps=yps, ch=ch, ti=ti: e.tensor_scalar(
                    out=TMP[ti][:, :], in0=yps[:, :], scalar1=sm(k, "pb", ch, 1), scalar2=sm(k, "pscale", ch, 1),
                    op0=ALU.add, op1=ALU.mult)), rd=[byps, k.bsm], wr=[bT[ti]])
                P.op("dve", (lambda e, ch=ch, ti=ti, hs=hs: e.scalar_tensor_tensor(
                    out=k.xT[:, ch, hs], in0=TMP[ti][:, :], scalar=k.modG[:, L, 1, ch:ch + 1], in1=k.xT[:, ch, hs],
                    op0=ALU.mult, op1=ALU.add)), rd=[bT[ti], k.bmods], wr=[k.bx])


def epilogue(k, last):
    P = k.P
    P.barrier()
    ov = k.out.rearrange("(kc p) t -> p kc t", p=128)
    if last:
        OUT = carve(k, 0, [128, KC, T], F32)
        bo = Buf("out")
        norm_modulate(k, lambda kc: sm(k, "final_g", kc, 1), None,
                      lambda kc, half: OUT[:, kc, half * 512:(half + 1) * 512], 65536, bout=bo)
        src, bsrc = OUT, bo
    else:
        src, bsrc = k.xT, k.bx
    toks = []
    for q in range(4):
        toks.append(P.dma("sp", ov[:, q * 4:(q + 1) * 4, :], src[:, q * 4:(q + 1) * 4, :], getds(k, "xout"), rd=[bsrc]))
    P.wait_only("sp", toks[-1:])


_CACHE = {}
LAUNCH_GROUPS = [[0, 1, 2, 3]]


def weights_for(inp, grp, mixers):
    w = {}
    g = list(grp)
    for n in ("ada_w", "ffn_w_gate", "ffn_w_up", "ffn_w_down"):
        w[n] = np.ascontiguousarray(np.asarray(inp[n], np.float32)[g])
    convL = [L for L in g if L % 3 == 0 and mixers]
    if convL:
        cj = [L // 3 for L in convL]
        w["conv_w_in"] = np.ascontiguousarray(np.asarray(inp["conv_w_in"], np.float32)[cj])
        w["conv_w_out"] = np.ascontiguousarray(np.asarray(inp["conv_w_out"], np.float32)[cj])
    if 1 in g and mixers:
        for n in ("gla_w_in", "gla_wa1", "gla_wa2", "gla_w_out"):
            w[n] = np.ascontiguousarray(np.asarray(inp[n], np.float32))
    if 2 in g and mixers:
        w["pool_w"] = np.ascontiguousarray(np.asarray(inp["pool_w"], np.float32))
    return w


def run_groups(inp, groups, mixers=True, final=True):
    x = np.asarray(inp["x"], np.float32)
    smalls = [pack_smalls(inp, c) for c in range(NCORES)]
    xT = [np.ascontiguousarray(x[c // 2, (c % 2) * T:(c % 2 + 1) * T, :].T) for c in range(NCORES)]
    for gi, grp in enumerate(groups):
        lastg = final and gi == len(groups) - 1
        key = (tuple(grp), lastg, mixers)
        if key not in _CACHE:
            _CACHE[key] = build_program(list(grp), gi == 0, lastg, mixers)
        nc = _CACHE[key]
        w = weights_for(inp, grp, mixers)
        in_maps = []
        if ADA_SPLIT:
            ada_half = [np.ascontiguousarray(w["ada_w"][:, :, r * 9216:(r + 1) * 9216]) for r in range(2)]
        for c in range(NCORES):
            m = {"xT_in": xT[c], "smalls": smalls[c]}
            m.update(w)
            if ADA_SPLIT:
                m["ada_w"] = ada_half[c % 2]
            in_maps.append(m)
        res = run_bass_kernel_spmd(nc, in_maps, core_ids=list(range(NCORES)))
        xT = [np.ascontiguousarray(res.results[c]["outT"]) for c in range(NCORES)]
    out = np.empty((4, SEQ, D), np.float32)
    for c in range(NCORES):
        out[c // 2, (c % 2) * T:(c % 2 + 1) * T, :] = xT[c].T
    return out


def kernel(**inputs):
    inp = {n: np.asarray(v) for n, v in inputs.items()}
    return run_groups(inp, LAUNCH_GROUPS, True, True)
```

```python
import contextlib
import numpy as np
import concourse.bass as bass
import concourse.mybir as mybir
from concourse.bass_utils import run_bass_kernel_spmd

F32 = mybir.dt.float32
BF16 = mybir.dt.bfloat16
AF = mybir.ActivationFunctionType
ALU = mybir.AluOpType
AX = mybir.AxisListType

NCORES = 8
D = 2048
KC = 16
T = 1024
SEQ = 2048
DEPTH = 4
FF = 5504
FC = 43
EPS = 1e-6
CONVW = 31
SEM_LIMIT = 12000
ADA_SPLIT = True
FFN_ON = True
GLA_STOP = 0
PAIRS = [[0, 1], [2, 3], [4, 5], [6, 7]]


class Buf:
    __slots__ = ("w", "r", "name")

    def __init__(self, name=""):
        self.w = None
        self.r = []
        self.name = name


class DmaSem:
    def __init__(self, name):
        self.name = name
        self.sem = None
        self.n = 0
        self.k = 0

    def advance(self, P):
        if self.sem is None or self.n >= SEM_LIMIT * 4:
            self.sem = P._alloc_sem(f"d_{self.name}_{self.k}")
            self.k += 1
            self.n = 0
        self.n += 16


class Prog:
    ENGS = ("sp", "act", "pe", "dve", "pool")

    def __init__(self, nc):
        self.nc = nc
        self.streams = {e: [] for e in self.ENGS}
        self.cur_sem = {}
        self.cnt = {}
        self.seen = {e: {} for e in self.ENGS}
        self.nsem = 0
        self._sem_cms = []
        self.last_tok = {}
        self.dma_last = {}
        for e in self.ENGS:
            self._new_eng_sem(e)

    def _alloc_sem(self, name):
        cm = self.nc.semaphore(name)
        s = cm.__enter__()
        self._sem_cms.append(cm)
        self.nsem += 1
        return s

    def _new_eng_sem(self, e):
        self.cur_sem[e] = self._alloc_sem(f"s_{e}_{self.nsem}")
        self.cnt[e] = 0

    def close(self):
        for cm in reversed(self._sem_cms):
            cm.__exit__(None, None, None)

    def _waits(self, eng, toks):
        out = []
        seen = self.seen[eng]
        for t in toks:
            if t is None:
                continue
            sem, val, src = t
            if src == "pe" and eng == "pe":
                continue
            key = id(sem)
            if seen.get(key, (None, 0))[1] >= val:
                continue
            seen[key] = (sem, val)
            out.append((sem, val))
        return out

    def _deps(self, rd, wr, waits):
        toks = list(waits)
        for b in rd:
            toks.append(b.w)
        for b in wr:
            toks.append(b.w)
            toks.extend(b.r)
        return toks

    def _commit(self, tok, rd, wr):
        for b in rd:
            b.r.append(tok)
        for b in wr:
            b.w = tok
            b.r = []

    def op(self, eng, fn, rd=(), wr=(), waits=(), sig=True):
        ws = self._waits(eng, self._deps(rd, wr, waits))
        tok = None
        if sig:
            if self.cnt[eng] >= SEM_LIMIT:
                self._new_eng_sem(eng)
            self.cnt[eng] += 1
            tok = (self.cur_sem[eng], self.cnt[eng], eng)
            self.last_tok[eng] = tok
        self.streams[eng].append((ws, fn, (tok[0], 1) if tok else None))
        if tok is not None:
            self._commit(tok, rd, wr)
        return tok

    def dma(self, eng, out, in_, dsem, rd=(), wr=(), waits=(), **kw):
        ws = self._waits(eng, self._deps(rd, wr, waits))
        dsem.advance(self)
        tok = (dsem.sem, dsem.n, "dma")
        self.dma_last[id(dsem.sem)] = tok
        self.streams[eng].append((ws, (lambda e: e.dma_start(out=out, in_=in_, **kw)), (dsem.sem, 16)))
        self._commit(tok, rd, wr)
        return tok

    def wait_only(self, eng, toks):
        ws = self._waits(eng, toks)
        if ws:
            self.streams[eng].append((ws, None, None))

    def barrier(self):
        toks = list(self.last_tok.values()) + list(self.dma_last.values())
        for e in self.ENGS:
            ws = self._waits(e, [t for t in toks if not (t[2] == e and e != "pe")] +
                             [t for t in toks if t[2] == e])
            if ws:
                self.streams[e].append((ws, None, None))

    def run(self, eng, h):
        for ws, fn, inc in self.streams[eng]:
            for sem, val in ws:
                h.wait_ge(sem, val)
            if fn is not None:
                ins = fn(h)
                if inc is not None:
                    ins.then_inc(inc[0], inc[1])


def emit_block(nc, P):
    with nc.Block() as block:
        @block.sync
        def _(e):
            P.run("sp", e)

        @block.scalar
        def _(e):
            P.run("act", e)

        @block.tensor
        def _(e):
            P.run("pe", e)

        @block.vector
        def _(e):
            P.run("dve", e)

        @block.gpsimd
        def _(e):
            P.run("pool", e)


def smalls_layout():
    off = {}
    n = 0

    def add(name, cols):
        nonlocal n
        off[name] = (n, cols)
        n += cols

    add("c", 16)
    for L in range(DEPTH):
        for s in range(3):
            add(f"ng{L}_{s}", 16)
        add(f"adab{L}", 144)
    add("final_g", 16)
    for j in range(2):
        add(f"cbin{j}", 32)
        add(f"cwdw{j}", 16 * CONVW)
        add(f"cbdw{j}", 16)
        add(f"clng{j}", 16)
        add(f"clnb{j}", 16)
        add(f"cbout{j}", 16)
    add("gba", 16)
    add("gng", 4)
    add("pb", 16)
    add("pscale", 16)
    add("ident", 128)
    add("mask_f", 64)
    add("mask_b", 64)
    add("is_odd", 1)
    add("is_even", 1)
    return off, n


def col(v):
    v = np.asarray(v, np.float32).reshape(-1, 128)
    return np.ascontiguousarray(v.T)


def pack_smalls(inp, core):
    off, n = smalls_layout()
    b = core // 2
    S = np.zeros((128, n), np.float32)

    def put(name, arr):
        o, c = off[name]
        assert arr.shape == (128, c), (name, arr.shape, c)
        S[:, o:o + c] = arr

    put("c", col(inp["c"][b]))
    for L in range(DEPTH):
        for s in range(3):
            put(f"ng{L}_{s}", col(inp["norm_g"][L, s]))
        put(f"adab{L}", col(inp["ada_b"][L]))
    put("final_g", col(inp["final_g"]))
    for j in range(2):
        put(f"cbin{j}", col(inp["conv_b_in"][j]))
        w = np.asarray(inp["conv_w_dw"][j], np.float32)
        w = w.reshape(CONVW, 16, 128).transpose(2, 1, 0).reshape(128, 16 * CONVW)
        put(f"cwdw{j}", w)
        put(f"cbdw{j}", col(inp["conv_b_dw"][j]))
        put(f"clng{j}", col(inp["conv_ln_g"][j]))
        put(f"clnb{j}", col(inp["conv_ln_b"][j]))
        put(f"cbout{j}", col(inp["conv_b_out"][j]))
    put("gba", col(inp["gla_ba"][0].reshape(-1)))
    put("gng", col(inp["gla_norm_g"][0]))
    put("pb", col(inp["pool_b"][0].reshape(-1)))
    put("pscale", col(inp["pool_scale"][0]))
    put("ident", np.eye(128, dtype=np.float32))
    mf = np.zeros((128, 64), np.float32)
    mb = np.zeros((128, 64), np.float32)
    si = np.arange(64)[:, None]
    ti = np.arange(64)[None, :]
    mf[:64] = (si <= ti)
    mb[:64] = (si > ti)
    put("mask_f", mf)
    put("mask_b", mb)
    S[:, off["is_odd"][0]] = float(core % 2)
    S[:, off["is_even"][0]] = float(1 - core % 2)
    return S


class K:
    pass


def build_program(layers, first, last, mixers=True):
    nc = bass.Bass("TRN2", target_bir_lowering=False)
    soff, NS = smalls_layout()
    k = K()
    k.nc = nc
    k.soff = soff
    dt = nc.dram_tensor
    k.xin = dt("xT_in", [D, T], F32, kind="ExternalInput").ap()
    k.smalls_d = dt("smalls", [128, NS], F32, kind="ExternalInput").ap()
    nl = len(layers)
    k.li = {L: i for i, L in enumerate(layers)}
    k.ada_w = dt("ada_w", [nl, D, 9216 if ADA_SPLIT else 9 * D], F32, kind="ExternalInput").ap()
    k.wg = dt("ffn_w_gate", [nl, 2, D, FF], F32, kind="ExternalInput").ap()
    k.wu = dt("ffn_w_up", [nl, 2, D, FF], F32, kind="ExternalInput").ap()
    k.wd = dt("ffn_w_down", [nl, 2, FF, D], F32, kind="ExternalInput").ap()
    convL = [L for L in layers if L % 3 == 0 and mixers]
    k.ci = {L: i for i, L in enumerate(convL)}
    if convL:
        k.conv_w_in = dt("conv_w_in", [len(convL), D, 2 * D], F32, kind="ExternalInput").ap()
        k.conv_w_out = dt("conv_w_out", [len(convL), D, D], F32, kind="ExternalInput").ap()
    if 1 in layers and mixers:
        k.gla_w_in = dt("gla_w_in", [1, D, 6144], F32, kind="ExternalInput").ap()
        k.gla_wa1 = dt("gla_wa1", [1, 2, D, 16], F32, kind="ExternalInput").ap()
        k.gla_wa2 = dt("gla_wa2", [1, 2, 16, 1024], F32, kind="ExternalInput").ap()
        k.gla_w_out = dt("gla_w_out", [1, D, D], F32, kind="ExternalInput").ap()
    if 2 in layers and mixers:
        k.pool_w = dt("pool_w", [1, 4, 512, 512], F32, kind="ExternalInput").ap()
    k.out = dt("outT", [D, T], F32, kind="ExternalOutput").ap()

    P = Prog(nc)
    k.P = P
    with contextlib.ExitStack() as es:
        def sb(name, shape, dtype):
            return es.enter_context(nc.sbuf_tensor(name, shape, dtype))

        k.xT = sb("xT", [128, KC, T], F32)
        k.smalls = sb("smalls_sb", [128, NS], F32)
        k.mods = sb("mods", [128, DEPTH, 144], F32)
        k.modA = sb("modA", [128, DEPTH, 3, KC], F32)
        k.modG = sb("modG", [128, DEPTH, 3, KC], F32)
        k.ones = sb("ones", [128, 128], F32)
        k.epsc = sb("epsc", [128, 1], F32)
        k.zeroc = sb("zeroc", [128, 1], F32)
        k.cact = sb("cact", [128, KC], BF16)
        k.identb = sb("identb", [128, 128], BF16)
        k.ctmp = sb("ctmp", [128, KC], F32)
        ARENA_WORDS = (nc.sbuf_bytes_remaining - 128) // 4
        k.arena = sb("arena", [128, ARENA_WORDS], F32)
        k.arena_bytes = ARENA_WORDS * 4
        k.pb = [es.enter_context(nc.psum_tensor(f"pb{i}", [128, 512], F32)) for i in range(8)]
        k.bpb = [Buf(f"pb{i}") for i in range(8)]
        k.bx = Buf("x")
        k.bsm = Buf("smalls")
        k.bmods = Buf("mods")
        k.bconst = Buf("const")
        k.dsem = {}

        prologue(k, layers)
        for L in layers:
            if FFN_ON:
                ffn_stage(k, L, 0)
            if mixers:
                kind = L % 3
                if kind == 0:
                    conv_stage(k, L, L // 3)
                elif kind == 1:
                    gla_stage(k, L)
                else:
                    pool_stage(k, L)
            if FFN_ON:
                ffn_stage(k, L, 1)
        epilogue(k, last)
        emit_block(nc, P)
    P.close()
    return nc


def carve(k, byte_off, shape, dtype):
    n = int(np.prod(shape[1:]))
    esz = 4 if dtype == F32 else 2
    nbytes = n * esz
    assert byte_off % 4 == 0 and nbytes % 4 == 0
    assert byte_off + nbytes <= k.arena_bytes, (byte_off, nbytes, k.arena_bytes)
    ap = k.arena[:, byte_off // 4:(byte_off + nbytes) // 4]
    if dtype != F32:
        ap = ap.bitcast(dtype)
    if len(shape) == 3:
        ap = ap.rearrange("p (a b) -> p a b", a=shape[1])
    elif len(shape) == 4:
        ap = ap.rearrange("p (a b c) -> p a b c", a=shape[1], b=shape[2])
    return ap


def sm(k, name, j0=0, n=None):
    o, c = k.soff[name]
    if n is None:
        n = c - j0
    return k.smalls[:, o + j0:o + j0 + n]


def getds(k, name):
    if name not in k.dsem:
        k.dsem[name] = DmaSem(name)
    return k.dsem[name]


def prologue(k, layers):
    P, nc = k.P, k.nc
    P.dma("sp", k.smalls[:, :], k.smalls_d[:, :], getds(k, "sm"), wr=[k.bsm])
    xv = k.xin.rearrange("(kc p) t -> p kc t", p=128)
    for q in range(4):
        P.dma("sp", k.xT[:, q * 4:(q + 1) * 4, :], xv[:, q * 4:(q + 1) * 4, :], getds(k, "xin"), wr=[k.bx])
    P.op("dve", lambda e: e.memset(k.ones[:, :], 1.0), wr=[k.bconst])
    P.op("dve", lambda e: e.memset(k.epsc[:, :], EPS), wr=[k.bconst])
    P.op("dve", lambda e: e.memset(k.zeroc[:, :], 0.0), wr=[k.bconst])
    P.op("dve", lambda e: e.tensor_copy(out=k.identb[:, :], in_=sm(k, "ident")), rd=[k.bsm], wr=[k.bconst])
    P.op("act", lambda e: e.activation(out=k.ctmp[:, :], in_=sm(k, "c"), func=AF.Silu), rd=[k.bsm], wr=[k.bconst])
    P.op("dve", lambda e: e.tensor_copy(out=k.cact[:, :], in_=k.ctmp[:, :]), rd=[k.bconst], wr=[k.bconst])
    if ADA_SPLIT:
        mods_split(k, layers)
    else:
        mods_local(k, layers)


def mods_finish(k, L):
    P = k.P
    for s3 in range(3):
        P.op("dve", (lambda e, L=L, s3=s3: e.scalar_tensor_tensor(
            out=k.modA[:, L, s3, :], in0=k.mods[:, L, (3 * s3 + 1) * 16:(3 * s3 + 2) * 16], scalar=1.0,
            in1=sm(k, f"ng{L}_{s3}"), op0=ALU.add, op1=ALU.mult)), rd=[k.bmods, k.bsm], wr=[k.bmods])
        P.op("dve", (lambda e, L=L, s3=s3: e.tensor_scalar(
            out=k.modG[:, L, s3, :], in0=k.mods[:, L, (3 * s3 + 2) * 16:(3 * s3 + 3) * 16],
            scalar1=(1.0 if s3 == 1 else 0.5), scalar2=None, op0=ALU.mult)), rd=[k.bmods], wr=[k.bmods])


def mods_split(k, layers):
    P, nc = k.P, k.nc
    nl = len(layers)
    MS = carve(k, 0, [128, nl * 72], F32)
    WB0 = 4096
    wslots = [carve(k, WB0 + s * 16384, [128, KC, 512], BF16) for s in range(2)]
    bw = [Buf(), Buf()]
    dsw = [getds(k, "adaw0"), getds(k, "adaw1")]
    bMS, bcin, bcout = Buf(), Buf(), Buf()
    NBLK = 18
    blocks = [(li, b) for li in range(nl) for b in range(NBLK)]
    psm = k.pb[0]

    def load(i):
        li, b = blocks[i]
        s = i % 2
        src = k.ada_w[li].rearrange("(kc p) n -> p kc n", p=128)[:, :, b * 512:(b + 1) * 512]
        P.dma("pool", wslots[s][:, :, :], src, dsw[s], wr=[bw[s]])

    for i in range(min(2, len(blocks))):
        load(i)
    for i, (li, b) in enumerate(blocks):
        s = i % 2
        for dc in range(4):
            c0 = li * 72 + b * 4 + dc
            for kc in range(KC):
                lastk = kc == KC - 1
                P.op("pe", (lambda e, s=s, dc=dc, kc=kc, c0=c0: e.matmul(
                    psm[:, c0:c0 + 1], lhsT=wslots[s][:, kc, dc * 128:(dc + 1) * 128],
                    rhs=k.cact[:, kc:kc + 1], start=(kc == 0), stop=(kc == KC - 1))),
                    rd=[bw[s], k.bconst] if kc == 0 or lastk else [],
                    wr=[k.bpb[0]] if (lastk and dc == 3) else [],
                    waits=(([k.bpb[0].w] + k.bpb[0].r) if (kc == 0 and dc == 0 and i == 0) else []),
                    sig=(lastk and dc == 3))
        if i + 2 < len(blocks):
            load(i + 2)
    P.op("dve", lambda e: e.tensor_copy(out=MS[:, :], in_=psm[:, 0:nl * 72]), rd=[k.bpb[0]], wr=[bMS])
    cc_in = nc.dram_tensor("ada_cc_in", [128, nl * 72], F32).ap()
    cc_out = nc.dram_tensor("ada_cc_out", [256, nl * 72], F32).ap()
    P.dma("sp", cc_in, MS[:, :], getds(k, "adacc"), rd=[bMS], wr=[bcin])
    P.op("pool", lambda e: e.collective_compute("AllGather", ALU.bypass, replica_groups=PAIRS,
                                                ins=[cc_in.opt()], outs=[cc_out.opt()]), rd=[bcin], wr=[bcout])
    ccv = cc_out.rearrange("(r p) n -> p r n", p=128)
    for li, L in enumerate(layers):
        P.dma("sp", k.mods[:, L, :].rearrange("p (r j) -> p r j", r=2), ccv[:, :, li * 72:(li + 1) * 72],
              getds(k, "adacc"), rd=[bcout], wr=[k.bmods])
    for li, L in enumerate(layers):
        P.op("dve", (lambda e, L=L: e.tensor_tensor(out=k.mods[:, L, :], in0=k.mods[:, L, :],
                                                     in1=sm(k, f"adab{L}"), op=ALU.add)),
             rd=[k.bsm], wr=[k.bmods])
        mods_finish(k, L)


def mods_local(k, layers):
    P, nc = k.P, k.nc
    NBLK = 36
    wslots = [carve(k, s * 16384, [128, KC, 512], BF16) for s in range(2)]
    bw = [Buf("adaw0"), Buf("adaw1")]
    dsw = [getds(k, "adaw0"), getds(k, "adaw1")]
    blocks = [(L, b) for L in layers for b in range(NBLK)]
    psm = k.pb[0]

    def load(i):
        L, b = blocks[i]
        s = i % 2
        src = k.ada_w[k.li[L]].rearrange("(kc p) n -> p kc n", p=128)[:, :, b * 512:(b + 1) * 512]
        P.dma("pool", wslots[s][:, :, :], src, dsw[s], wr=[bw[s]])

    for i in range(min(2, len(blocks))):
        load(i)
    for i, (L, b) in enumerate(blocks):
        s = i % 2
        for dc in range(4):
            colj = (b * 4 + dc)
            for kc in range(KC):
                lastk = kc == KC - 1
                P.op("pe", (lambda e, s=s, dc=dc, kc=kc, colj=colj: e.matmul(
                    psm[:, colj:colj + 1], lhsT=wslots[s][:, kc, dc * 128:(dc + 1) * 128],
                    rhs=k.cact[:, kc:kc + 1], start=(kc == 0), stop=(kc == KC - 1))),
                    rd=[bw[s], k.bconst] if kc == 0 or lastk else [],
                    wr=[k.bpb[0]] if (lastk and dc == 3) else [],
                    waits=(([k.bpb[0].w] + k.bpb[0].r) if (kc == 0 and dc == 0 and b == 0) else []),
                    sig=(lastk and dc == 3))
        if b == NBLK - 1:
            P.op("dve", (lambda e, L=L: e.tensor_tensor(out=k.mods[:, L, :], in0=psm[:, 0:144],
                                                         in1=sm(k, f"adab{L}"), op=ALU.add)),
                 rd=[k.bpb[0], k.bsm], wr=[k.bmods])
            mods_finish(k, L)
        if i + 2 < len(blocks):
            load(i + 2)


def norm_modulate(k, A_fn, S_fn, out_fn, base, out_is_bf16=True, bout=None):
    P = k.P
    sq = [carve(k, base + i * 2048, [128, 512], F32) for i in range(2)]
    rstd = [carve(k, base + 4096 + i * 2048, [128, 512], F32) for i in range(2)]
    tmp = [carve(k, base + 8192 + i * 2048, [128, 512], F32) for i in range(2)]
    bsq = [Buf(), Buf()]
    brs = [Buf(), Buf()]
    btmp = [Buf(), Buf()]
    if bout is None:
        bout = Buf()
    if not isinstance(bout, (list, tuple)):
        bout = [bout, bout]
    for half in range(2):
        ps = k.pb[6 + half]
        bps = k.bpb[6 + half]
        hs = slice(half * 512, (half + 1) * 512)
        for kc in range(KC):
            i = kc % 2
            if kc % 2 == 0:
                P.op("act", (lambda e, kc=kc, i=i, hs=hs: e.activation(out=sq[i][:, :], in_=k.xT[:, kc, hs], func=AF.Square)),
                     rd=[k.bx], wr=[bsq[i]])
            else:
                P.op("dve", (lambda e, kc=kc, i=i, hs=hs: e.tensor_tensor(out=sq[i][:, :], in0=k.xT[:, kc, hs],
                                                                  in1=k.xT[:, kc, hs], op=ALU.mult)),
                     rd=[k.bx], wr=[bsq[i]])
            P.op("pe", (lambda e, kc=kc, i=i, ps=ps: e.matmul(ps[:, :], lhsT=k.ones[:, :], rhs=sq[i][:, :],
                                                             start=(kc == 0), stop=(kc == KC - 1))),
                 rd=[bsq[i], k.bconst], wr=[bps] if kc == KC - 1 else [],
                 waits=([bps.w] + bps.r) if kc == 0 else [], sig=True)
    for half in range(2):
        ps = k.pb[6 + half]
        bps = k.bpb[6 + half]
        hs = slice(half * 512, (half + 1) * 512)
        P.op("act", (lambda e, half=half, ps=ps: e.activation(out=rstd[half][:, :], in_=ps[:, :], func=AF.Sqrt,
                                                              bias=k.epsc[:, 0:1], scale=1.0 / D)),
             rd=[bps, k.bconst], wr=[brs[half]])
        P.op("dve", (lambda e, half=half: e.reciprocal(out=rstd[half][:, :], in_=rstd[half][:, :])),
             rd=[brs[half]], wr=[brs[half]])
        for kc in range(KC):
            i = kc % 2
            P.op("dve", (lambda e, kc=kc, i=i, half=half, hs=hs: e.tensor_tensor(out=tmp[i][:, :], in0=k.xT[:, kc, hs],
                                                                         in1=rstd[half][:, :], op=ALU.mult)),
                 rd=[k.bx, brs[half]], wr=[btmp[i]])
            if S_fn is not None:
                P.op("act", (lambda e, kc=kc, i=i, half=half: e.activation(
                    out=out_fn(kc, half), in_=tmp[i][:, :], func=AF.Identity, bias=S_fn(kc), scale=A_fn(kc))),
                    rd=[btmp[i], k.bmods, k.bsm], wr=[bout[half]])
            else:
                P.op("act", (lambda e, kc=kc, i=i, half=half: e.activation(
                    out=out_fn(kc, half), in_=tmp[i][:, :], func=AF.Identity, bias=k.zeroc[:, 0:1], scale=A_fn(kc))),
                    rd=[btmp[i], k.bmods, k.bsm], wr=[bout[half]])
    return bout


def ffn_stage(k, L, which):
    P = k.P
    P.barrier()
    s3 = 0 if which == 0 else 2
    hT = carve(k, 0, [128, KC, T], BF16)
    WG = [carve(k, 32768 + s * 8192, [128, KC, 256], BF16) for s in range(2)]
    WU = [carve(k, 49152 + s * 8192, [128, KC, 256], BF16) for s in range(2)]
    WD = [carve(k, 65536 + s * 8192, [128, 8, 512], BF16) for s in range(2)]
    ACT = carve(k, 81920, [128, 8, T], BF16)
    SIL = [carve(k, 98304 + s * 2048, [128, 512], F32) for s in range(3)]
    NB = 104448
    bh = [Buf("hT0"), Buf("hT1")]
    bWG = [Buf(), Buf()]
    bWU = [Buf(), Buf()]
    bWD = [Buf(), Buf()]
    bsil = [Buf(), Buf(), Buf()]
    dWG = [getds(k, "wg0"), getds(k, "wg1")]
    dWU = [getds(k, "wu0"), getds(k, "wu1")]
    dWD = [getds(k, "wd0"), getds(k, "wd1")]
    wgv = k.wg[k.li[L], which].rearrange("(kc p) f -> p kc f", p=128)
    wuv = k.wu[k.li[L], which].rearrange("(kc p) f -> p kc f", p=128)
    wdv = k.wd[k.li[L], which].rearrange("(fc p) d -> p fc d", p=128)

    groups = [(g * 2, min(2, FC - g * 2)) for g in range((FC + 1) // 2)]
    phases = []
    for p0 in range(0, FC, 8):
        phases.append((p0, min(8, FC - p0)))
    gl = {"n": 0}

    def load_group(gi):
        f0, nf = groups[gi]
        s = gi % 2
        P.dma("pool", WG[s][:, :, 0:nf * 128], wgv[:, :, f0 * 128:(f0 + nf) * 128], dWG[s], wr=[bWG[s]])
        P.dma("pool", WU[s][:, :, 0:nf * 128], wuv[:, :, f0 * 128:(f0 + nf) * 128], dWU[s], wr=[bWU[s]])

    dl = {"n": 0}
    dlist = [(pi, dg) for pi in range(len(phases)) for dg in range(4)]

    def load_wd(i):
        pi, dg = dlist[i]
        p0, pn = phases[pi]
        s = i % 2
        P.dma("pool", WD[s][:, 0:pn, :], wdv[:, p0:p0 + pn, dg * 512:(dg + 1) * 512], dWD[s], wr=[bWD[s]])

    load_group(0)
    load_group(1)
    next_group = 2
    load_wd(0)
    load_wd(1)
    next_wd = 2
    norm_modulate(k, lambda kc: k.modA[:, L, s3, kc:kc + 1],
                  lambda kc: k.mods[:, L, (3 * s3) * 16 + kc:(3 * s3) * 16 + kc + 1],
                  lambda kc, half: hT[:, kc, half * 512:(half + 1) * 512], NB, bout=bh)
    unit = 0
    dunit = 0
    bact = Buf("act")
    gi = 0
    for pi, (p0, pn) in enumerate(phases):
        ngr = (pn + 1) // 2
        for _ in range(ngr):
            f0, nf = groups[gi]
            s = gi % 2
            for half in range(2):
                hs = slice(half * 512, (half + 1) * 512)
                for fi in range(nf):
                    f = f0 + fi
                    slot = unit % 3
                    unit += 1
                    pg, pu = k.pb[2 * slot], k.pb[2 * slot + 1]
                    bpg, bpu = k.bpb[2 * slot], k.bpb[2 * slot + 1]
                    for (W, bW, ps, bps) in ((WG[s], bWG[s], pg, bpg), (WU[s], bWU[s], pu, bpu)):
                        for kc in range(KC):
                            P.op("pe", (lambda e, W=W, ps=ps, kc=kc, fi=fi, hs=hs: e.matmul(
                                ps[:, :], lhsT=W[:, kc, fi * 128:(fi + 1) * 128], rhs=hT[:, kc, hs],
                                start=(kc == 0), stop=(kc == KC - 1))),
                                rd=[bW, bh[half]] if kc in (0, KC - 1) else [],
                                wr=[bps] if kc == KC - 1 else [],
                                waits=([bps.w] + bps.r) if kc == 0 else [], sig=(kc == KC - 1))
                    P.op("act", (lambda e, slot=slot, pg=pg: e.activation(out=SIL[slot][:, :], in_=pg[:, :], func=AF.Silu)),
                         rd=[bpg], wr=[bsil[slot]])
                    P.op("dve", (lambda e, slot=slot, pu=pu, f=f, p0=p0, hs=hs: e.tensor_tensor(
                        out=ACT[:, f - p0, hs], in0=SIL[slot][:, :], in1=pu[:, :], op=ALU.mult)),
                        rd=[bsil[slot], bpu], wr=[bact])
            gi += 1
            if next_group < len(groups):
                load_group(next_group)
                next_group += 1
        for dg in range(4):
            i = pi * 4 + dg
            s = i % 2
            for dc4 in range(4):
                dc = dg * 4 + dc4
                for half in range(2):
                    hs = slice(half * 512, (half + 1) * 512)
                    b = 6 + (dunit % 2)
                    dunit += 1
                    ps, bps = k.pb[b], k.bpb[b]
                    for fi in range(pn):
                        P.op("pe", (lambda e, ps=ps, fi=fi, dc4=dc4, hs=hs, s=s, pn=pn: e.matmul(
                            ps[:, :], lhsT=WD[s][:, fi, dc4 * 128:(dc4 + 1) * 128], rhs=ACT[:, fi, hs],
                            start=(fi == 0), stop=(fi == pn - 1))),
                            rd=[bWD[s], bact] if fi in (0, pn - 1) else [],
                            wr=[bps] if fi == pn - 1 else [],
                            waits=([bps.w] + bps.r) if fi == 0 else [], sig=(fi == pn - 1))
                    P.op("dve", (lambda e, ps=ps, dc=dc, hs=hs: e.scalar_tensor_tensor(
                        out=k.xT[:, dc, hs], in0=ps[:, :], scalar=k.modG[:, L, s3, dc:dc + 1],
                        in1=k.xT[:, dc, hs], op0=ALU.mult, op1=ALU.add)),
                        rd=[bps, k.bmods], wr=[k.bx])
            if next_wd < len(dlist):
                load_wd(next_wd)
                next_wd += 1


def conv_stage(k, L, j):
    P, nc = k.P, k.nc
    P.barrier()
    ci = k.ci[L]
    UW = 1056
    hT = carve(k, 0, [128, KC, T], BF16)
    U = carve(k, 32768, [128, KC, UW], BF16)
    CB = 32768 + 33792
    SIL = [carve(k, CB + 12288 + s * 2048, [128, 512], F32) for s in range(3)]
    SBh = carve(k, CB + 20480, [128, KC, 32], BF16)
    RBh = carve(k, CB + 21504, [128, 2, KC, 32], BF16)
    WB = CB + 32768
    WG = [carve(k, WB + s * 8192, [128, KC, 256], BF16) for s in range(2)]
    WU = [carve(k, WB + 16384 + s * 8192, [128, KC, 256], BF16) for s in range(2)]
    bh, bU = [Buf("hT0"), Buf("hT1")], Buf("U")
    bWG, bWU = [Buf(), Buf()], [Buf(), Buf()]
    bsil = [Buf(), Buf(), Buf()]
    dWG = [getds(k, "wg0"), getds(k, "wg1")]
    dWU = [getds(k, "wu0"), getds(k, "wu1")]
    win = k.conv_w_in[ci].rearrange("(kc p) f -> p kc f", p=128)
    wout = k.conv_w_out[ci].rearrange("(kc p) f -> p kc f", p=128)
    cbin = f"cbin{j}"

    def load_group(gi):
        s = gi % 2
        P.dma("pool", WG[s][:, :, :], win[:, :, D + gi * 256:D + (gi + 1) * 256], dWG[s], wr=[bWG[s]])
        P.dma("pool", WU[s][:, :, :], win[:, :, gi * 256:(gi + 1) * 256], dWU[s], wr=[bWU[s]])

    load_group(0)
    load_group(1)
    norm_modulate(k, lambda kc: k.modA[:, L, 1, kc:kc + 1],
                  lambda kc: k.mods[:, L, 48 + kc:48 + kc + 1],
                  lambda kc, half: hT[:, kc, half * 512:(half + 1) * 512], CB, bout=bh)
    unit = 0
    for gi in range(8):
        s = gi % 2
        for half in range(2):
            hs = slice(half * 512, (half + 1) * 512)
            for fi in range(2):
                dc = gi * 2 + fi
                slot = unit % 3
                unit += 1
                pg, pu = k.pb[2 * slot], k.pb[2 * slot + 1]
                bpg, bpu = k.bpb[2 * slot], k.bpb[2 * slot + 1]
                for (W, bW, ps, bps) in ((WG[s], bWG[s], pg, bpg), (WU[s], bWU[s], pu, bpu)):
                    for kc in range(KC):
                        P.op("pe", (lambda e, W=W, ps=ps, kc=kc, fi=fi, hs=hs: e.matmul(
                            ps[:, :], lhsT=W[:, kc, fi * 128:(fi + 1) * 128], rhs=hT[:, kc, hs],
                            start=(kc == 0), stop=(kc == KC - 1))),
                            rd=[bW, bh[half]] if kc in (0, KC - 1) else [],
                            wr=[bps] if kc == KC - 1 else [],
                            waits=([bps.w] + bps.r) if kc == 0 else [], sig=(kc == KC - 1))
                P.op("act", (lambda e, slot=slot, pg=pg, dc=dc: e.activation(
                    out=SIL[slot][:, :], in_=pg[:, :], func=AF.Sigmoid, bias=sm(k, cbin, 16 + dc, 1), scale=1.0)),
                    rd=[bpg, k.bsm], wr=[bsil[slot]])
                P.op("dve", (lambda e, slot=slot, pu=pu, dc=dc, half=half: e.scalar_tensor_tensor(
                    out=U[:, dc, 16 + half * 512:16 + (half + 1) * 512], in0=pu[:, :], scalar=sm(k, cbin, dc, 1),
                    in1=SIL[slot][:, :], op0=ALU.add, op1=ALU.mult)),
                    rd=[bsil[slot], bpu, k.bsm], wr=[bU])
        if gi + 2 < 8:
            load_group(gi + 2)
    cc_in = nc.dram_tensor(f"cvh_in{j}", [128, 512], BF16)
    cc_out = nc.dram_tensor(f"cvh_out{j}", [256, 512], BF16)
    bSB, bRB, bcin, bcout = Buf(), Buf(), Buf(), Buf()
    P.op("dve", lambda e: e.tensor_copy(out=SBh[:, :, 0:16], in_=U[:, :, 16:32]), rd=[bU], wr=[bSB])
    P.op("dve", lambda e: e.tensor_copy(out=SBh[:, :, 16:32], in_=U[:, :, 1024:1040]), rd=[bU], wr=[bSB])
    P.dma("sp", cc_in.ap(), SBh.rearrange("p a b -> p (a b)"), getds(k, "halo"), rd=[bSB], wr=[bcin])
    P.op("pool", lambda e: e.collective_compute("AllGather", ALU.bypass, replica_groups=PAIRS,
                                                ins=[cc_in.ap().opt()], outs=[cc_out.ap().opt()]),
         rd=[bcin], wr=[bcout])
    P.dma("sp", RBh.rearrange("p r a b -> p r (a b)"), cc_out.ap().rearrange("(r p) n -> p r n", p=128),
          getds(k, "halo"), rd=[bcout], wr=[bRB])
    P.op("dve", lambda e: e.tensor_scalar(out=U[:, :, 1:16], in0=RBh[:, 0, :, 17:32], scalar1=sm(k, "is_odd"),
                                          scalar2=None, op0=ALU.mult), rd=[bRB, k.bsm], wr=[bU])
    P.op("dve", lambda e: e.tensor_scalar(out=U[:, :, 1040:1055], in0=RBh[:, 1, :, 0:15], scalar1=sm(k, "is_even"),
                                          scalar2=None, op0=ALU.mult), rd=[bRB, k.bsm], wr=[bU])
    P.barrier()
    SN = carve(k, 0, [128, KC, 512], BF16)
    DG = [carve(k, 16384 + s * 7936, [128, CONVW, 128], BF16) for s in range(2)]
    C = carve(k, CB, [128, KC, 512], F32)
    WO = [carve(k, WB + s * 8192, [128, KC, 256], BF16) for s in range(2)]
    TB = WB + 16384
    csq = [carve(k, TB + i * 2048, [128, 512], F32) for i in range(2)]
    MEAN = carve(k, TB + 4096, [128, 512], F32)
    T1 = carve(k, TB + 6144, [128, 512], F32)
    RSTD = carve(k, TB + 8192, [128, 512], F32)
    NMR = carve(k, TB + 10240, [128, 512], F32)
    tt = [carve(k, TB + 12288 + i * 2048, [128, 512], F32) for i in range(2)]
    bSN, bC = Buf("SN"), Buf("C")
    bDG = [Buf(), Buf()]
    bWO = [Buf(), Buf()]
    bcsq = [Buf(), Buf()]
    bst, btt = Buf("stats"), [Buf(), Buf()]
    dWO = [getds(k, "wo0"), getds(k, "wo1")]
    ident3 = sm(k, "ident").unsqueeze(1).to_broadcast([128, CONVW, 128])
    wo_loads = [(half, g) for half in range(2) for g in range(8)]

    def load_wo(i):
        half, g = wo_loads[i]
        s = i % 2
        P.dma("pool", WO[s][:, :, :], wout[:, :, g * 256:(g + 1) * 256], dWO[s], wr=[bWO[s]])

    load_wo(0)
    load_wo(1)
    nwo = 2
    cu = 0
    ou = 0
    for half in range(2):
        S1, S2 = k.pb[4], k.pb[5]
        bS1, bS2 = k.bpb[4], k.bpb[5]
        for dc in range(KC):
            s = dc % 2
            w3 = sm(k, f"cwdw{j}", dc * CONVW, CONVW).unsqueeze(2).to_broadcast([128, CONVW, 128])
            P.op("pool" if dc % 2 == 0 else "dve",
                 (lambda e, s=s, w3=w3: e.tensor_tensor(out=DG[s][:, :, :], in0=ident3, in1=w3, op=ALU.mult)),
                 rd=[k.bsm], wr=[bDG[s]])
            cps, bcps = k.pb[cu % 3], k.bpb[cu % 3]
            cu += 1
            for t in range(CONVW):
                c0 = 1 + half * 512 + t
                P.op("pe", (lambda e, cps=cps, s=s, t=t, dc=dc, c0=c0: e.matmul(
                    cps[:, :], lhsT=DG[s][:, t, :], rhs=U[:, dc, c0:c0 + 512], start=(t == 0), stop=(t == CONVW - 1))),
                    rd=[bDG[s], bU] if t in (0, CONVW - 1) else [],
                    wr=[bcps] if t == CONVW - 1 else [],
                    waits=([bcps.w] + bcps.r) if t == 0 else [], sig=(t == CONVW - 1))
            P.op("act", (lambda e, cps=cps, dc=dc: e.activation(out=C[:, dc, :], in_=cps[:, :], func=AF.Identity,
                                                               bias=sm(k, f"cbdw{j}", dc, 1), scale=1.0)),
                 rd=[bcps, k.bsm], wr=[bC])
            i = dc % 2
            P.op("dve", (lambda e, dc=dc, i=i: e.tensor_tensor(out=csq[i][:, :], in0=C[:, dc, :], in1=C[:, dc, :],
                                                              op=ALU.mult)), rd=[bC], wr=[bcsq[i]])
            P.op("pe", (lambda e, dc=dc: e.matmul(S1[:, :], lhsT=k.ones[:, :], rhs=C[:, dc, :],
                                                  start=(dc == 0), stop=(dc == KC - 1))),
                 rd=[bC, k.bconst], wr=[bS1] if dc == KC - 1 else [],
                 waits=([bS1.w] + bS1.r) if dc == 0 else [], sig=True)
            P.op("pe", (lambda e, dc=dc, i=i: e.matmul(S2[:, :], lhsT=k.ones[:, :], rhs=csq[i][:, :],
                                                       start=(dc == 0), stop=(dc == KC - 1))),
                 rd=[bcsq[i], k.bconst], wr=[bS2] if dc == KC - 1 else [],
                 waits=([bS2.w] + bS2.r) if dc == 0 else [], sig=True)
        P.op("dve", lambda e: e.tensor_scalar(out=MEAN[:, :], in0=S1[:, :], scalar1=1.0 / D, scalar2=None,
                                              op0=ALU.mult), rd=[bS1], wr=[bst])
        P.op("dve", lambda e: e.tensor_tensor(out=T1[:, :], in0=MEAN[:, :], in1=MEAN[:, :], op=ALU.mult),
             rd=[bst], wr=[bst])
        P.op("dve", lambda e: e.scalar_tensor_tensor(out=T1[:, :], in0=S2[:, :], scalar=1.0 / D, in1=T1[:, :],
                                                     op0=ALU.mult, op1=ALU.subtract), rd=[bS2, bst], wr=[bst])
        P.op("act", lambda e: e.activation(out=RSTD[:, :], in_=T1[:, :], func=AF.Sqrt, bias=k.epsc[:, 0:1], scale=1.0),
             rd=[bst, k.bconst], wr=[bst])
        P.op("dve", lambda e: e.reciprocal(out=RSTD[:, :], in_=RSTD[:, :]), rd=[bst], wr=[bst])
        P.op("dve", lambda e: e.scalar_tensor_tensor(out=NMR[:, :], in0=MEAN[:, :], scalar=-1.0, in1=RSTD[:, :],
                                                     op0=ALU.mult, op1=ALU.mult), rd=[bst], wr=[bst])
        for dc in range(KC):
            i = dc % 2
            P.op("dve", (lambda e, dc=dc, i=i: e.tensor_tensor(out=tt[i][:, :], in0=C[:, dc, :], in1=RSTD[:, :],
                                                              op=ALU.mult)), rd=[bC, bst], wr=[btt[i]])
            P.op("dve", (lambda e, i=i: e.tensor_tensor(out=tt[i][:, :], in0=tt[i][:, :], in1=NMR[:, :], op=ALU.add)),
                 rd=[bst], wr=[btt[i]])
            P.op("act", (lambda e, dc=dc, i=i: e.activation(out=SN[:, dc, :], in_=tt[i][:, :], func=AF.Silu,
                                                           bias=sm(k, f"clnb{j}", dc, 1), scale=sm(k, f"clng{j}", dc, 1))),
                 rd=[btt[i], k.bsm], wr=[bSN])
        hs = slice(half * 512, (half + 1) * 512)
        for g in range(8):
            i = half * 8 + g
            s = i % 2
            for fi in range(2):
                dc = g * 2 + fi
                b = 6 + ou % 2
                ou += 1
                yps, byps = k.pb[b], k.bpb[b]
                for kc in range(KC):
                    P.op("pe", (lambda e, yps=yps, s=s, kc=kc, fi=fi: e.matmul(
                        yps[:, :], lhsT=WO[s][:, kc, fi * 128:(fi + 1) * 128], rhs=SN[:, kc, :],
                        start=(kc == 0), stop=(kc == KC - 1))),
                        rd=[bWO[s], bSN] if kc in (0, KC - 1) else [],
                        wr=[byps] if kc == KC - 1 else [],
                        waits=([byps.w] + byps.r) if kc == 0 else [], sig=(kc == KC - 1))
                ti = ou % 2
                P.op("dve", (lambda e, yps=yps, dc=dc, ti=ti: e.tensor_scalar(
                    out=tt[ti][:, :], in0=yps[:, :], scalar1=sm(k, f"cbout{j}", dc, 1),
                    scalar2=k.modG[:, L, 1, dc:dc + 1], op0=ALU.add, op1=ALU.mult)),
                    rd=[byps, k.bsm, k.bmods], wr=[btt[ti]])
                P.op("dve", (lambda e, dc=dc, ti=ti, hs=hs: e.tensor_tensor(
                    out=k.xT[:, dc, hs], in0=k.xT[:, dc, hs], in1=tt[ti][:, :], op=ALU.add)),
                    rd=[btt[ti]], wr=[k.bx])
            if nwo < len(wo_loads):
                load_wo(nwo)
                nwo += 1


def gla_stage(k, L):
    P, nc = k.P, k.nc
    P.barrier()
    win = k.gla_w_in[0].rearrange("(kc p) f -> p kc f", p=128)
    wout = k.gla_w_out[0].rearrange("(kc p) f -> p kc f", p=128)
    hT = carve(k, 0, [128, KC, T], BF16)
    OH = carve(k, 32768, [128, 4, T], F32)
    WS = [carve(k, 49152 + s * 8192, [128, KC, 256], BF16) for s in range(2)]
    Qb = carve(k, 65536, [128, 2, T], BF16)
    Kb = carve(k, 69632, [128, 2, T], BF16)
    LG = carve(k, 73728, [128, T], F32)
    Bc = carve(k, 77824, [128, T], F32)
    T1 = carve(k, 81920, [128, T], F32)
    QE = carve(k, 86016, [128, 2, T], BF16)
    KE = carve(k, 90112, [128, 2, T], BF16)
    KL = carve(k, 94208, [128, 2, T], BF16)
    QG = carve(k, 98304, [128, 2, T], BF16)
    Vt = carve(k, 102400, [128, 16, 512], BF16)
    S = carve(k, 118784, [128, 2, 512], F32)
    Sb = carve(k, 122880, [128, 2, 512], BF16)
    KLT = [carve(k, 124928 + i * 512, [128, 2, 128], BF16) for i in range(2)]
    ST = [carve(k, 125952 + i * 128, [128, 64], BF16) for i in range(2)]
    Z1 = carve(k, 126208, [128, 2, T], BF16)
    WA1 = carve(k, 130304, [128, 2, KC, 16], BF16)
    WA2 = carve(k, 131328, [128, 2, 256], BF16)
    SMT = carve(k, 132352, [128, 8, 16], F32)
    TOT, INC, PX, EP = SMT[:, 0, :], SMT[:, 1, :], SMT[:, 2, :], SMT[:, 3, :]
    DL = SMT[:, 4:6, :]
    ONE16 = SMT[:, 6, :]
    M01 = carve(k, 118784, [128, T], F32)
    bh, bOH, bQ, bK, bLG, bB, bT1 = [Buf("hT0"), Buf("hT1")], Buf("OH"), Buf("Q"), Buf("K"), Buf("LG"), Buf("B"), Buf("T1")
    bQE, bKE, bKL, bQG, bV, bS, bSb = Buf("QE"), Buf("KE"), Buf("KL"), Buf("QG"), Buf("Vt"), Buf("S"), Buf("Sb")
    bKLT, bST = [Buf(), Buf()], [Buf(), Buf()]
    bZ1, bWA1, bWA2, bSM = Buf(), Buf(), Buf(), Buf()
    bM01 = bS
    bWS = [Buf(), Buf()]
    dWS = [getds(k, "gws0"), getds(k, "gws1")]
    qg_d = nc.dram_tensor("gla_qg", [4, 2, 128, 2 * T], BF16).ap()
    o_d = nc.dram_tensor("gla_o", [4, 128, 4 * T], F32).ap()
    st_in = [[nc.dram_tensor(f"gla_st_in{h}_{d}", [128, 1024], F32).ap() for d in range(2)] for h in range(4)]
    st_out = [[nc.dram_tensor(f"gla_st_out{h}_{d}", [256, 1024], F32).ap() for d in range(2)] for h in range(4)]
    bsti = [[Buf() for d in range(2)] for h in range(4)]
    bsto = [[Buf() for d in range(2)] for h in range(4)]
    bqgd, bod, bstin, bstout = Buf(), Buf(), Buf(), Buf()
    identb = k.identb
    gba = "gba"

    pieces = []
    for h in range(4):
        pieces += [("q", h, h * 256), ("k", h, 1024 + h * 256), ("v0", h, 2048 + h * 512), ("v1", h, 2048 + h * 512 + 256)]
    pst = {"n": 0}

    def load_piece():
        i = pst["n"]
        if i >= len(pieces):
            return
        pst["n"] += 1
        s = i % 2
        c0 = pieces[i][2]
        P.dma("pool", WS[s][:, :, :], win[:, :, c0:c0 + 256], dWS[s], wr=[bWS[s]])

    load_piece()
    load_piece()
    wa1v = k.gla_wa1[0].rearrange("d (kc p) r -> p d kc r", p=128)
    for d in range(2):
        for q in range(4):
            P.dma("pool", WA1[:, d, q * 4:(q + 1) * 4, :], wa1v[:, d, q * 4:(q + 1) * 4, :], getds(k, "gwa"), wr=[bWA1])
    norm_modulate(k, lambda kc: k.modA[:, L, 1, kc:kc + 1],
                  lambda kc: k.mods[:, L, 48 + kc:48 + kc + 1],
                  lambda kc, half: hT[:, kc, half * 512:(half + 1) * 512], 102400, bout=bh)
    P.op("dve", lambda e: e.memset(SMT[:, 6, :], 1.0), wr=[bSM])
    pc = {"n": 0}

    def nextbank():
        b = pc["n"] % 8
        pc["n"] += 1
        return k.pb[b], k.bpb[b]

    for d in range(2):
        for half in range(2):
            hs = slice(half * 512, (half + 1) * 512)
            ps, bps = nextbank()
            for kc in range(KC):
                P.op("pe", (lambda e, ps=ps, d=d, kc=kc, hs=hs: e.matmul(
                    ps[0:16, :], lhsT=WA1[:, d, kc, :], rhs=hT[:, kc, hs], start=(kc == 0), stop=(kc == KC - 1))),
                    rd=[bWA1, bh[half]] if kc in (0, KC - 1) else [], wr=[bps] if kc == KC - 1 else [],
                    waits=([bps.w] + bps.r) if kc == 0 else [], sig=(kc == KC - 1))
            P.op("act", (lambda e, ps=ps, d=d, hs=hs: e.copy(out=Z1[0:16, d, hs], in_=ps[0:16, :])),
                 rd=[bps], wr=[bZ1])

    def proj_fm(dst, bdst, s, scale):
        for dkc in range(2):
            for half in range(2):
                hs = slice(half * 512, (half + 1) * 512)
                ps, bps = nextbank()
                for kc in range(KC):
                    P.op("pe", (lambda e, ps=ps, s=s, kc=kc, dkc=dkc, hs=hs: e.matmul(
                        ps[:, :], lhsT=WS[s][:, kc, dkc * 128:(dkc + 1) * 128], rhs=hT[:, kc, hs],
                        start=(kc == 0), stop=(kc == KC - 1))),
                        rd=[bWS[s], bh[half]] if kc in (0, KC - 1) else [], wr=[bps] if kc == KC - 1 else [],
                        waits=([bps.w] + bps.r) if kc == 0 else [], sig=(kc == KC - 1))
                P.op("act", (lambda e, ps=ps, dkc=dkc, hs=hs: e.mul(out=dst[:, dkc, hs], in_=ps[:, :], mul=scale)),
                     rd=[bps], wr=[bdst])

    if GLA_STOP == 1:
        return
    for h in range(4):
        s = (h * 4) % 2
        proj_fm(Qb, bQ, (h * 4) % 2, 1.0 / 16.0)
        load_piece()
        proj_fm(Kb, bK, (h * 4 + 1) % 2, 1.0)
        load_piece()
        for vp in range(2):
            s = (h * 4 + 2 + vp) % 2
            for c in range(16):
                ps, bps = nextbank()
                for kc in range(KC):
                    P.op("pe", (lambda e, ps=ps, s=s, kc=kc, c=c: e.matmul(
                        ps[0:64, 0:256], lhsT=hT[:, kc, c * 64:(c + 1) * 64], rhs=WS[s][:, kc, :],
                        start=(kc == 0), stop=(kc == KC - 1))),
                        rd=[bWS[s], bh[c // 8]] if kc in (0, KC - 1) else [], wr=[bps] if kc == KC - 1 else [],
                        waits=([bps.w] + bps.r) if kc == 0 else [], sig=(kc == KC - 1))
                eng = "act" if c % 2 == 0 else "dve"
                if eng == "act":
                    P.op("act", (lambda e, ps=ps, c=c, vp=vp: e.copy(
                        out=Vt[0:64, c, vp * 256:(vp + 1) * 256], in_=ps[0:64, 0:256])),
                        rd=[bps], wr=[bV])
                else:
                    P.op("dve", (lambda e, ps=ps, c=c, vp=vp: e.tensor_copy(
                        out=Vt[0:64, c, vp * 256:(vp + 1) * 256], in_=ps[0:64, 0:256])), rd=[bps], wr=[bV])
            load_piece()
        P.dma("pool", WA2[0:16, :, :], k.gla_wa2[0].rearrange("d r n -> r d n")[:, :, h * 256:(h + 1) * 256],
              getds(k, "gwa"), wr=[bWA2])
        if GLA_STOP == 2:
            return
        first_o = True
        for d in range(2):
            P.op("dve", lambda e: e.memset(M01[:, :], 1.0), wr=[bM01])
            P.op("dve", lambda e: e.memset(M01.rearrange("p (c s) -> p c s", s=64)[:, :, 0:1], 0.0), wr=[bM01])
            for dkc in range(2):
                gcol = d * 8 + h * 2 + dkc
                for half in range(2):
                    hs = slice(half * 512, (half + 1) * 512)
                    ps, bps = nextbank()
                    P.op("pe", (lambda e, ps=ps, d=d, dkc=dkc, hs=hs: e.matmul(
                        ps[:, :], lhsT=WA2[0:16, d, dkc * 128:(dkc + 1) * 128], rhs=Z1[0:16, d, hs],
                        start=True, stop=True)),
                        rd=[bWA2, bZ1], wr=[bps], waits=([bps.w] + bps.r))
                    P.op("act", (lambda e, ps=ps, hs=hs, gcol=gcol: e.activation(
                        out=LG[:, hs], in_=ps[:, :], func=AF.Sigmoid, bias=sm(k, gba, gcol, 1), scale=1.0)),
                        rd=[bps, k.bsm], wr=[bLG])
                P.op("act", lambda e: e.activation(out=LG[:, :], in_=LG[:, :], func=AF.Ln), rd=[bLG], wr=[bLG])
                P.op("dve", lambda e: e.tensor_tensor_scan(out=Bc[:, :], data0=M01[:, :], data1=LG[:, :], initial=0.0,
                                                           op0=ALU.mult, op1=ALU.add), rd=[bM01, bLG], wr=[bB])
                B3 = Bc.rearrange("p (c s) -> p c s", s=64)
                totv = B3[:, :, 63]
                P.op("dve", lambda e, totv=totv: e.tensor_copy(out=TOT, in_=totv), rd=[bB], wr=[bSM])
                tot3 = TOT.unsqueeze(2).to_broadcast([128, 16, 64])
                T13 = T1.rearrange("p (c s) -> p c s", s=64)
                LG3 = LG.rearrange("p (c s) -> p c s", s=64)
                if d == 0:
                    P.op("act", lambda e: e.activation(out=T1[:, :], in_=Bc[:, :], func=AF.Exp, scale=1.0 / 16),
                         rd=[bB], wr=[bT1])
                    P.op("dve", (lambda e, dkc=dkc: e.tensor_tensor(out=QE[:, dkc, :], in0=Qb[:, dkc, :], in1=T1[:, :],
                                                                   op=ALU.mult)), rd=[bQ, bT1], wr=[bQE])
                    P.op("act", lambda e: e.activation(out=T1[:, :], in_=Bc[:, :], func=AF.Exp, scale=-1.0 / 16),
                         rd=[bB], wr=[bT1])
                    P.op("dve", (lambda e, dkc=dkc: e.tensor_tensor(out=KE[:, dkc, :], in0=Kb[:, dkc, :], in1=T1[:, :],
                                                                   op=ALU.mult)), rd=[bK, bT1], wr=[bKE])
                    P.op("dve", (lambda e, tot3=tot3, T13=T13, B3=B3: e.tensor_tensor(out=T13, in0=tot3, in1=B3,
                                                                                     op=ALU.subtract)),
                         rd=[bSM, bB], wr=[bT1])
                    P.op("act", lambda e: e.activation(out=T1[:, :], in_=T1[:, :], func=AF.Exp, scale=1.0 / 16),
                         rd=[bT1], wr=[bT1])
                    P.op("dve", (lambda e, dkc=dkc: e.tensor_tensor(out=KL[:, dkc, :], in0=Kb[:, dkc, :], in1=T1[:, :],
                                                                   op=ALU.mult)), rd=[bK, bT1], wr=[bKL])
                else:
                    P.op("dve", (lambda e, tot3=tot3, T13=T13, B3=B3: e.tensor_tensor(out=T13, in0=tot3, in1=B3,
                                                                                     op=ALU.subtract)),
                         rd=[bSM, bB], wr=[bT1])
                    P.op("dve", lambda e: e.tensor_tensor(out=T1[:, :], in0=T1[:, :], in1=LG[:, :], op=ALU.add),
                         rd=[bLG], wr=[bT1])
                    P.op("dve", lambda e: e.tensor_tensor(out=LG[:, :], in0=Bc[:, :], in1=LG[:, :], op=ALU.subtract),
                         rd=[bB], wr=[bLG])
                    P.op("act", lambda e: e.activation(out=Bc[:, :], in_=T1[:, :], func=AF.Exp, scale=1.0 / 16),
                         rd=[bT1], wr=[bB])
                    P.op("dve", (lambda e, dkc=dkc: e.tensor_tensor(out=QE[:, dkc, :], in0=Qb[:, dkc, :], in1=Bc[:, :],
                                                                   op=ALU.mult)), rd=[bQ, bB], wr=[bQE])
                    P.op("act", lambda e: e.activation(out=Bc[:, :], in_=T1[:, :], func=AF.Exp, scale=-1.0 / 16),
                         rd=[bT1], wr=[bB])
                    P.op("dve", (lambda e, dkc=dkc: e.tensor_tensor(out=KE[:, dkc, :], in0=Kb[:, dkc, :], in1=Bc[:, :],
                                                                   op=ALU.mult)), rd=[bK, bB], wr=[bKE])
                    P.op("act", lambda e: e.activation(out=Bc[:, :], in_=LG[:, :], func=AF.Exp, scale=1.0 / 16),
                         rd=[bLG], wr=[bB])
                    P.op("dve", (lambda e, dkc=dkc: e.tensor_tensor(out=KL[:, dkc, :], in0=Kb[:, dkc, :], in1=Bc[:, :],
                                                                   op=ALU.mult)), rd=[bK, bB], wr=[bKL])
                P.op("act", (lambda e, dkc=dkc: e.activation(out=DL[:, dkc, :], in_=TOT, func=AF.Exp, scale=1.0 / 16)),
                     rd=[bSM], wr=[bSM])
                P.op("dve", lambda e: e.tensor_tensor_scan(out=INC, data0=ONE16, data1=TOT, initial=0.0,
                                                           op0=ALU.mult, op1=ALU.add), rd=[bSM], wr=[bSM])
                if d == 0:
                    P.op("dve", lambda e: e.tensor_tensor(out=PX, in0=INC, in1=TOT, op=ALU.subtract), rd=[bSM], wr=[bSM])
                else:
                    P.op("dve", lambda e: e.tensor_scalar(out=PX, in0=INC, scalar1=SMT[:, 1, 15:16], scalar2=-1.0,
                                                          op0=ALU.subtract, op1=ALU.mult), rd=[bSM], wr=[bSM])
                P.op("act", lambda e: e.activation(out=EP, in_=PX, func=AF.Exp, scale=1.0 / 16), rd=[bSM], wr=[bSM])
                ep3 = EP.unsqueeze(2).to_broadcast([128, 16, 64])
                P.op("dve", (lambda e, dkc=dkc, ep3=ep3: e.tensor_tensor(
                    out=QG[:, dkc, :].rearrange("p (c s) -> p c s", s=64),
                    in0=QE[:, dkc, :].rearrange("p (c s) -> p c s", s=64), in1=ep3, op=ALU.mult)),
                    rd=[bQE, bSM], wr=[bQG])
            P.dma("sp", qg_d[h, d], QG.rearrange("p a t -> p (a t)"), getds(k, "gqg"), rd=[bQG], wr=[bqgd])
            if GLA_STOP == 3:
                return
            P.op("dve", lambda e: e.memset(S[:, :, :], 0.0), wr=[bS])
            P.op("dve", lambda e: e.memset(Sb[:, :, :], 0.0), wr=[bSb])
            order = list(range(16)) if d == 0 else list(range(15, -1, -1))
            mask = sm(k, "mask_f" if d == 0 else "mask_b")
            for ci, c in enumerate(order):
                cs = slice(c * 64, (c + 1) * 64)
                sl = ci % 2
                pss, bpss = k.pb[sl], k.bpb[sl]
                pso, bpso = k.pb[2 + sl], k.bpb[2 + sl]
                psu = [k.pb[4], k.pb[5]]
                bpsu = [k.bpb[4], k.bpb[5]]
                ptp, bptp = k.pb[6 + sl], k.bpb[6 + sl]
                tpv = ptp[0:64, 0:128].bitcast(BF16).rearrange("p (a b) -> p a b", a=2)
                for dkc in range(2):
                    P.op("pe", (lambda e, tpv=tpv, dkc=dkc, cs=cs: e.transpose(tpv[:, dkc, :], KL[:, dkc, cs], identb[:, :])),
                         rd=[bKL, k.bconst], wr=[bptp] if dkc == 1 else [],
                         waits=([bptp.w] + bptp.r) if dkc == 0 else [], sig=(dkc == 1))
                P.op("act", (lambda e, tpv=tpv, sl=sl: e.copy(out=KLT[sl][0:64, :, :], in_=tpv)),
                     rd=[bptp], wr=[bKLT[sl]])
                for dkc in range(2):
                    P.op("pe", (lambda e, dkc=dkc, sl=sl, c=c: e.matmul(
                        psu[dkc][:, :], lhsT=KLT[sl][0:64, dkc, :], rhs=Vt[0:64, c, :], start=True, stop=True)),
                        rd=[bKLT[sl], bV], wr=[bpsu[dkc]], waits=([bpsu[dkc].w] + bpsu[dkc].r))
                for dkc in range(2):
                    P.op("pe", (lambda e, pss=pss, dkc=dkc, cs=cs: e.matmul(
                        pss[0:64, 0:64], lhsT=KE[:, dkc, cs], rhs=QE[:, dkc, cs], start=(dkc == 0), stop=(dkc == 1))),
                        rd=[bKE, bQE], wr=[bpss] if dkc == 1 else [],
                        waits=([bpss.w] + bpss.r) if dkc == 0 else [], sig=(dkc == 1))
                P.op("dve", (lambda e, pss=pss, sl=sl, mask=mask: e.tensor_tensor(
                    out=ST[sl][0:64, :], in0=pss[0:64, 0:64], in1=mask[0:64, :], op=ALU.mult)),
                    rd=[bpss, k.bsm], wr=[bST[sl]])
                pso3 = pso[:, 0:256].rearrange("p (a b) -> p a b", a=4)
                for ec in range(4):
                    es = slice(ec * 128, (ec + 1) * 128)
                    P.op("pe", (lambda e, pso3=pso3, ec=ec, es=es, cs=cs: e.matmul(
                        pso3[:, ec, :], lhsT=Sb[:, 0, es], rhs=QE[:, 0, cs], start=True, stop=False)),
                        rd=[bSb, bQE] if ec == 0 else [], waits=([bpso.w] + bpso.r) if ec == 0 else [], sig=False)
                    P.op("pe", (lambda e, pso3=pso3, ec=ec, es=es, cs=cs: e.matmul(
                        pso3[:, ec, :], lhsT=Sb[:, 1, es], rhs=QE[:, 1, cs], start=False, stop=False)), sig=False)
                    P.op("pe", (lambda e, pso3=pso3, ec=ec, es=es, c=c, sl=sl: e.matmul(
                        pso3[:, ec, :], lhsT=Vt[0:64, c, es], rhs=ST[sl][0:64, :], start=False, stop=True)),
                        rd=[bST[sl], bV, bSb, bQE] if ec in (0, 3) else [], wr=[bpso] if ec == 3 else [], sig=(ec == 3))
                if d == 0:
                    P.op("act", (lambda e, pso3=pso3, cs=cs: e.copy(out=OH[:, :, cs], in_=pso3)),
                         rd=[bpso], wr=[bOH])
                else:
                    P.op("dve", (lambda e, pso3=pso3, cs=cs: e.tensor_tensor(out=OH[:, :, cs], in0=OH[:, :, cs], in1=pso3,
                                                                            op=ALU.add)), rd=[bpso], wr=[bOH])
                for dkc in range(2):
                    P.op("dve", (lambda e, dkc=dkc, c=c: e.scalar_tensor_tensor(
                        out=S[:, dkc, :], in0=S[:, dkc, :], scalar=DL[:, dkc, c:c + 1], in1=psu[dkc][:, :],
                        op0=ALU.mult, op1=ALU.add)), rd=[bpsu[dkc], bSM], wr=[bS])
                P.op("act", lambda e: e.copy(out=Sb[:, :, :], in_=S[:, :, :]), rd=[bS], wr=[bSb])
            P.dma("sp", st_in[h][d], S.rearrange("p a b -> p (a b)"), getds(k, "gst"), rd=[bS], wr=[bsti[h][d]])
            P.op("pool", (lambda e, h=h, d=d: e.collective_compute(
                "AllGather", ALU.bypass, replica_groups=PAIRS, ins=[st_in[h][d].opt()], outs=[st_out[h][d].opt()])),
                rd=[bsti[h][d]], wr=[bsto[h][d]])
            if GLA_STOP == 4:
                return
        P.dma("sp", o_d[h], OH.rearrange("p a t -> p (a t)"), getds(k, "god"), rd=[bOH], wr=[bod])

    if GLA_STOP == 5:
        return
    P.barrier()
    if GLA_STOP == 6:
        return
    ON = carve(k, 65536, [128, KC, T], BF16)
    QGf = carve(k, 98304, [128, 2, T], BF16)
    QGb = carve(k, 102400, [128, 2, T], BF16)
    SST = [carve(k, 106496 + i * 4096, [128, 2, 512], F32) for i in range(2)]
    SIN = [carve(k, 114688 + i * 2048, [128, 2, 512], BF16) for i in range(2)]
    RS = carve(k, 118784, [128, T], F32)
    SQ = [carve(k, 122880 + i * 2048, [128, 512], F32) for i in range(2)]
    SG = [carve(k, 126976 + i * 2048, [128, 512], F32) for i in range(2)]
    TT = SQ[0]
    bON, bQGf, bQGb, bRS, bTT = Buf("ON"), Buf(), Buf(), Buf(), Buf()
    bSST, bSIN, bSQ, bSG = [Buf(), Buf()], [Buf(), Buf()], [Buf(), Buf()], [Buf(), Buf()]
    bTT = bSQ[0]
    gp = {"n": 0}
    gpieces = [(h, gpi) for h in range(4) for gpi in range(2)]

    def load_gpiece():
        i = gp["n"]
        if i >= len(gpieces):
            return
        gp["n"] += 1
        h, gpi = gpieces[i]
        c0 = 4096 + h * 512 + gpi * 256
        P.dma("pool", WS[i % 2][:, :, :], win[:, :, c0:c0 + 256], dWS[i % 2], wr=[bWS[i % 2]])

    load_gpiece()
    load_gpiece()
    for h in range(4):
        P.dma("sp", OH.rearrange("p a t -> p (a t)"), o_d[h], getds(k, "god2"), rd=[bod], wr=[bOH])
        P.dma("sp", QGf.rearrange("p a t -> p (a t)"), qg_d[h, 0], getds(k, "gqg2"), rd=[bqgd], wr=[bQGf])
        P.dma("sp", QGb.rearrange("p a t -> p (a t)"), qg_d[h, 1], getds(k, "gqg3"), rd=[bqgd], wr=[bQGb])
        for d in range(2):
            P.dma("sp", SST[d].rearrange("p a b -> p (a b)"), st_out[h][d][d * 128:(d + 1) * 128, :], getds(k, f"gsi{d}"),
                  rd=[bsto[h][d]], wr=[bSST[d]])
            P.op("dve", (lambda e, d=d: e.tensor_scalar(out=SIN[d][:, :, :], in0=SST[d][:, :, :],
                                                        scalar1=sm(k, "is_odd" if d == 0 else "is_even"), scalar2=None,
                                                        op0=ALU.mult)), rd=[bSST[d], k.bsm], wr=[bSIN[d]])
        for ec in range(4):
            es = slice(ec * 128, (ec + 1) * 128)
            for half in range(2):
                hs = slice(half * 512, (half + 1) * 512)
                ps, bps = nextbank()
                n = 0
                for d, QGx, bQGx in ((0, QGf, bQGf), (1, QGb, bQGb)):
                    for dkc in range(2):
                        P.op("pe", (lambda e, ps=ps, d=d, dkc=dkc, es=es, hs=hs, QGx=QGx, n=n: e.matmul(
                            ps[:, :], lhsT=SIN[d][:, dkc, es], rhs=QGx[:, dkc, hs], start=(n == 0), stop=(n == 3))),
                            rd=[bSIN[d], bQGx], wr=[bps] if n == 3 else [],
                            waits=([bps.w] + bps.r) if n == 0 else [], sig=(n == 3))
                        n += 1
                P.op("dve", (lambda e, ps=ps, ec=ec, hs=hs: e.tensor_tensor(out=OH[:, ec, hs], in0=OH[:, ec, hs],
                                                                           in1=ps[:, :], op=ALU.add)),
                     rd=[bps], wr=[bOH])
        for half in range(2):
            hs = slice(half * 512, (half + 1) * 512)
            ps, bps = nextbank()
            for ec in range(4):
                i = ec % 2
                P.op("dve", (lambda e, ec=ec, i=i, hs=hs: e.tensor_tensor(out=SQ[i][:, :], in0=OH[:, ec, hs],
                                                                         in1=OH[:, ec, hs], op=ALU.mult)),
                     rd=[bOH], wr=[bSQ[i]])
                P.op("pe", (lambda e, ps=ps, ec=ec, i=i: e.matmul(ps[:, :], lhsT=k.ones[:, :], rhs=SQ[i][:, :],
                                                                  start=(ec == 0), stop=(ec == 3))),
                     rd=[bSQ[i], k.bconst], wr=[bps] if ec == 3 else [],
                     waits=([bps.w] + bps.r) if ec == 0 else [], sig=True)
            P.op("act", (lambda e, ps=ps, hs=hs: e.activation(out=RS[:, hs], in_=ps[:, :], func=AF.Sqrt,
                                                              bias=k.epsc[:, 0:1], scale=1.0 / 512)),
                 rd=[bps, k.bconst], wr=[bRS])
            P.op("dve", (lambda e, hs=hs: e.reciprocal(out=RS[:, hs], in_=RS[:, hs])), rd=[bRS], wr=[bRS])
        for gpi in range(2):
            i = h * 2 + gpi
            s = i % 2
            for f in range(2):
                ec = gpi * 2 + f
                for half in range(2):
                    hs = slice(half * 512, (half + 1) * 512)
                    ps, bps = nextbank()
                    for kc in range(KC):
                        P.op("pe", (lambda e, ps=ps, s=s, kc=kc, f=f, hs=hs: e.matmul(
                            ps[:, :], lhsT=WS[s][:, kc, f * 128:(f + 1) * 128], rhs=hT[:, kc, hs],
                            start=(kc == 0), stop=(kc == KC - 1))),
                            rd=[bWS[s], bh[half]] if kc in (0, KC - 1) else [], wr=[bps] if kc == KC - 1 else [],
                            waits=([bps.w] + bps.r) if kc == 0 else [], sig=(kc == KC - 1))
                    gi2 = (ec * 2 + half) % 2
                    P.op("act", (lambda e, ps=ps, gi2=gi2: e.activation(out=SG[gi2][:, :], in_=ps[:, :], func=AF.Silu)),
                         rd=[bps], wr=[bSG[gi2]])
                    P.op("dve", (lambda e, ec=ec, hs=hs: e.tensor_tensor(out=TT[:, :], in0=OH[:, ec, hs], in1=RS[:, hs],
                                                                        op=ALU.mult)), rd=[bOH, bRS], wr=[bTT])
                    P.op("dve", (lambda e, ec=ec, hs=hs, gi2=gi2, h=h: e.scalar_tensor_tensor(
                        out=ON[:, h * 4 + ec, hs], in0=TT[:, :], scalar=sm(k, "gng", ec, 1), in1=SG[gi2][:, :],
                        op0=ALU.mult, op1=ALU.mult)), rd=[bTT, bSG[gi2], k.bsm], wr=[bON])
            load_gpiece()
    P.barrier()
    if GLA_STOP == 7:
        return
    WO = [carve(k, 49152 + s * 8192, [128, KC, 256], BF16) for s in range(2)]
    bWO = [Buf(), Buf()]
    dWO = [getds(k, "wo0"), getds(k, "wo1")]

    def load_wo(g):
        P.dma("pool", WO[g % 2][:, :, :], wout[:, :, g * 256:(g + 1) * 256], dWO[g % 2], wr=[bWO[g % 2]])

    load_wo(0)
    load_wo(1)
    ou = 0
    for g in range(8):
        s = g % 2
        for fi in range(2):
            dc = g * 2 + fi
            for half in range(2):
                hs = slice(half * 512, (half + 1) * 512)
                b = 6 + ou % 2
                ou += 1
                yps, byps = k.pb[b], k.bpb[b]
                for kc in range(KC):
                    P.op("pe", (lambda e, yps=yps, s=s, kc=kc, fi=fi, hs=hs: e.matmul(
                        yps[:, :], lhsT=WO[s][:, kc, fi * 128:(fi + 1) * 128], rhs=ON[:, kc, hs],
                        start=(kc == 0), stop=(kc == KC - 1))),
                        rd=[bWO[s], bON] if kc in (0, KC - 1) else [], wr=[byps] if kc == KC - 1 else [],
                        waits=([byps.w] + byps.r) if kc == 0 else [], sig=(kc == KC - 1))
                P.op("dve", (lambda e, yps=yps, dc=dc, hs=hs: e.scalar_tensor_tensor(
                    out=k.xT[:, dc, hs], in0=yps[:, :], scalar=k.modG[:, L, 1, dc:dc + 1], in1=k.xT[:, dc, hs],
                    op0=ALU.mult, op1=ALU.add)), rd=[byps, k.bmods], wr=[k.bx])
        if g + 2 < 8:
            load_wo(g + 2)


def pool_stage(k, L):
    P, nc = k.P, k.nc
    P.barrier()
    HW = 1040
    H = carve(k, 0, [128, KC, HW], F32)
    B0 = 66560
    A_ = [carve(k, B0 + i * 8448, [128, 4, 528], F32) for i in range(2)]
    B1 = B0 + 2 * 8448
    DBF = [carve(k, B1 + i * 4096, [128, 4, 512], BF16) for i in range(2)]
    B2 = B1 + 8192
    PW = carve(k, B2, [128, 4, 4, 512], BF16)
    B3 = B2 + 16384
    V = carve(k, B3, [128, HW], F32)
    VA = [carve(k, B3 + 4160 + i * 2112, [128, 528], F32) for i in range(2)]
    INV = carve(k, B3 + 4160 + 4224, [128, 512], F32)
    SBh = carve(k, B3 + 10432, [128, KC, 16], F32)
    RBh = carve(k, B3 + 11456, [128, 2, KC, 16], F32)
    TMP = [carve(k, B3 + 13504 + i * 2048, [128, 512], F32) for i in range(2)]
    NB = B0
    bH, bA, bD, bPW, bV, bVA, bINV, bT = Buf("H"), [Buf(), Buf()], [Buf(), Buf()], Buf(), Buf(), [Buf(), Buf()], Buf(), [Buf(), Buf()]
    P.dma("pool", PW.rearrange("p g i o -> p (g i) o"),
          k.pool_w[0].rearrange("g (i p) o -> p (g i) o", p=128), getds(k, "pw"), wr=[bPW])
    norm_modulate(k, lambda kc: k.modA[:, L, 1, kc:kc + 1],
                  lambda kc: k.mods[:, L, 48 + kc:48 + kc + 1],
                  lambda kc, half: H[:, kc, 8 + half * 512:8 + (half + 1) * 512], NB, bout=bH)
    cc_in = nc.dram_tensor("plh_in", [128, 256], F32)
    cc_out = nc.dram_tensor("plh_out", [256, 256], F32)
    bSB, bRB, bcin, bcout = Buf(), Buf(), Buf(), Buf()
    P.op("dve", lambda e: e.tensor_copy(out=SBh[:, :, 0:8], in_=H[:, :, 8:16]), rd=[bH], wr=[bSB])
    P.op("dve", lambda e: e.tensor_copy(out=SBh[:, :, 8:16], in_=H[:, :, 1024:1032]), rd=[bH], wr=[bSB])
    P.dma("sp", cc_in.ap(), SBh.rearrange("p a b -> p (a b)"), getds(k, "halo"), rd=[bSB], wr=[bcin])
    P.op("pool", lambda e: e.collective_compute("AllGather", ALU.bypass, replica_groups=PAIRS,
                                                ins=[cc_in.ap().opt()], outs=[cc_out.ap().opt()]),
         rd=[bcin], wr=[bcout])
    P.dma("sp", RBh.rearrange("p r a b -> p r (a b)"), cc_out.ap().rearrange("(r p) n -> p r n", p=128),
          getds(k, "halo"), rd=[bcout], wr=[bRB])
    P.op("dve", lambda e: e.tensor_scalar(out=H[:, :, 0:8], in0=RBh[:, 0, :, 8:16], scalar1=sm(k, "is_odd"),
                                          scalar2=None, op0=ALU.mult), rd=[bRB, k.bsm], wr=[bH])
    P.op("dve", lambda e: e.tensor_scalar(out=H[:, :, 1032:1040], in0=RBh[:, 1, :, 0:8], scalar1=sm(k, "is_even"),
                                          scalar2=None, op0=ALU.mult), rd=[bRB, k.bsm], wr=[bH])
    P.op("dve", lambda e: e.memset(V[:, :], 1.0), wr=[bV])
    P.op("dve", lambda e: e.tensor_copy(out=V[:, 0:8], in_=sm(k, "is_odd").to_broadcast([128, 8])), rd=[k.bsm], wr=[bV])
    P.op("dve", lambda e: e.tensor_copy(out=V[:, 1032:1040], in_=sm(k, "is_even").to_broadcast([128, 8])),
         rd=[k.bsm], wr=[bV])
    u = 0
    for gi in range(4):
        w = 2 << gi
        cs = slice(gi * 4, gi * 4 + 4)
        for half in range(2):
            b0 = half * 512 + 8 - w // 2
            n = 512 + w - 1
            src, bsrc = H[:, cs, b0:b0 + n], bH
            vsrc, bvsrc = V[:, b0:b0 + n], bV
            st = 1
            lvl = 0
            while st < w:
                dst, bdst = A_[lvl % 2], bA[lvl % 2]
                vdst, bvdst = VA[lvl % 2], bVA[lvl % 2]
                nn = n - st
                P.op("dve", (lambda e, dst=dst, src=src, nn=nn, st=st: e.tensor_tensor(
                    out=dst[:, :, 0:nn], in0=src[:, :, 0:nn], in1=src[:, :, st:st + nn], op=ALU.add)),
                    rd=[bsrc], wr=[bdst])
                P.op("dve", (lambda e, vdst=vdst, vsrc=vsrc, nn=nn, st=st: e.tensor_tensor(
                    out=vdst[:, 0:nn], in0=vsrc[:, 0:nn], in1=vsrc[:, st:st + nn], op=ALU.add)),
                    rd=[bvsrc], wr=[bvdst])
                src, bsrc = dst, bdst
                vsrc, bvsrc = vdst, bvdst
                n = nn
                st *= 2
                lvl += 1
            assert n == 512
            P.op("dve", (lambda e, vsrc=vsrc: e.reciprocal(out=INV[:, :], in_=vsrc[:, 0:512])), rd=[bvsrc], wr=[bINV])
            inv3 = INV[:, :].unsqueeze(1).to_broadcast([128, 4, 512])
            P.op("dve", (lambda e, src=src, inv3=inv3: e.tensor_tensor(out=src[:, :, 0:512], in0=src[:, :, 0:512],
                                                                      in1=inv3, op=ALU.mult)),
                 rd=[bINV], wr=[bsrc])
            di = u % 2
            u += 1
            P.op("dve", (lambda e, src=src, di=di, cs=cs, half=half: e.tensor_tensor(
                out=DBF[di][:, :, :], in0=src[:, :, 0:512], in1=H[:, cs, 8 + half * 512:8 + (half + 1) * 512],
                op=ALU.subtract)), rd=[bsrc, bH], wr=[bD[di]])
            hs = slice(half * 512, (half + 1) * 512)
            for oc in range(4):
                ch = gi * 4 + oc
                b = 6 + (u * 4 + oc) % 2
                yps, byps = k.pb[b], k.bpb[b]
                for ic in range(4):
                    P.op("pe", (lambda e, yps=yps, gi=gi, ic=ic, oc=oc, di=di: e.matmul(
                        yps[:, :], lhsT=PW[:, gi, ic, oc * 128:(oc + 1) * 128], rhs=DBF[di][:, ic, :],
                        start=(ic == 0), stop=(ic == 3))),
                        rd=[bPW, bD[di]] if ic in (0, 3) else [],
                        wr=[byps] if ic == 3 else [],
                        waits=([byps.w] + byps.r) if ic == 0 else [], sig=(ic == 3))
                ti = oc % 2
                P.op("dve", (lambda e, yps=yps, ch=ch, ti=ti: e.tensor_scalar(
                    out=TMP[ti][:, :], in0=yps[:, :], scalar1=sm(k, "pb", ch, 1), scalar2=sm(k, "pscale", ch, 1),
                    op0=ALU.add, op1=ALU.mult)), rd=[byps, k.bsm], wr=[bT[ti]])
                P.op("dve", (lambda e, ch=ch, ti=ti, hs=hs: e.scalar_tensor_tensor(
                    out=k.xT[:, ch, hs], in0=TMP[ti][:, :], scalar=k.modG[:, L, 1, ch:ch + 1], in1=k.xT[:, ch, hs],
                    op0=ALU.mult, op1=ALU.add)), rd=[bT[ti], k.bmods], wr=[k.bx])


def epilogue(k, last):
    P = k.P
    P.barrier()
    ov = k.out.rearrange("(kc p) t -> p kc t", p=128)
    if last:
        OUT = carve(k, 0, [128, KC, T], F32)
        bo = Buf("out")
        norm_modulate(k, lambda kc: sm(k, "final_g", kc, 1), None,
                      lambda kc, half: OUT[:, kc, half * 512:(half + 1) * 512], 65536, bout=bo)
        src, bsrc = OUT, bo
    else:
        src, bsrc = k.xT, k.bx
    toks = []
    for q in range(4):
        toks.append(P.dma("sp", ov[:, q * 4:(q + 1) * 4, :], src[:, q * 4:(q + 1) * 4, :], getds(k, "xout"), rd=[bsrc]))
    P.wait_only("sp", toks[-1:])


_CACHE = {}
LAUNCH_GROUPS = [[0, 1, 2, 3]]


def weights_for(inp, grp, mixers):
    w = {}
    g = list(grp)
    for n in ("ada_w", "ffn_w_gate", "ffn_w_up", "ffn_w_down"):
        w[n] = np.ascontiguousarray(np.asarray(inp[n], np.float32)[g])
    convL = [L for L in g if L % 3 == 0 and mixers]
    if convL:
        cj = [L // 3 for L in convL]
        w["conv_w_in"] = np.ascontiguousarray(np.asarray(inp["conv_w_in"], np.float32)[cj])
        w["conv_w_out"] = np.ascontiguousarray(np.asarray(inp["conv_w_out"], np.float32)[cj])
    if 1 in g and mixers:
        for n in ("gla_w_in", "gla_wa1", "gla_wa2", "gla_w_out"):
            w[n] = np.ascontiguousarray(np.asarray(inp[n], np.float32))
    if 2 in g and mixers:
        w["pool_w"] = np.ascontiguousarray(np.asarray(inp["pool_w"], np.float32))
    return w


def run_groups(inp, groups, mixers=True, final=True):
    x = np.asarray(inp["x"], np.float32)
    smalls = [pack_smalls(inp, c) for c in range(NCORES)]
    xT = [np.ascontiguousarray(x[c // 2, (c % 2) * T:(c % 2 + 1) * T, :].T) for c in range(NCORES)]
    for gi, grp in enumerate(groups):
        lastg = final and gi == len(groups) - 1
        key = (tuple(grp), lastg, mixers)
        if key not in _CACHE:
            _CACHE[key] = build_program(list(grp), gi == 0, lastg, mixers)
        nc = _CACHE[key]
        w = weights_for(inp, grp, mixers)
        in_maps = []
        if ADA_SPLIT:
            ada_half = [np.ascontiguousarray(w["ada_w"][:, :, r * 9216:(r + 1) * 9216]) for r in range(2)]
        for c in range(NCORES):
            m = {"xT_in": xT[c], "smalls": smalls[c]}
            m.update(w)
            if ADA_SPLIT:
                m["ada_w"] = ada_half[c % 2]
            in_maps.append(m)
        res = run_bass_kernel_spmd(nc, in_maps, core_ids=list(range(NCORES)))
        xT = [np.ascontiguousarray(res.results[c]["outT"]) for c in range(NCORES)]
    out = np.empty((4, SEQ, D), np.float32)
    for c in range(NCORES):
        out[c // 2, (c % 2) * T:(c % 2 + 1) * T, :] = xT[c].T
    return out


def kernel(**inputs):
    inp = {n: np.asarray(v) for n, v in inputs.items()}
    return run_groups(inp, LAUNCH_GROUPS, True, True)
```

```python
import contextlib
import numpy as np
import concourse.bass as bass
import concourse.mybir as mybir
from concourse.bass_utils import run_bass_kernel_spmd

F32 = mybir.dt.float32
BF16 = mybir.dt.bfloat16
AF = mybir.ActivationFunctionType
ALU = mybir.AluOpType
AX = mybir.AxisListType

NCORES = 8
D = 2048
KC = 16
T = 1024
SEQ = 2048
DEPTH = 4
FF = 5504
FC = 43
EPS = 1e-6
CONVW = 31
SEM_LIMIT = 12000
ADA_SPLIT = True
FFN_ON = True
GLA_STOP = 0
PAIRS = [[0, 1], [2, 3], [4, 5], [6, 7]]


class Buf:
    __slots__ = ("w", "r", "name")

    def __init__(self, name=""):
        self.w = None
        self.r = []
        self.name = name


class DmaSem:
    def __init__(self, name):
        self.name = name
        self.sem = None
        self.n = 0
        self.k = 0

    def advance(self, P):
        if self.sem is None or self.n >= SEM_LIMIT * 4:
            self.sem = P._alloc_sem(f"d_{self.name}_{self.k}")
            self.k += 1
            self.n = 0
        self.n += 16


class Prog:
    ENGS = ("sp", "act", "pe", "dve", "pool")

    def __init__(self, nc):
        self.nc = nc
        self.streams = {e: [] for e in self.ENGS}
        self.cur_sem = {}
        self.cnt = {}
        self.seen = {e: {} for e in self.ENGS}
        self.nsem = 0
        self._sem_cms = []
        self.last_tok = {}
        self.dma_last = {}
        for e in self.ENGS:
            self._new_eng_sem(e)

    def _alloc_sem(self, name):
        cm = self.nc.semaphore(name)
        s = cm.__enter__()
        self._sem_cms.append(cm)
        self.nsem += 1
        return s

    def _new_eng_sem(self, e):
        self.cur_sem[e] = self._alloc_sem(f"s_{e}_{self.nsem}")
        self.cnt[e] = 0

    def close(self):
        for cm in reversed(self._sem_cms):
            cm.__exit__(None, None, None)

    def _waits(self, eng, toks):
        out = []
        seen = self.seen[eng]
        for t in toks:
            if t is None:
                continue
            sem, val, src = t
            if src == "pe" and eng == "pe":
                continue
            key = id(sem)
            if seen.get(key, (None, 0))[1] >= val:
                continue
            seen[key] = (sem, val)
            out.append((sem, val))
        return out

    def _deps(self, rd, wr, waits):
        toks = list(waits)
        for b in rd:
            toks.append(b.w)
        for b in wr:
            toks.append(b.w)
            toks.extend(b.r)
        return toks

    def _commit(self, tok, rd, wr):
        for b in rd:
            b.r.append(tok)
        for b in wr:
            b.w = tok
            b.r = []

    def op(self, eng, fn, rd=(), wr=(), waits=(), sig=True):
        ws = self._waits(eng, self._deps(rd, wr, waits))
        tok = None
        if sig:
            if self.cnt[eng] >= SEM_LIMIT:
                self._new_eng_sem(eng)
            self.cnt[eng] += 1
            tok = (self.cur_sem[eng], self.cnt[eng], eng)
            self.last_tok[eng] = tok
        self.streams[eng].append((ws, fn, (tok[0], 1) if tok else None))
        if tok is not None:
            self._commit(tok, rd, wr)
        return tok

    def dma(self, eng, out, in_, dsem, rd=(), wr=(), waits=(), **kw):
        ws = self._waits(eng, self._deps(rd, wr, waits))
        dsem.advance(self)
        tok = (dsem.sem, dsem.n, "dma")
        self.dma_last[id(dsem.sem)] = tok
        self.streams[eng].append((ws, (lambda e: e.dma_start(out=out, in_=in_, **kw)), (dsem.sem, 16)))
        self._commit(tok, rd, wr)
        return tok

    def wait_only(self, eng, toks):
        ws = self._waits(eng, toks)
        if ws:
            self.streams[eng].append((ws, None, None))

    def barrier(self):
        toks = list(self.last_tok.values()) + list(self.dma_last.values())
        for e in self.ENGS:
            ws = self._waits(e, [t for t in toks if not (t[2] == e and e != "pe")] +
                             [t for t in toks if t[2] == e])
            if ws:
                self.streams[e].append((ws, None, None))

    def run(self, eng, h):
        for ws, fn, inc in self.streams[eng]:
            for sem, val in ws:
                h.wait_ge(sem, val)
            if fn is not None:
                ins = fn(h)
                if inc is not None:
                    ins.then_inc(inc[0], inc[1])


def emit_block(nc, P):
    with nc.Block() as block:
        @block.sync
        def _(e):
            P.run("sp", e)

        @block.scalar
        def _(e):
            P.run("act", e)

        @block.tensor
        def _(e):
            P.run("pe", e)

        @block.vector
        def _(e):
            P.run("dve", e)

        @block.gpsimd
        def _(e):
            P.run("pool", e)


def smalls_layout():
    off = {}
    n = 0

    def add(name, cols):
        nonlocal n
        off[name] = (n, cols)
        n += cols

    add("c", 16)
    for L in range(DEPTH):
        for s in range(3):
            add(f"ng{L}_{s}", 16)
        add(f"adab{L}", 144)
    add("final_g", 16)
    for j in range(2):
        add(f"cbin{j}", 32)
        add(f"cwdw{j}", 16 * CONVW)
        add(f"cbdw{j}", 16)
        add(f"clng{j}", 16)
        add(f"clnb{j}", 16)
        add(f"cbout{j}", 16)
    add("gba", 16)
    add("gng", 4)
    add("pb", 16)
    add("pscale", 16)
    add("ident", 128)
    add("mask_f", 64)
    add("mask_b", 64)
    add("is_odd", 1)
    add("is_even", 1)
    return off, n


def col(v):
    v = np.asarray(v, np.float32).reshape(-1, 128)
    return np.ascontiguousarray(v.T)


def pack_smalls(inp, core):
    off, n = smalls_layout()
    b = core // 2
    S = np.zeros((128, n), np.float32)

    def put(name, arr):
        o, c = off[name]
        assert arr.shape == (128, c), (name, arr.shape, c)
        S[:, o:o + c] = arr

    put("c", col(inp["c"][b]))
    for L in range(DEPTH):
        for s in range(3):
            put(f"ng{L}_{s}", col(inp["norm_g"][L, s]))
        put(f"adab{L}", col(inp["ada_b"][L]))
    put("final_g", col(inp["final_g"]))
    for j in range(2):
        put(f"cbin{j}", col(inp["conv_b_in"][j]))
        w = np.asarray(inp["conv_w_dw"][j], np.float32)
        w = w.reshape(CONVW, 16, 128).transpose(2, 1, 0).reshape(128, 16 * CONVW)
        put(f"cwdw{j}", w)
        put(f"cbdw{j}", col(inp["conv_b_dw"][j]))
        put(f"clng{j}", col(inp["conv_ln_g"][j]))
        put(f"clnb{j}", col(inp["conv_ln_b"][j]))
        put(f"cbout{j}", col(inp["conv_b_out"][j]))
    put("gba", col(inp["gla_ba"][0].reshape(-1)))
    put("gng", col(inp["gla_norm_g"][0]))
    put("pb", col(inp["pool_b"][0].reshape(-1)))
    put("pscale", col(inp["pool_scale"][0]))
    put("ident", np.eye(128, dtype=np.float32))
    mf = np.zeros((128, 64), np.float32)
    mb = np.zeros((128, 64), np.float32)
    si = np.arange(64)[:, None]
    ti = np.arange(64)[None, :]
    mf[:64] = (si <= ti)
    mb[:64] = (si > ti)
    put("mask_f", mf)
    put("mask_b", mb)
    S[:, off["is_odd"][0]] = float(core % 2)
    S[:, off["is_even"][0]] = float(1 - core % 2)
    return S


class K:
    pass


def build_program(layers, first, last, mixers=True):
    nc = bass.Bass("TRN2", target_bir_lowering=False)
    soff, NS = smalls_layout()
    k = K()
    k.nc = nc
    k.soff = soff
    dt = nc.dram_tensor
    k.xin = dt("xT_in", [D, T], F32, kind="ExternalInput").ap()
    k.smalls_d = dt("smalls", [128, NS], F32, kind="ExternalInput").ap()
    nl = len(layers)
    k.li = {L: i for i, L in enumerate(layers)}
    k.ada_w = dt("ada_w", [nl, D, 9216 if ADA_SPLIT else 9 * D], F32, kind="ExternalInput").ap()
    k.wg = dt("ffn_w_gate", [nl, 2, D, FF], F32, kind="ExternalInput").ap()
    k.wu = dt("ffn_w_up", [nl, 2, D, FF], F32, kind="ExternalInput").ap()
    k.wd = dt("ffn_w_down", [nl, 2, FF, D], F32, kind="ExternalInput").ap()
    convL = [L for L in layers if L % 3 == 0 and mixers]
    k.ci = {L: i for i, L in enumerate(convL)}
    if convL:
        k.conv_w_in = dt("conv_w_in", [len(convL), D, 2 * D], F32, kind="ExternalInput").ap()
        k.conv_w_out = dt("conv_w_out", [len(convL), D, D], F32, kind="ExternalInput").ap()
    if 1 in layers and mixers:
        k.gla_w_in = dt("gla_w_in", [1, D, 6144], F32, kind="ExternalInput").ap()
        k.gla_wa1 = dt("gla_wa1", [1, 2, D, 16], F32, kind="ExternalInput").ap()
        k.gla_wa2 = dt("gla_wa2", [1, 2, 16, 1024], F32, kind="ExternalInput").ap()
        k.gla_w_out = dt("gla_w_out", [1, D, D], F32, kind="ExternalInput").ap()
    if 2 in layers and mixers:
        k.pool_w = dt("pool_w", [1, 4, 512, 512], F32, kind="ExternalInput").ap()
    k.out = dt("outT", [D, T], F32, kind="ExternalOutput").ap()

    P = Prog(nc)
    k.P = P
    with contextlib.ExitStack() as es:
        def sb(name, shape, dtype):
            return es.enter_context(nc.sbuf_tensor(name, shape, dtype))

        k.xT = sb("xT", [128, KC, T], F32)
        k.smalls = sb("smalls_sb", [128, NS], F32)
        k.mods = sb("mods", [128, DEPTH, 144], F32)
        k.modA = sb("modA", [128, DEPTH, 3, KC], F32)
        k.modG = sb("modG", [128, DEPTH, 3, KC], F32)
        k.ones = sb("ones", [128, 128], F32)
        k.epsc = sb("epsc", [128, 1], F32)
        k.zeroc = sb("zeroc", [128, 1], F32)
        k.cact = sb("cact", [128, KC], BF16)
        k.identb = sb("identb", [128, 128], BF16)
        k.ctmp = sb("ctmp", [128, KC], F32)
        ARENA_WORDS = (nc.sbuf_bytes_remaining - 128) // 4
        k.arena = sb("arena", [128, ARENA_WORDS], F32)
        k.arena_bytes = ARENA_WORDS * 4
        k.pb = [es.enter_context(nc.psum_tensor(f"pb{i}", [128, 512], F32)) for i in range(8)]
        k.bpb = [Buf(f"pb{i}") for i in range(8)]
        k.bx = Buf("x")
        k.bsm = Buf("smalls")
        k.bmods = Buf("mods")
        k.bconst = Buf("const")
        k.dsem = {}

        k.use_side = ADA_SPLIT and len(layers) > 1 and layers[0] % 3 == 0
        prologue(k, layers)
        side = SideAda(k, layers) if k.use_side else None
        for L in layers:
            if FFN_ON:
                ffn_stage(k, L, 0, side)
            if mixers:
                kind = L % 3
                if kind == 0:
                    conv_stage(k, L, L // 3, side)
                elif kind == 1:
                    gla_stage(k, L)
                else:
                    pool_stage(k, L)
            if FFN_ON:
                ffn_stage(k, L, 1, side)
            if side is not None:
                side.finish()
        epilogue(k, last)
        emit_block(nc, P)
    P.close()
    return nc


def carve(k, byte_off, shape, dtype):
    n = int(np.prod(shape[1:]))
    esz = 4 if dtype == F32 else 2
    nbytes = n * esz
    assert byte_off % 4 == 0 and nbytes % 4 == 0
    assert byte_off + nbytes <= k.arena_bytes, (byte_off, nbytes, k.arena_bytes)
    ap = k.arena[:, byte_off // 4:(byte_off + nbytes) // 4]
    if dtype != F32:
        ap = ap.bitcast(dtype)
    if len(shape) == 3:
        ap = ap.rearrange("p (a b) -> p a b", a=shape[1])
    elif len(shape) == 4:
        ap = ap.rearrange("p (a b c) -> p a b c", a=shape[1], b=shape[2])
    return ap


def sm(k, name, j0=0, n=None):
    o, c = k.soff[name]
    if n is None:
        n = c - j0
    return k.smalls[:, o + j0:o + j0 + n]


def getds(k, name):
    if name not in k.dsem:
        k.dsem[name] = DmaSem(name)
    return k.dsem[name]


def prologue(k, layers):
    P, nc = k.P, k.nc
    P.dma("sp", k.smalls[:, :], k.smalls_d[:, :], getds(k, "sm"), wr=[k.bsm])
    xv = k.xin.rearrange("(kc p) t -> p kc t", p=128)
    for q in range(4):
        P.dma("sp", k.xT[:, q * 4:(q + 1) * 4, :], xv[:, q * 4:(q + 1) * 4, :], getds(k, "xin"), wr=[k.bx])
    P.op("dve", lambda e: e.memset(k.ones[:, :], 1.0), wr=[k.bconst])
    P.op("dve", lambda e: e.memset(k.epsc[:, :], EPS), wr=[k.bconst])
    P.op("dve", lambda e: e.memset(k.zeroc[:, :], 0.0), wr=[k.bconst])
    P.op("dve", lambda e: e.tensor_copy(out=k.identb[:, :], in_=sm(k, "ident")), rd=[k.bsm], wr=[k.bconst])
    P.op("act", lambda e: e.activation(out=k.ctmp[:, :], in_=sm(k, "c"), func=AF.Silu), rd=[k.bsm], wr=[k.bconst])
    P.op("dve", lambda e: e.tensor_copy(out=k.cact[:, :], in_=k.ctmp[:, :]), rd=[k.bconst], wr=[k.bconst])
    if ADA_SPLIT:
        mods_split(k, layers[:1] if k.use_side else layers)
    else:
        mods_local(k, layers)


class SideAda:
    def __init__(self, k, layers):
        self.k = k
        self.layers = list(layers[1:])
        self.blocks = [(li, b) for li in range(1, len(layers)) for b in range(36)]
        self.nload = 0
        self.ncons = 0
        self.bw = [Buf(), Buf()]
        self.ds = [getds(k, "sda0"), getds(k, "sda1")]
        self.psm = k.pb[3]
        self.bpsm = k.bpb[3]
        self.started = False
        self.finished = not self.blocks

    def active(self):
        return self.ncons < len(self.blocks)

    def _consume(self, slots):
        k, P = self.k, self.k.P
        i = self.ncons
        self.ncons += 1
        li, b = self.blocks[i]
        s = i % 2
        for dc in range(2):
            c0 = (li - 1) * 72 + b * 2 + dc
            for kc in range(KC):
                lastk = kc == KC - 1
                first = not self.started
                self.started = True
                P.op("pe", (lambda e, s=s, dc=dc, kc=kc, c0=c0, sl=slots[s]: e.matmul(
                    self.psm[:, c0:c0 + 1], lhsT=sl[:, kc, dc * 128:(dc + 1) * 128],
                    rhs=k.cact[:, kc:kc + 1], start=(kc == 0), stop=(kc == KC - 1))),
                    rd=[self.bw[s], k.bconst] if kc == 0 or (lastk and dc == 1) else [],
                    wr=[self.bpsm] if (lastk and dc == 1) else [],
                    waits=(([self.bpsm.w] + self.bpsm.r) if first else []),
                    sig=(lastk and dc == 1))

    def _load(self, slots, waits):
        k, P = self.k, self.k.P
        i = self.nload
        self.nload += 1
        li, b = self.blocks[i]
        s = i % 2
        src = k.ada_w[li].rearrange("(kc p) n -> p kc n", p=128)[:, :, b * 256:(b + 1) * 256]
        P.dma("pool", slots[s][:, :, :], src, self.ds[s], wr=[self.bw[s]], waits=waits)

    def step(self, slots, waits=()):
        if self.nload - self.ncons >= 2:
            self._consume(slots)
        if self.nload < len(self.blocks):
            self._load(slots, waits)

    def drain(self, slots):
        while self.ncons < self.nload:
            self._consume(slots)

    def finish(self):
        if self.finished:
            return
        self.finished = True
        k, P, nc = self.k, self.k.P, self.k.nc
        P.barrier()
        slots = [carve(k, 4096 + s * 8192, [128, KC, 256], BF16) for s in range(2)]
        while self.ncons < len(self.blocks):
            self.step(slots)
            if self.nload >= len(self.blocks):
                self.drain(slots)
        nr = len(self.layers)
        MS = carve(k, 0, [128, nr * 72], F32)
        bMS, bcin, bcout = Buf(), Buf(), Buf()
        P.op("dve", lambda e: e.tensor_copy(out=MS[:, :], in_=self.psm[:, 0:nr * 72]), rd=[self.bpsm], wr=[bMS])
        cc_in = nc.dram_tensor("ada2_cc_in", [128, nr * 72], F32).ap()
        cc_out = nc.dram_tensor("ada2_cc_out", [256, nr * 72], F32).ap()
        P.dma("sp", cc_in, MS[:, :], getds(k, "adacc"), rd=[bMS], wr=[bcin])
        P.op("pool", lambda e: e.collective_compute("AllGather", ALU.bypass, replica_groups=PAIRS,
                                                    ins=[cc_in.opt()], outs=[cc_out.opt()]), rd=[bcin], wr=[bcout])
        ccv = cc_out.rearrange("(r p) n -> p r n", p=128)
        for i, L in enumerate(self.layers):
            P.dma("sp", k.mods[:, L, :].rearrange("p (r j) -> p r j", r=2), ccv[:, :, i * 72:(i + 1) * 72],
                  getds(k, "adacc"), rd=[bcout], wr=[k.bmods])
        for L in self.layers:
            P.op("dve", (lambda e, L=L: e.tensor_tensor(out=k.mods[:, L, :], in0=k.mods[:, L, :],
                                                         in1=sm(k, f"adab{L}"), op=ALU.add)),
                 rd=[k.bsm], wr=[k.bmods])
            mods_finish(k, L)


def mods_finish(k, L):
    P = k.P
    for s3 in range(3):
        P.op("dve", (lambda e, L=L, s3=s3: e.scalar_tensor_tensor(
            out=k.modA[:, L, s3, :], in0=k.mods[:, L, (3 * s3 + 1) * 16:(3 * s3 + 2) * 16], scalar=1.0,
            in1=sm(k, f"ng{L}_{s3}"), op0=ALU.add, op1=ALU.mult)), rd=[k.bmods, k.bsm], wr=[k.bmods])
        P.op("dve", (lambda e, L=L, s3=s3: e.tensor_scalar(
            out=k.modG[:, L, s3, :], in0=k.mods[:, L, (3 * s3 + 2) * 16:(3 * s3 + 3) * 16],
            scalar1=(1.0 if s3 == 1 else 0.5), scalar2=None, op0=ALU.mult)), rd=[k.bmods], wr=[k.bmods])


def mods_split(k, layers):
    P, nc = k.P, k.nc
    nl = len(layers)
    MS = carve(k, 0, [128, nl * 72], F32)
    WB0 = 4096
    wslots = [carve(k, WB0 + s * 16384, [128, KC, 512], BF16) for s in range(2)]
    bw = [Buf(), Buf()]
    dsw = [getds(k, "adaw0"), getds(k, "adaw1")]
    bMS, bcin, bcout = Buf(), Buf(), Buf()
    NBLK = 18
    blocks = [(li, b) for li in range(nl) for b in range(NBLK)]
    psm = k.pb[0]

    def load(i):
        li, b = blocks[i]
        s = i % 2
        src = k.ada_w[li].rearrange("(kc p) n -> p kc n", p=128)[:, :, b * 512:(b + 1) * 512]
        P.dma("pool", wslots[s][:, :, :], src, dsw[s], wr=[bw[s]])

    for i in range(min(2, len(blocks))):
        load(i)
    for i, (li, b) in enumerate(blocks):
        s = i % 2
        for dc in range(4):
            c0 = li * 72 + b * 4 + dc
            for kc in range(KC):
                lastk = kc == KC - 1
                P.op("pe", (lambda e, s=s, dc=dc, kc=kc, c0=c0: e.matmul(
                    psm[:, c0:c0 + 1], lhsT=wslots[s][:, kc, dc * 128:(dc + 1) * 128],
                    rhs=k.cact[:, kc:kc + 1], start=(kc == 0), stop=(kc == KC - 1))),
                    rd=[bw[s], k.bconst] if kc == 0 or lastk else [],
                    wr=[k.bpb[0]] if (lastk and dc == 3) else [],
                    waits=(([k.bpb[0].w] + k.bpb[0].r) if (kc == 0 and dc == 0 and i == 0) else []),
                    sig=(lastk and dc == 3))
        if i + 2 < len(blocks):
            load(i + 2)
    P.op("dve", lambda e: e.tensor_copy(out=MS[:, :], in_=psm[:, 0:nl * 72]), rd=[k.bpb[0]], wr=[bMS])
    cc_in = nc.dram_tensor("ada_cc_in", [128, nl * 72], F32).ap()
    cc_out = nc.dram_tensor("ada_cc_out", [256, nl * 72], F32).ap()
    P.dma("sp", cc_in, MS[:, :], getds(k, "adacc"), rd=[bMS], wr=[bcin])
    P.op("pool", lambda e: e.collective_compute("AllGather", ALU.bypass, replica_groups=PAIRS,
                                                ins=[cc_in.opt()], outs=[cc_out.opt()]), rd=[bcin], wr=[bcout])
    ccv = cc_out.rearrange("(r p) n -> p r n", p=128)
    for li, L in enumerate(layers):
        P.dma("sp", k.mods[:, L, :].rearrange("p (r j) -> p r j", r=2), ccv[:, :, li * 72:(li + 1) * 72],
              getds(k, "adacc"), rd=[bcout], wr=[k.bmods])
    for li, L in enumerate(layers):
        P.op("dve", (lambda e, L=L: e.tensor_tensor(out=k.mods[:, L, :], in0=k.mods[:, L, :],
                                                     in1=sm(k, f"adab{L}"), op=ALU.add)),
             rd=[k.bsm], wr=[k.bmods])
        mods_finish(k, L)


def mods_local(k, layers):
    P, nc = k.P, k.nc
    NBLK = 36
    wslots = [carve(k, s * 16384, [128, KC, 512], BF16) for s in range(2)]
    bw = [Buf("adaw0"), Buf("adaw1")]
    dsw = [getds(k, "adaw0"), getds(k, "adaw1")]
    blocks = [(L, b) for L in layers for b in range(NBLK)]
    psm = k.pb[0]

    def load(i):
        L, b = blocks[i]
        s = i % 2
        src = k.ada_w[k.li[L]].rearrange("(kc p) n -> p kc n", p=128)[:, :, b * 512:(b + 1) * 512]
        P.dma("pool", wslots[s][:, :, :], src, dsw[s], wr=[bw[s]])

    for i in range(min(2, len(blocks))):
        load(i)
    for i, (L, b) in enumerate(blocks):
        s = i % 2
        for dc in range(4):
            colj = (b * 4 + dc)
            for kc in range(KC):
                lastk = kc == KC - 1
                P.op("pe", (lambda e, s=s, dc=dc, kc=kc, colj=colj: e.matmul(
                    psm[:, colj:colj + 1], lhsT=wslots[s][:, kc, dc * 128:(dc + 1) * 128],
                    rhs=k.cact[:, kc:kc + 1], start=(kc == 0), stop=(kc == KC - 1))),
                    rd=[bw[s], k.bconst] if kc == 0 or lastk else [],
                    wr=[k.bpb[0]] if (lastk and dc == 3) else [],
                    waits=(([k.bpb[0].w] + k.bpb[0].r) if (kc == 0 and dc == 0 and b == 0) else []),
                    sig=(lastk and dc == 3))
        if b == NBLK - 1:
            P.op("dve", (lambda e, L=L: e.tensor_tensor(out=k.mods[:, L, :], in0=psm[:, 0:144],
                                                         in1=sm(k, f"adab{L}"), op=ALU.add)),
                 rd=[k.bpb[0], k.bsm], wr=[k.bmods])
            mods_finish(k, L)
        if i + 2 < len(blocks):
            load(i + 2)


def norm_modulate(k, A_fn, S_fn, out_fn, base, out_is_bf16=True, bout=None):
    P = k.P
    sq = [carve(k, base + i * 2048, [128, 512], F32) for i in range(2)]
    rstd = [carve(k, base + 4096 + i * 2048, [128, 512], F32) for i in range(2)]
    tmp = [carve(k, base + 8192 + i * 2048, [128, 512], F32) for i in range(2)]
    bsq = [Buf(), Buf()]
    brs = [Buf(), Buf()]
    btmp = [Buf(), Buf()]
    if bout is None:
        bout = Buf()
    if not isinstance(bout, (list, tuple)):
        bout = [bout, bout]
    for half in range(2):
        ps = k.pb[6 + half]
        bps = k.bpb[6 + half]
        hs = slice(half * 512, (half + 1) * 512)
        for kc in range(KC):
            i = kc % 2
            if kc % 2 == 0:
                P.op("act", (lambda e, kc=kc, i=i, hs=hs: e.activation(out=sq[i][:, :], in_=k.xT[:, kc, hs], func=AF.Square)),
                     rd=[k.bx], wr=[bsq[i]])
            else:
                P.op("dve", (lambda e, kc=kc, i=i, hs=hs: e.tensor_tensor(out=sq[i][:, :], in0=k.xT[:, kc, hs],
                                                                  in1=k.xT[:, kc, hs], op=ALU.mult)),
                     rd=[k.bx], wr=[bsq[i]])
            P.op("pe", (lambda e, kc=kc, i=i, ps=ps: e.matmul(ps[:, :], lhsT=k.ones[:, :], rhs=sq[i][:, :],
                                                             start=(kc == 0), stop=(kc == KC - 1))),
                 rd=[bsq[i], k.bconst], wr=[bps] if kc == KC - 1 else [],
                 waits=([bps.w] + bps.r) if kc == 0 else [], sig=True)
    for half in range(2):
        ps = k.pb[6 + half]
        bps = k.bpb[6 + half]
        hs = slice(half * 512, (half + 1) * 512)
        P.op("act", (lambda e, half=half, ps=ps: e.activation(out=rstd[half][:, :], in_=ps[:, :], func=AF.Sqrt,
                                                              bias=k.epsc[:, 0:1], scale=1.0 / D)),
             rd=[bps, k.bconst], wr=[brs[half]])
        P.op("dve", (lambda e, half=half: e.reciprocal(out=rstd[half][:, :], in_=rstd[half][:, :])),
             rd=[brs[half]], wr=[brs[half]])
        for kc in range(KC):
            i = kc % 2
            P.op("dve", (lambda e, kc=kc, i=i, half=half, hs=hs: e.tensor_tensor(out=tmp[i][:, :], in0=k.xT[:, kc, hs],
                                                                         in1=rstd[half][:, :], op=ALU.mult)),
                 rd=[k.bx, brs[half]], wr=[btmp[i]])
            if S_fn is not None:
                P.op("act", (lambda e, kc=kc, i=i, half=half: e.activation(
                    out=out_fn(kc, half), in_=tmp[i][:, :], func=AF.Identity, bias=S_fn(kc), scale=A_fn(kc))),
                    rd=[btmp[i], k.bmods, k.bsm], wr=[bout[half]])
            else:
                P.op("act", (lambda e, kc=kc, i=i, half=half: e.activation(
                    out=out_fn(kc, half), in_=tmp[i][:, :], func=AF.Identity, bias=k.zeroc[:, 0:1], scale=A_fn(kc))),
                    rd=[btmp[i], k.bmods, k.bsm], wr=[bout[half]])
    return bout


def ffn_stage(k, L, which, side=None):
    P = k.P
    P.barrier()
    use_side = side is not None and side.active()
    pairs = [(0, 1), (4, 5)] if use_side else [(0, 1), (2, 3), (4, 5)]
    SJ = [carve(k, 116736, [128, KC, 256], BF16), carve(k, 104448, [128, KC, 256], BF16)]
    s3 = 0 if which == 0 else 2
    hT = carve(k, 0, [128, KC, T], BF16)
    WG = [carve(k, 32768 + s * 8192, [128, KC, 256], BF16) for s in range(2)]
    WU = [carve(k, 49152 + s * 8192, [128, KC, 256], BF16) for s in range(2)]
    WD = [carve(k, 65536 + s * 8192, [128, 8, 512], BF16) for s in range(2)]
    ACT = carve(k, 81920, [128, 8, T], BF16)
    SIL = [carve(k, 98304 + s * 2048, [128, 512], F32) for s in range(3)]
    NB = 104448
    bh = [Buf("hT0"), Buf("hT1")]
    bWG = [Buf(), Buf()]
    bWU = [Buf(), Buf()]
    bWD = [Buf(), Buf()]
    bsil = [Buf(), Buf(), Buf()]
    dWG = [getds(k, "wg0"), getds(k, "wg1")]
    dWU = [getds(k, "wu0"), getds(k, "wu1")]
    dWD = [getds(k, "wd0"), getds(k, "wd1")]
    wgv = k.wg[k.li[L], which].rearrange("(kc p) f -> p kc f", p=128)
    wuv = k.wu[k.li[L], which].rearrange("(kc p) f -> p kc f", p=128)
    wdv = k.wd[k.li[L], which].rearrange("(fc p) d -> p fc d", p=128)

    groups = [(g * 2, min(2, FC - g * 2)) for g in range((FC + 1) // 2)]
    phases = []
    for p0 in range(0, FC, 8):
        phases.append((p0, min(8, FC - p0)))
    gl = {"n": 0}

    def load_group(gi):
        f0, nf = groups[gi]
        s = gi % 2
        P.dma("pool", WG[s][:, :, 0:nf * 128], wgv[:, :, f0 * 128:(f0 + nf) * 128], dWG[s], wr=[bWG[s]])
        P.dma("pool", WU[s][:, :, 0:nf * 128], wuv[:, :, f0 * 128:(f0 + nf) * 128], dWU[s], wr=[bWU[s]])

    dl = {"n": 0}
    dlist = [(pi, dg) for pi in range(len(phases)) for dg in range(4)]

    def load_wd(i):
        pi, dg = dlist[i]
        p0, pn = phases[pi]
        s = i % 2
        P.dma("pool", WD[s][:, 0:pn, :], wdv[:, p0:p0 + pn, dg * 512:(dg + 1) * 512], dWD[s], wr=[bWD[s]])

    load_group(0)
    load_group(1)
    next_group = 2
    load_wd(0)
    load_wd(1)
    next_wd = 2
    norm_modulate(k, lambda kc: k.modA[:, L, s3, kc:kc + 1],
                  lambda kc: k.mods[:, L, (3 * s3) * 16 + kc:(3 * s3) * 16 + kc + 1],
                  lambda kc, half: hT[:, kc, half * 512:(half + 1) * 512], NB, bout=bh)
    unit = 0
    dunit = 0
    bact = Buf("act")
    gi = 0
    for pi, (p0, pn) in enumerate(phases):
        ngr = (pn + 1) // 2
        for _ in range(ngr):
            f0, nf = groups[gi]
            s = gi % 2
            for half in range(2):
                hs = slice(half * 512, (half + 1) * 512)
                for fi in range(nf):
                    f = f0 + fi
                    slot = unit % 3
                    pr = pairs[unit % len(pairs)]
                    unit += 1
                    pg, pu = k.pb[pr[0]], k.pb[pr[1]]
                    bpg, bpu = k.bpb[pr[0]], k.bpb[pr[1]]
                    for (W, bW, ps, bps) in ((WG[s], bWG[s], pg, bpg), (WU[s], bWU[s], pu, bpu)):
                        for kc in range(KC):
                            P.op("pe", (lambda e, W=W, ps=ps, kc=kc, fi=fi, hs=hs: e.matmul(
                                ps[:, :], lhsT=W[:, kc, fi * 128:(fi + 1) * 128], rhs=hT[:, kc, hs],
                                start=(kc == 0), stop=(kc == KC - 1))),
                                rd=[bW, bh[half]] if kc in (0, KC - 1) else [],
                                wr=[bps] if kc == KC - 1 else [],
                                waits=([bps.w] + bps.r) if kc == 0 else [], sig=(kc == KC - 1))
                    P.op("act", (lambda e, slot=slot, pg=pg: e.activation(out=SIL[slot][:, :], in_=pg[:, :], func=AF.Silu)),
                         rd=[bpg], wr=[bsil[slot]])
                    P.op("dve", (lambda e, slot=slot, pu=pu, f=f, p0=p0, hs=hs: e.tensor_tensor(
                        out=ACT[:, f - p0, hs], in0=SIL[slot][:, :], in1=pu[:, :], op=ALU.mult)),
                        rd=[bsil[slot], bpu], wr=[bact])
                    if use_side:
                        side.step(SJ, waits=[bh[0].w, bh[1].w])
            gi += 1
            if next_group < len(groups):
                load_group(next_group)
                next_group += 1
        for dg in range(4):
            i = pi * 4 + dg
            s = i % 2
            for dc4 in range(4):
                dc = dg * 4 + dc4
                for half in range(2):
                    hs = slice(half * 512, (half + 1) * 512)
                    b = 6 + (dunit % 2)
                    dunit += 1
                    ps, bps = k.pb[b], k.bpb[b]
                    for fi in range(pn):
                        P.op("pe", (lambda e, ps=ps, fi=fi, dc4=dc4, hs=hs, s=s, pn=pn: e.matmul(
                            ps[:, :], lhsT=WD[s][:, fi, dc4 * 128:(dc4 + 1) * 128], rhs=ACT[:, fi, hs],
                            start=(fi == 0), stop=(fi == pn - 1))),
                            rd=[bWD[s], bact] if fi in (0, pn - 1) else [],
                            wr=[bps] if fi == pn - 1 else [],
                            waits=([bps.w] + bps.r) if fi == 0 else [], sig=(fi == pn - 1))
                    P.op("dve", (lambda e, ps=ps, dc=dc, hs=hs: e.scalar_tensor_tensor(
                        out=k.xT[:, dc, hs], in0=ps[:, :], scalar=k.modG[:, L, s3, dc:dc + 1],
                        in1=k.xT[:, dc, hs], op0=ALU.mult, op1=ALU.add)),
                        rd=[bps, k.bmods], wr=[k.bx])
            if next_wd < len(dlist):
                load_wd(next_wd)
                next_wd += 1
    if use_side:
        side.drain(SJ)


def conv_stage(k, L, j, side=None):
    P, nc = k.P, k.nc
    P.barrier()
    pairs = [(0, 1), (4, 5)] if (side is not None and side.active()) else [(0, 1), (2, 3), (4, 5)]
    ci = k.ci[L]
    UW = 1056
    hT = carve(k, 0, [128, KC, T], BF16)
    U = carve(k, 32768, [128, KC, UW], BF16)
    CB = 32768 + 33792
    SIL = [carve(k, CB + 12288 + s * 2048, [128, 512], F32) for s in range(3)]
    SBh = carve(k, CB + 20480, [128, KC, 32], BF16)
    RBh = carve(k, CB + 21504, [128, 2, KC, 32], BF16)
    WB = CB + 32768
    WG = [carve(k, WB + s * 8192, [128, KC, 256], BF16) for s in range(2)]
    WU = [carve(k, WB + 16384 + s * 8192, [128, KC, 256], BF16) for s in range(2)]
    bh, bU = [Buf("hT0"), Buf("hT1")], Buf("U")
    bWG, bWU = [Buf(), Buf()], [Buf(), Buf()]
    bsil = [Buf(), Buf(), Buf()]
    dWG = [getds(k, "wg0"), getds(k, "wg1")]
    dWU = [getds(k, "wu0"), getds(k, "wu1")]
    win = k.conv_w_in[ci].rearrange("(kc p) f -> p kc f", p=128)
    wout = k.conv_w_out[ci].rearrange("(kc p) f -> p kc f", p=128)
    cbin = f"cbin{j}"

    def load_group(gi):
        s = gi % 2
        P.dma("pool", WG[s][:, :, :], win[:, :, D + gi * 256:D + (gi + 1) * 256], dWG[s], wr=[bWG[s]])
        P.dma("pool", WU[s][:, :, :], win[:, :, gi * 256:(gi + 1) * 256], dWU[s], wr=[bWU[s]])

    load_group(0)
    load_group(1)
    norm_modulate(k, lambda kc: k.modA[:, L, 1, kc:kc + 1],
                  lambda kc: k.mods[:, L, 48 + kc:48 + kc + 1],
                  lambda kc, half: hT[:, kc, half * 512:(half + 1) * 512], CB, bout=bh)
    unit = 0
    for gi in range(8):
        s = gi % 2
        for half in range(2):
            hs = slice(half * 512, (half + 1) * 512)
            for fi in range(2):
                dc = gi * 2 + fi
                slot = unit % 3
                pr = pairs[unit % len(pairs)]
                unit += 1
                pg, pu = k.pb[pr[0]], k.pb[pr[1]]
                bpg, bpu = k.bpb[pr[0]], k.bpb[pr[1]]
                for (W, bW, ps, bps) in ((WG[s], bWG[s], pg, bpg), (WU[s], bWU[s], pu, bpu)):
                    for kc in range(KC):
                        P.op("pe", (lambda e, W=W, ps=ps, kc=kc, fi=fi, hs=hs: e.matmul(
                            ps[:, :], lhsT=W[:, kc, fi * 128:(fi + 1) * 128], rhs=hT[:, kc, hs],
                            start=(kc == 0), stop=(kc == KC - 1))),
                            rd=[bW, bh[half]] if kc in (0, KC - 1) else [],
                            wr=[bps] if kc == KC - 1 else [],
                            waits=([bps.w] + bps.r) if kc == 0 else [], sig=(kc == KC - 1))
                P.op("act", (lambda e, slot=slot, pg=pg, dc=dc: e.activation(
                    out=SIL[slot][:, :], in_=pg[:, :], func=AF.Sigmoid, bias=sm(k, cbin, 16 + dc, 1), scale=1.0)),
                    rd=[bpg, k.bsm], wr=[bsil[slot]])
                P.op("dve", (lambda e, slot=slot, pu=pu, dc=dc, half=half: e.scalar_tensor_tensor(
                    out=U[:, dc, 16 + half * 512:16 + (half + 1) * 512], in0=pu[:, :], scalar=sm(k, cbin, dc, 1),
                    in1=SIL[slot][:, :], op0=ALU.add, op1=ALU.mult)),
                    rd=[bsil[slot], bpu, k.bsm], wr=[bU])
        if gi + 2 < 8:
            load_group(gi + 2)
    cc_in = nc.dram_tensor(f"cvh_in{j}", [128, 512], BF16)
    cc_out = nc.dram_tensor(f"cvh_out{j}", [256, 512], BF16)
    bSB, bRB, bcin, bcout = Buf(), Buf(), Buf(), Buf()
    P.op("dve", lambda e: e.tensor_copy(out=SBh[:, :, 0:16], in_=U[:, :, 16:32]), rd=[bU], wr=[bSB])
    P.op("dve", lambda e: e.tensor_copy(out=SBh[:, :, 16:32], in_=U[:, :, 1024:1040]), rd=[bU], wr=[bSB])
    P.dma("sp", cc_in.ap(), SBh.rearrange("p a b -> p (a b)"), getds(k, "halo"), rd=[bSB], wr=[bcin])
    P.op("pool", lambda e: e.collective_compute("AllGather", ALU.bypass, replica_groups=PAIRS,
                                                ins=[cc_in.ap().opt()], outs=[cc_out.ap().opt()]),
         rd=[bcin], wr=[bcout])
    P.dma("sp", RBh.rearrange("p r a b -> p r (a b)"), cc_out.ap().rearrange("(r p) n -> p r n", p=128),
          getds(k, "halo"), rd=[bcout], wr=[bRB])
    P.op("dve", lambda e: e.tensor_scalar(out=U[:, :, 1:16], in0=RBh[:, 0, :, 17:32], scalar1=sm(k, "is_odd"),
                                          scalar2=None, op0=ALU.mult), rd=[bRB, k.bsm], wr=[bU])
    P.op("dve", lambda e: e.tensor_scalar(out=U[:, :, 1040:1055], in0=RBh[:, 1, :, 0:15], scalar1=sm(k, "is_even"),
                                          scalar2=None, op0=ALU.mult), rd=[bRB, k.bsm], wr=[bU])
    P.barrier()
    SN = carve(k, 0, [128, KC, 512], BF16)
    DG = [carve(k, 16384 + s * 7936, [128, CONVW, 128], BF16) for s in range(2)]
    C = carve(k, CB, [128, KC, 512], F32)
    WO = [carve(k, WB + s * 8192, [128, KC, 256], BF16) for s in range(2)]
    TB = WB + 16384
    csq = [carve(k, TB + i * 2048, [128, 512], F32) for i in range(2)]
    MEAN = carve(k, TB + 4096, [128, 512], F32)
    T1 = carve(k, TB + 6144, [128, 512], F32)
    RSTD = carve(k, TB + 8192, [128, 512], F32)
    NMR = carve(k, TB + 10240, [128, 512], F32)
    tt = [carve(k, TB + 12288 + i * 2048, [128, 512], F32) for i in range(2)]
    bSN, bC = Buf("SN"), Buf("C")
    bDG = [Buf(), Buf()]
    bWO = [Buf(), Buf()]
    bcsq = [Buf(), Buf()]
    bst, btt = Buf("stats"), [Buf(), Buf()]
    dWO = [getds(k, "wo0"), getds(k, "wo1")]
    ident3 = sm(k, "ident").unsqueeze(1).to_broadcast([128, CONVW, 128])
    wo_loads = [(half, g) for half in range(2) for g in range(8)]

    def load_wo(i):
        half, g = wo_loads[i]
        s = i % 2
        P.dma("pool", WO[s][:, :, :], wout[:, :, g * 256:(g + 1) * 256], dWO[s], wr=[bWO[s]])

    load_wo(0)
    load_wo(1)
    nwo = 2
    cu = 0
    ou = 0
    for half in range(2):
        S1, S2 = k.pb[4], k.pb[5]
        bS1, bS2 = k.bpb[4], k.bpb[5]
        pending_stats = []
        for dc in range(KC):
            s = dc % 2
            w3 = sm(k, f"cwdw{j}", dc * CONVW, CONVW).unsqueeze(2).to_broadcast([128, CONVW, 128])
            P.op("pool" if dc % 2 == 0 else "dve",
                 (lambda e, s=s, w3=w3: e.tensor_tensor(out=DG[s][:, :, :], in0=ident3, in1=w3, op=ALU.mult)),
                 rd=[k.bsm], wr=[bDG[s]])
            cps, bcps = k.pb[cu % 3], k.bpb[cu % 3]
            cu += 1
            for t in range(CONVW):
                c0 = 1 + half * 512 + t
                P.op("pe", (lambda e, cps=cps, s=s, t=t, dc=dc, c0=c0: e.matmul(
                    cps[:, :], lhsT=DG[s][:, t, :], rhs=U[:, dc, c0:c0 + 512], start=(t == 0), stop=(t == CONVW - 1))),
                    rd=[bDG[s], bU] if t in (0, CONVW - 1) else [],
                    wr=[bcps] if t == CONVW - 1 else [],
                    waits=([bcps.w] + bcps.r) if t == 0 else [], sig=(t == CONVW - 1))
            P.op("act", (lambda e, cps=cps, dc=dc: e.activation(out=C[:, dc, :], in_=cps[:, :], func=AF.Identity,
                                                               bias=sm(k, f"cbdw{j}", dc, 1), scale=1.0)),
                 rd=[bcps, k.bsm], wr=[bC])
            i = dc % 2
            P.op("dve", (lambda e, dc=dc, i=i: e.tensor_tensor(out=csq[i][:, :], in0=C[:, dc, :], in1=C[:, dc, :],
                                                              op=ALU.mult)), rd=[bC], wr=[bcsq[i]])
            def stats_mm(dc=dc, i=i):
                P.op("pe", (lambda e, dc=dc: e.matmul(S1[:, :], lhsT=k.ones[:, :], rhs=C[:, dc, :],
                                                      start=(dc == 0), stop=(dc == KC - 1))),
                     rd=[bC, k.bconst], wr=[bS1] if dc == KC - 1 else [],
                     waits=([bS1.w] + bS1.r) if dc == 0 else [], sig=True)
                P.op("pe", (lambda e, dc=dc, i=i: e.matmul(S2[:, :], lhsT=k.ones[:, :], rhs=csq[i][:, :],
                                                           start=(dc == 0), stop=(dc == KC - 1))),
                     rd=[bcsq[i], k.bconst], wr=[bS2] if dc == KC - 1 else [],
                     waits=([bS2.w] + bS2.r) if dc == 0 else [], sig=True)
            pending_stats.append(stats_mm)
            if len(pending_stats) > 1:
                pending_stats.pop(0)()
        while pending_stats:
            pending_stats.pop(0)()
        P.op("dve", lambda e: e.tensor_scalar(out=MEAN[:, :], in0=S1[:, :], scalar1=1.0 / D, scalar2=None,
                                              op0=ALU.mult), rd=[bS1], wr=[bst])
        P.op("dve", lambda e: e.tensor_tensor(out=T1[:, :], in0=MEAN[:, :], in1=MEAN[:, :], op=ALU.mult),
             rd=[bst], wr=[bst])
        P.op("dve", lambda e: e.scalar_tensor_tensor(out=T1[:, :], in0=S2[:, :], scalar=1.0 / D, in1=T1[:, :],
                                                     op0=ALU.mult, op1=ALU.subtract), rd=[bS2, bst], wr=[bst])
        P.op("act", lambda e: e.activation(out=RSTD[:, :], in_=T1[:, :], func=AF.Sqrt, bias=k.epsc[:, 0:1], scale=1.0),
             rd=[bst, k.bconst], wr=[bst])
        P.op("dve", lambda e: e.reciprocal(out=RSTD[:, :], in_=RSTD[:, :]), rd=[bst], wr=[bst])
        P.op("dve", lambda e: e.scalar_tensor_tensor(out=NMR[:, :], in0=MEAN[:, :], scalar=-1.0, in1=RSTD[:, :],
                                                     op0=ALU.mult, op1=ALU.mult), rd=[bst], wr=[bst])
        for dc in range(KC):
            i = dc % 2
            P.op("dve", (lambda e, dc=dc, i=i: e.tensor_tensor(out=tt[i][:, :], in0=C[:, dc, :], in1=RSTD[:, :],
                                                              op=ALU.mult)), rd=[bC, bst], wr=[btt[i]])
            P.op("dve", (lambda e, i=i: e.tensor_tensor(out=tt[i][:, :], in0=tt[i][:, :], in1=NMR[:, :], op=ALU.add)),
                 rd=[bst], wr=[btt[i]])
            P.op("act", (lambda e, dc=dc, i=i: e.activation(out=SN[:, dc, :], in_=tt[i][:, :], func=AF.Silu,
                                                           bias=sm(k, f"clnb{j}", dc, 1), scale=sm(k, f"clng{j}", dc, 1))),
                 rd=[btt[i], k.bsm], wr=[bSN])
        hs = slice(half * 512, (half + 1) * 512)
        for g in range(8):
            i = half * 8 + g
            s = i % 2
            for fi in range(2):
                dc = g * 2 + fi
                b = 6 + ou % 2
                ou += 1
                yps, byps = k.pb[b], k.bpb[b]
                for kc in range(KC):
                    P.op("pe", (lambda e, yps=yps, s=s, kc=kc, fi=fi: e.matmul(
                        yps[:, :], lhsT=WO[s][:, kc, fi * 128:(fi + 1) * 128], rhs=SN[:, kc, :],
                        start=(kc == 0), stop=(kc == KC - 1))),
                        rd=[bWO[s], bSN] if kc in (0, KC - 1) else [],
                        wr=[byps] if kc == KC - 1 else [],
                        waits=([byps.w] + byps.r) if kc == 0 else [], sig=(kc == KC - 1))
                ti = ou % 2
                P.op("dve", (lambda e, yps=yps, dc=dc, ti=ti: e.tensor_scalar(
                    out=tt[ti][:, :], in0=yps[:, :], scalar1=sm(k, f"cbout{j}", dc, 1),
                    scalar2=k.modG[:, L, 1, dc:dc + 1], op0=ALU.add, op1=ALU.mult)),
                    rd=[byps, k.bsm, k.bmods], wr=[btt[ti]])
                P.op("dve", (lambda e, dc=dc, ti=ti, hs=hs: e.tensor_tensor(
                    out=k.xT[:, dc, hs], in0=k.xT[:, dc, hs], in1=tt[ti][:, :], op=ALU.add)),
                    rd=[btt[ti]], wr=[k.bx])
            if nwo < len(wo_loads):
                load_wo(nwo)
                nwo += 1


def gla_stage(k, L):
    P, nc = k.P, k.nc
    P.barrier()
    win = k.gla_w_in[0].rearrange("(kc p) f -> p kc f", p=128)
    wout = k.gla_w_out[0].rearrange("(kc p) f -> p kc f", p=128)
    hT = carve(k, 0, [128, KC, T], BF16)
    OH = carve(k, 32768, [128, 4, T], F32)
    WS = [carve(k, 49152 + s * 8192, [128, KC, 256], BF16) for s in range(2)]
    Qb = carve(k, 65536, [128, 2, T], BF16)
    Kb = carve(k, 69632, [128, 2, T], BF16)
    LG = carve(k, 73728, [128, T], F32)
    Bc = carve(k, 77824, [128, T], F32)
    T1 = carve(k, 81920, [128, T], F32)
    QE = carve(k, 86016, [128, 2, T], BF16)
    KE = carve(k, 90112, [128, 2, T], BF16)
    KL = carve(k, 94208, [128, 2, T], BF16)
    QG = carve(k, 98304, [128, 2, T], BF16)
    Vt = carve(k, 102400, [128, 16, 512], BF16)
    S = carve(k, 118784, [128, 2, 512], F32)
    Sb = carve(k, 122880, [128, 2, 512], BF16)
    KLT = [carve(k, 124928 + i * 512, [128, 2, 128], BF16) for i in range(2)]
    ST = [carve(k, 125952 + i * 128, [128, 64], BF16) for i in range(2)]
    Z1 = carve(k, 126208, [128, 2, T], BF16)
    WA1 = carve(k, 130304, [128, 2, KC, 16], BF16)
    WA2 = carve(k, 131328, [128, 2, 256], BF16)
    SMT = carve(k, 132352, [128, 8, 16], F32)
    TOT, INC, PX, EP = SMT[:, 0, :], SMT[:, 1, :], SMT[:, 2, :], SMT[:, 3, :]
    DL = SMT[:, 4:6, :]
    ONE16 = SMT[:, 6, :]
    M01 = carve(k, 118784, [128, T], F32)
    bh, bOH, bQ, bK, bLG, bB, bT1 = [Buf("hT0"), Buf("hT1")], Buf("OH"), Buf("Q"), Buf("K"), Buf("LG"), Buf("B"), Buf("T1")
    bQE, bKE, bKL, bQG, bV, bS, bSb = Buf("QE"), Buf("KE"), Buf("KL"), Buf("QG"), Buf("Vt"), Buf("S"), Buf("Sb")
    bKLT, bST = [Buf(), Buf()], [Buf(), Buf()]
    bZ1, bWA1, bWA2, bSM = Buf(), Buf(), Buf(), Buf()
    bM01 = bS
    bWS = [Buf(), Buf()]
    dWS = [getds(k, "gws0"), getds(k, "gws1")]
    qg_d = nc.dram_tensor("gla_qg", [4, 2, 128, 2 * T], BF16).ap()
    o_d = nc.dram_tensor("gla_o", [4, 128, 4 * T], F32).ap()
    st_in = [[nc.dram_tensor(f"gla_st_in{h}_{d}", [128, 1024], F32).ap() for d in range(2)] for h in range(4)]
    st_out = [[nc.dram_tensor(f"gla_st_out{h}_{d}", [256, 1024], F32).ap() for d in range(2)] for h in range(4)]
    bsti = [[Buf() for d in range(2)] for h in range(4)]
    bsto = [[Buf() for d in range(2)] for h in range(4)]
    bqgd, bod, bstin, bstout = Buf(), Buf(), Buf(), Buf()
    identb = k.identb
    gba = "gba"

    pieces = []
    for h in range(4):
        pieces += [("q", h, h * 256), ("k", h, 1024 + h * 256), ("v0", h, 2048 + h * 512), ("v1", h, 2048 + h * 512 + 256)]
    pst = {"n": 0}

    def load_piece():
        i = pst["n"]
        if i >= len(pieces):
            return
        pst["n"] += 1
        s = i % 2
        c0 = pieces[i][2]
        P.dma("pool", WS[s][:, :, :], win[:, :, c0:c0 + 256], dWS[s], wr=[bWS[s]])

    load_piece()
    load_piece()
    wa1v = k.gla_wa1[0].rearrange("d (kc p) r -> p d kc r", p=128)
    for d in range(2):
        for q in range(4):
            P.dma("pool", WA1[:, d, q * 4:(q + 1) * 4, :], wa1v[:, d, q * 4:(q + 1) * 4, :], getds(k, "gwa"), wr=[bWA1])
    norm_modulate(k, lambda kc: k.modA[:, L, 1, kc:kc + 1],
                  lambda kc: k.mods[:, L, 48 + kc:48 + kc + 1],
                  lambda kc, half: hT[:, kc, half * 512:(half + 1) * 512], 102400, bout=bh)
    P.op("dve", lambda e: e.memset(SMT[:, 6, :], 1.0), wr=[bSM])
    pc = {"n": 0}

    def nextbank():
        b = pc["n"] % 8
        pc["n"] += 1
        return k.pb[b], k.bpb[b]

    for d in range(2):
        for half in range(2):
            hs = slice(half * 512, (half + 1) * 512)
            ps, bps = nextbank()
            for kc in range(KC):
                P.op("pe", (lambda e, ps=ps, d=d, kc=kc, hs=hs: e.matmul(
                    ps[0:16, :], lhsT=WA1[:, d, kc, :], rhs=hT[:, kc, hs], start=(kc == 0), stop=(kc == KC - 1))),
                    rd=[bWA1, bh[half]] if kc in (0, KC - 1) else [], wr=[bps] if kc == KC - 1 else [],
                    waits=([bps.w] + bps.r) if kc == 0 else [], sig=(kc == KC - 1))
            P.op("act", (lambda e, ps=ps, d=d, hs=hs: e.copy(out=Z1[0:16, d, hs], in_=ps[0:16, :])),
                 rd=[bps], wr=[bZ1])

    def proj_fm(dst, bdst, s, scale):
        for dkc in range(2):
            for half in range(2):
                hs = slice(half * 512, (half + 1) * 512)
                ps, bps = nextbank()
                for kc in range(KC):
                    P.op("pe", (lambda e, ps=ps, s=s, kc=kc, dkc=dkc, hs=hs: e.matmul(
                        ps[:, :], lhsT=WS[s][:, kc, dkc * 128:(dkc + 1) * 128], rhs=hT[:, kc, hs],
                        start=(kc == 0), stop=(kc == KC - 1))),
                        rd=[bWS[s], bh[half]] if kc in (0, KC - 1) else [], wr=[bps] if kc == KC - 1 else [],
                        waits=([bps.w] + bps.r) if kc == 0 else [], sig=(kc == KC - 1))
                P.op("act", (lambda e, ps=ps, dkc=dkc, hs=hs: e.mul(out=dst[:, dkc, hs], in_=ps[:, :], mul=scale)),
                     rd=[bps], wr=[bdst])

    if GLA_STOP == 1:
        return
    for h in range(4):
        s = (h * 4) % 2
        proj_fm(Qb, bQ, (h * 4) % 2, 1.0 / 16.0)
        load_piece()
        proj_fm(Kb, bK, (h * 4 + 1) % 2, 1.0)
        load_piece()
        for vp in range(2):
            s = (h * 4 + 2 + vp) % 2
            for c in range(16):
                ps, bps = nextbank()
                for kc in range(KC):
                    P.op("pe", (lambda e, ps=ps, s=s, kc=kc, c=c: e.matmul(
                        ps[0:64, 0:256], lhsT=hT[:, kc, c * 64:(c + 1) * 64], rhs=WS[s][:, kc, :],
                        start=(kc == 0), stop=(kc == KC - 1))),
                        rd=[bWS[s], bh[c // 8]] if kc in (0, KC - 1) else [], wr=[bps] if kc == KC - 1 else [],
                        waits=([bps.w] + bps.r) if kc == 0 else [], sig=(kc == KC - 1))
                eng = "act" if c % 2 == 0 else "dve"
                if eng == "act":
                    P.op("act", (lambda e, ps=ps, c=c, vp=vp: e.copy(
                        out=Vt[0:64, c, vp * 256:(vp + 1) * 256], in_=ps[0:64, 0:256])),
                        rd=[bps], wr=[bV])
                else:
                    P.op("dve", (lambda e, ps=ps, c=c, vp=vp: e.tensor_copy(
                        out=Vt[0:64, c, vp * 256:(vp + 1) * 256], in_=ps[0:64, 0:256])), rd=[bps], wr=[bV])
            load_piece()
        P.dma("pool", WA2[0:16, :, :], k.gla_wa2[0].rearrange("d r n -> r d n")[:, :, h * 256:(h + 1) * 256],
              getds(k, "gwa"), wr=[bWA2])
        if GLA_STOP == 2:
            return
        first_o = True
        for d in range(2):
            P.op("dve", lambda e: e.memset(M01[:, :], 1.0), wr=[bM01])
            P.op("dve", lambda e: e.memset(M01.rearrange("p (c s) -> p c s", s=64)[:, :, 0:1], 0.0), wr=[bM01])
            for dkc in range(2):
                gcol = d * 8 + h * 2 + dkc
                for half in range(2):
                    hs = slice(half * 512, (half + 1) * 512)
                    ps, bps = nextbank()
                    P.op("pe", (lambda e, ps=ps, d=d, dkc=dkc, hs=hs: e.matmul(
                        ps[:, :], lhsT=WA2[0:16, d, dkc * 128:(dkc + 1) * 128], rhs=Z1[0:16, d, hs],
                        start=True, stop=True)),
                        rd=[bWA2, bZ1], wr=[bps], waits=([bps.w] + bps.r))
                    P.op("act", (lambda e, ps=ps, hs=hs, gcol=gcol: e.activation(
                        out=LG[:, hs], in_=ps[:, :], func=AF.Sigmoid, bias=sm(k, gba, gcol, 1), scale=1.0)),
                        rd=[bps, k.bsm], wr=[bLG])
                P.op("act", lambda e: e.activation(out=LG[:, :], in_=LG[:, :], func=AF.Ln), rd=[bLG], wr=[bLG])
                P.op("dve", lambda e: e.tensor_tensor_scan(out=Bc[:, :], data0=M01[:, :], data1=LG[:, :], initial=0.0,
                                                           op0=ALU.mult, op1=ALU.add), rd=[bM01, bLG], wr=[bB])
                B3 = Bc.rearrange("p (c s) -> p c s", s=64)
                totv = B3[:, :, 63]
                P.op("dve", lambda e, totv=totv: e.tensor_copy(out=TOT, in_=totv), rd=[bB], wr=[bSM])
                tot3 = TOT.unsqueeze(2).to_broadcast([128, 16, 64])
                T13 = T1.rearrange("p (c s) -> p c s", s=64)
                LG3 = LG.rearrange("p (c s) -> p c s", s=64)
                if d == 0:
                    P.op("act", lambda e: e.activation(out=T1[:, :], in_=Bc[:, :], func=AF.Exp, scale=1.0 / 16),
                         rd=[bB], wr=[bT1])
                    P.op("dve", (lambda e, dkc=dkc: e.tensor_tensor(out=QE[:, dkc, :], in0=Qb[:, dkc, :], in1=T1[:, :],
                                                                   op=ALU.mult)), rd=[bQ, bT1], wr=[bQE])
                    P.op("act", lambda e: e.activation(out=T1[:, :], in_=Bc[:, :], func=AF.Exp, scale=-1.0 / 16),
                         rd=[bB], wr=[bT1])
                    P.op("dve", (lambda e, dkc=dkc: e.tensor_tensor(out=KE[:, dkc, :], in0=Kb[:, dkc, :], in1=T1[:, :],
                                                                   op=ALU.mult)), rd=[bK, bT1], wr=[bKE])
                    P.op("dve", (lambda e, tot3=tot3, T13=T13, B3=B3: e.tensor_tensor(out=T13, in0=tot3, in1=B3,
                                                                                     op=ALU.subtract)),
                         rd=[bSM, bB], wr=[bT1])
                    P.op("act", lambda e: e.activation(out=T1[:, :], in_=T1[:, :], func=AF.Exp, scale=1.0 / 16),
                         rd=[bT1], wr=[bT1])
                    P.op("dve", (lambda e, dkc=dkc: e.tensor_tensor(out=KL[:, dkc, :], in0=Kb[:, dkc, :], in1=T1[:, :],
                                                                   op=ALU.mult)), rd=[bK, bT1], wr=[bKL])
                else:
                    P.op("dve", (lambda e, tot3=tot3, T13=T13, B3=B3: e.tensor_tensor(out=T13, in0=tot3, in1=B3,
                                                                                     op=ALU.subtract)),
                         rd=[bSM, bB], wr=[bT1])
                    P.op("dve", lambda e: e.tensor_tensor(out=T1[:, :], in0=T1[:, :], in1=LG[:, :], op=ALU.add),
                         rd=[bLG], wr=[bT1])
                    P.op("dve", lambda e: e.tensor_tensor(out=LG[:, :], in0=Bc[:, :], in1=LG[:, :], op=ALU.subtract),
                         rd=[bB], wr=[bLG])
                    P.op("act", lambda e: e.activation(out=Bc[:, :], in_=T1[:, :], func=AF.Exp, scale=1.0 / 16),
                         rd=[bT1], wr=[bB])
                    P.op("dve", (lambda e, dkc=dkc: e.tensor_tensor(out=QE[:, dkc, :], in0=Qb[:, dkc, :], in1=Bc[:, :],
                                                                   op=ALU.mult)), rd=[bQ, bB], wr=[bQE])
                    P.op("act", lambda e: e.activation(out=Bc[:, :], in_=T1[:, :], func=AF.Exp, scale=-1.0 / 16),
                         rd=[bT1], wr=[bB])
                    P.op("dve", (lambda e, dkc=dkc: e.tensor_tensor(out=KE[:, dkc, :], in0=Kb[:, dkc, :], in1=Bc[:, :],
                                                                   op=ALU.mult)), rd=[bK, bB], wr=[bKE])
                    P.op("act", lambda e: e.activation(out=Bc[:, :], in_=LG[:, :], func=AF.Exp, scale=1.0 / 16),
                         rd=[bLG], wr=[bB])
                    P.op("dve", (lambda e, dkc=dkc: e.tensor_tensor(out=KL[:, dkc, :], in0=Kb[:, dkc, :], in1=Bc[:, :],
                                                                   op=ALU.mult)), rd=[bK, bB], wr=[bKL])
                P.op("act", (lambda e, dkc=dkc: e.activation(out=DL[:, dkc, :], in_=TOT, func=AF.Exp, scale=1.0 / 16)),
                     rd=[bSM], wr=[bSM])
                P.op("dve", lambda e: e.tensor_tensor_scan(out=INC, data0=ONE16, data1=TOT, initial=0.0,
                                                           op0=ALU.mult, op1=ALU.add), rd=[bSM], wr=[bSM])
                if d == 0:
                    P.op("dve", lambda e: e.tensor_tensor(out=PX, in0=INC, in1=TOT, op=ALU.subtract), rd=[bSM], wr=[bSM])
                else:
                    P.op("dve", lambda e: e.tensor_scalar(out=PX, in0=INC, scalar1=SMT[:, 1, 15:16], scalar2=-1.0,
                                                          op0=ALU.subtract, op1=ALU.mult), rd=[bSM], wr=[bSM])
                P.op("act", lambda e: e.activation(out=EP, in_=PX, func=AF.Exp, scale=1.0 / 16), rd=[bSM], wr=[bSM])
                ep3 = EP.unsqueeze(2).to_broadcast([128, 16, 64])
                P.op("dve", (lambda e, dkc=dkc, ep3=ep3: e.tensor_tensor(
                    out=QG[:, dkc, :].rearrange("p (c s) -> p c s", s=64),
                    in0=QE[:, dkc, :].rearrange("p (c s) -> p c s", s=64), in1=ep3, op=ALU.mult)),
                    rd=[bQE, bSM], wr=[bQG])
            P.dma("sp", qg_d[h, d], QG.rearrange("p a t -> p (a t)"), getds(k, "gqg"), rd=[bQG], wr=[bqgd])
            if GLA_STOP == 3:
                return
            P.op("dve", lambda e: e.memset(S[:, :, :], 0.0), wr=[bS])
            P.op("dve", lambda e: e.memset(Sb[:, :, :], 0.0), wr=[bSb])
            order = list(range(16)) if d == 0 else list(range(15, -1, -1))
            mask = sm(k, "mask_f" if d == 0 else "mask_b")
            for ci, c in enumerate(order):
                cs = slice(c * 64, (c + 1) * 64)
                sl = ci % 2
                pss, bpss = k.pb[sl], k.bpb[sl]
                pso, bpso = k.pb[2 + sl], k.bpb[2 + sl]
                psu = [k.pb[4], k.pb[5]]
                bpsu = [k.bpb[4], k.bpb[5]]
                ptp, bptp = k.pb[6 + sl], k.bpb[6 + sl]
                tpv = ptp[0:64, 0:128].bitcast(BF16).rearrange("p (a b) -> p a b", a=2)
                for dkc in range(2):
                    P.op("pe", (lambda e, tpv=tpv, dkc=dkc, cs=cs: e.transpose(tpv[:, dkc, :], KL[:, dkc, cs], identb[:, :])),
                         rd=[bKL, k.bconst], wr=[bptp] if dkc == 1 else [],
                         waits=([bptp.w] + bptp.r) if dkc == 0 else [], sig=(dkc == 1))
                P.op("act", (lambda e, tpv=tpv, sl=sl: e.copy(out=KLT[sl][0:64, :, :], in_=tpv)),
                     rd=[bptp], wr=[bKLT[sl]])
                for dkc in range(2):
                    P.op("pe", (lambda e, dkc=dkc, sl=sl, c=c: e.matmul(
                        psu[dkc][:, :], lhsT=KLT[sl][0:64, dkc, :], rhs=Vt[0:64, c, :], start=True, stop=True)),
                        rd=[bKLT[sl], bV], wr=[bpsu[dkc]], waits=([bpsu[dkc].w] + bpsu[dkc].r))
                for dkc in range(2):
                    P.op("pe", (lambda e, pss=pss, dkc=dkc, cs=cs: e.matmul(
                        pss[0:64, 0:64], lhsT=KE[:, dkc, cs], rhs=QE[:, dkc, cs], start=(dkc == 0), stop=(dkc == 1))),
                        rd=[bKE, bQE], wr=[bpss] if dkc == 1 else [],
                        waits=([bpss.w] + bpss.r) if dkc == 0 else [], sig=(dkc == 1))
                P.op("dve", (lambda e, pss=pss, sl=sl, mask=mask: e.tensor_tensor(
                    out=ST[sl][0:64, :], in0=pss[0:64, 0:64], in1=mask[0:64, :], op=ALU.mult)),
                    rd=[bpss, k.bsm], wr=[bST[sl]])
                pso3 = pso[:, 0:256].rearrange("p (a b) -> p a b", a=4)
                for ec in range(4):
                    es = slice(ec * 128, (ec + 1) * 128)
                    P.op("pe", (lambda e, pso3=pso3, ec=ec, es=es, cs=cs: e.matmul(
                        pso3[:, ec, :], lhsT=Sb[:, 0, es], rhs=QE[:, 0, cs], start=True, stop=False)),
                        rd=[bSb, bQE] if ec == 0 else [], waits=([bpso.w] + bpso.r) if ec == 0 else [], sig=False)
                    P.op("pe", (lambda e, pso3=pso3, ec=ec, es=es, cs=cs: e.matmul(
                        pso3[:, ec, :], lhsT=Sb[:, 1, es], rhs=QE[:, 1, cs], start=False, stop=False)), sig=False)
                    P.op("pe", (lambda e, pso3=pso3, ec=ec, es=es, c=c, sl=sl: e.matmul(
                        pso3[:, ec, :], lhsT=Vt[0:64, c, es], rhs=ST[sl][0:64, :], start=False, stop=True)),
                        rd=[bST[sl], bV, bSb, bQE] if ec in (0, 3) else [], wr=[bpso] if ec == 3 else [], sig=(ec == 3))
                if d == 0:
                    P.op("act", (lambda e, pso3=pso3, cs=cs: e.copy(out=OH[:, :, cs], in_=pso3)),
                         rd=[bpso], wr=[bOH])
                else:
                    P.op("dve", (lambda e, pso3=pso3, cs=cs: e.tensor_tensor(out=OH[:, :, cs], in0=OH[:, :, cs], in1=pso3,
                                                                            op=ALU.add)), rd=[bpso], wr=[bOH])
                for dkc in range(2):
                    P.op("dve", (lambda e, dkc=dkc, c=c: e.scalar_tensor_tensor(
                        out=S[:, dkc, :], in0=S[:, dkc, :], scalar=DL[:, dkc, c:c + 1], in1=psu[dkc][:, :],
                        op0=ALU.mult, op1=ALU.add)), rd=[bpsu[dkc], bSM], wr=[bS])
                P.op("act", lambda e: e.copy(out=Sb[:, :, :], in_=S[:, :, :]), rd=[bS], wr=[bSb])
            P.dma("sp", st_in[h][d], S.rearrange("p a b -> p (a b)"), getds(k, "gst"), rd=[bS], wr=[bsti[h][d]])
            P.op("pool", (lambda e, h=h, d=d: e.collective_compute(
                "AllGather", ALU.bypass, replica_groups=PAIRS, ins=[st_in[h][d].opt()], outs=[st_out[h][d].opt()])),
                rd=[bsti[h][d]], wr=[bsto[h][d]])
            if GLA_STOP == 4:
                return
        P.dma("sp", o_d[h], OH.rearrange("p a t -> p (a t)"), getds(k, "god"), rd=[bOH], wr=[bod])

    if GLA_STOP == 5:
        return
    P.barrier()
    if GLA_STOP == 6:
        return
    ON = carve(k, 65536, [128, KC, T], BF16)
    QGf = carve(k, 98304, [128, 2, T], BF16)
    QGb = carve(k, 102400, [128, 2, T], BF16)
    SST = [carve(k, 106496 + i * 4096, [128, 2, 512], F32) for i in range(2)]
    SIN = [carve(k, 114688 + i * 2048, [128, 2, 512], BF16) for i in range(2)]
    RS = carve(k, 118784, [128, T], F32)
    SQ = [carve(k, 122880 + i * 2048, [128, 512], F32) for i in range(2)]
    SG = [carve(k, 126976 + i * 2048, [128, 512], F32) for i in range(2)]
    TT = SQ[0]
    bON, bQGf, bQGb, bRS, bTT = Buf("ON"), Buf(), Buf(), Buf(), Buf()
    bSST, bSIN, bSQ, bSG = [Buf(), Buf()], [Buf(), Buf()], [Buf(), Buf()], [Buf(), Buf()]
    bTT = bSQ[0]
    gp = {"n": 0}
    gpieces = [(h, gpi) for h in range(4) for gpi in range(2)]

    def load_gpiece():
        i = gp["n"]
        if i >= len(gpieces):
            return
        gp["n"] += 1
        h, gpi = gpieces[i]
        c0 = 4096 + h * 512 + gpi * 256
        P.dma("pool", WS[i % 2][:, :, :], win[:, :, c0:c0 + 256], dWS[i % 2], wr=[bWS[i % 2]])

    load_gpiece()
    load_gpiece()
    for h in range(4):
        P.dma("sp", OH.rearrange("p a t -> p (a t)"), o_d[h], getds(k, "god2"), rd=[bod], wr=[bOH])
        P.dma("sp", QGf.rearrange("p a t -> p (a t)"), qg_d[h, 0], getds(k, "gqg2"), rd=[bqgd], wr=[bQGf])
        P.dma("sp", QGb.rearrange("p a t -> p (a t)"), qg_d[h, 1], getds(k, "gqg3"), rd=[bqgd], wr=[bQGb])
        for d in range(2):
            P.dma("sp", SST[d].rearrange("p a b -> p (a b)"), st_out[h][d][d * 128:(d + 1) * 128, :], getds(k, f"gsi{d}"),
                  rd=[bsto[h][d]], wr=[bSST[d]])
            P.op("dve", (lambda e, d=d: e.tensor_scalar(out=SIN[d][:, :, :], in0=SST[d][:, :, :],
                                                        scalar1=sm(k, "is_odd" if d == 0 else "is_even"), scalar2=None,
                                                        op0=ALU.mult)), rd=[bSST[d], k.bsm], wr=[bSIN[d]])
        for ec in range(4):
            es = slice(ec * 128, (ec + 1) * 128)
            for half in range(2):
                hs = slice(half * 512, (half + 1) * 512)
                ps, bps = nextbank()
                n = 0
                for d, QGx, bQGx in ((0, QGf, bQGf), (1, QGb, bQGb)):
                    for dkc in range(2):
                        P.op("pe", (lambda e, ps=ps, d=d, dkc=dkc, es=es, hs=hs, QGx=QGx, n=n: e.matmul(
                            ps[:, :], lhsT=SIN[d][:, dkc, es], rhs=QGx[:, dkc, hs], start=(n == 0), stop=(n == 3))),
                            rd=[bSIN[d], bQGx], wr=[bps] if n == 3 else [],
                            waits=([bps.w] + bps.r) if n == 0 else [], sig=(n == 3))
                        n += 1
                P.op("dve", (lambda e, ps=ps, ec=ec, hs=hs: e.tensor_tensor(out=OH[:, ec, hs], in0=OH[:, ec, hs],
                                                                           in1=ps[:, :], op=ALU.add)),
                     rd=[bps], wr=[bOH])
        for half in range(2):
            hs = slice(half * 512, (half + 1) * 512)
            ps, bps = nextbank()
            for ec in range(4):
                i = ec % 2
                P.op("dve", (lambda e, ec=ec, i=i, hs=hs: e.tensor_tensor(out=SQ[i][:, :], in0=OH[:, ec, hs],
                                                                         in1=OH[:, ec, hs], op=ALU.mult)),
                     rd=[bOH], wr=[bSQ[i]])
                P.op("pe", (lambda e, ps=ps, ec=ec, i=i: e.matmul(ps[:, :], lhsT=k.ones[:, :], rhs=SQ[i][:, :],
                                                                  start=(ec == 0), stop=(ec == 3))),
                     rd=[bSQ[i], k.bconst], wr=[bps] if ec == 3 else [],
                     waits=([bps.w] + bps.r) if ec == 0 else [], sig=True)
            P.op("act", (lambda e, ps=ps, hs=hs: e.activation(out=RS[:, hs], in_=ps[:, :], func=AF.Sqrt,
                                                              bias=k.epsc[:, 0:1], scale=1.0 / 512)),
                 rd=[bps, k.bconst], wr=[bRS])
            P.op("dve", (lambda e, hs=hs: e.reciprocal(out=RS[:, hs], in_=RS[:, hs])), rd=[bRS], wr=[bRS])
        for gpi in range(2):
            i = h * 2 + gpi
            s = i % 2
            for f in range(2):
                ec = gpi * 2 + f
                for half in range(2):
                    hs = slice(half * 512, (half + 1) * 512)
                    ps, bps = nextbank()
                    for kc in range(KC):
                        P.op("pe", (lambda e, ps=ps, s=s, kc=kc, f=f, hs=hs: e.matmul(
                            ps[:, :], lhsT=WS[s][:, kc, f * 128:(f + 1) * 128], rhs=hT[:, kc, hs],
                            start=(kc == 0), stop=(kc == KC - 1))),
                            rd=[bWS[s], bh[half]] if kc in (0, KC - 1) else [], wr=[bps] if kc == KC - 1 else [],
                            waits=([bps.w] + bps.r) if kc == 0 else [], sig=(kc == KC - 1))
                    gi2 = (ec * 2 + half) % 2
                    P.op("act", (lambda e, ps=ps, gi2=gi2: e.activation(out=SG[gi2][:, :], in_=ps[:, :], func=AF.Silu)),
                         rd=[bps], wr=[bSG[gi2]])
                    P.op("dve", (lambda e, ec=ec, hs=hs: e.tensor_tensor(out=TT[:, :], in0=OH[:, ec, hs], in1=RS[:, hs],
                                                                        op=ALU.mult)), rd=[bOH, bRS], wr=[bTT])
                    P.op("dve", (lambda e, ec=ec, hs=hs, gi2=gi2, h=h: e.scalar_tensor_tensor(
                        out=ON[:, h * 4 + ec, hs], in0=TT[:, :], scalar=sm(k, "gng", ec, 1), in1=SG[gi2][:, :],
                        op0=ALU.mult, op1=ALU.mult)), rd=[bTT, bSG[gi2], k.bsm], wr=[bON])
            load_gpiece()
    P.barrier()
    if GLA_STOP == 7:
        return
    WO = [carve(k, 49152 + s * 8192, [128, KC, 256], BF16) for s in range(2)]
    bWO = [Buf(), Buf()]
    dWO = [getds(k, "wo0"), getds(k, "wo1")]

    def load_wo(g):
        P.dma("pool", WO[g % 2][:, :, :], wout[:, :, g * 256:(g + 1) * 256], dWO[g % 2], wr=[bWO[g % 2]])

    load_wo(0)
    load_wo(1)
    ou = 0
    for g in range(8):
        s = g % 2
        for fi in range(2):
            dc = g * 2 + fi
            for half in range(2):
                hs = slice(half * 512, (half + 1) * 512)
                b = 6 + ou % 2
                ou += 1
                yps, byps = k.pb[b], k.bpb[b]
                for kc in range(KC):
                    P.op("pe", (lambda e, yps=yps, s=s, kc=kc, fi=fi, hs=hs: e.matmul(
                        yps[:, :], lhsT=WO[s][:, kc, fi * 128:(fi + 1) * 128], rhs=ON[:, kc, hs],
                        start=(kc == 0), stop=(kc == KC - 1))),
                        rd=[bWO[s], bON] if kc in (0, KC - 1) else [], wr=[byps] if kc == KC - 1 else [],
                        waits=([byps.w] + byps.r) if kc == 0 else [], sig=(kc == KC - 1))
                P.op("dve", (lambda e, yps=yps, dc=dc, hs=hs: e.scalar_tensor_tensor(
                    out=k.xT[:, dc, hs], in0=yps[:, :], scalar=k.modG[:, L, 1, dc:dc + 1], in1=k.xT[:, dc, hs],
                    op0=ALU.mult, op1=ALU.add)), rd=[byps, k.bmods], wr=[k.bx])
        if g + 2 < 8:
            load_wo(g + 2)


def pool_stage(k, L):
    P, nc = k.P, k.nc
    P.barrier()
    HW = 1040
    H = carve(k, 0, [128, KC, HW], F32)
    B0 = 66560
    A_ = [carve(k, B0 + i * 8448, [128, 4, 528], F32) for i in range(2)]
    B1 = B0 + 2 * 8448
    DBF = [carve(k, B1 + i * 4096, [128, 4, 512], BF16) for i in range(2)]
    B2 = B1 + 8192
    PW = carve(k, B2, [128, 4, 4, 512], BF16)
    B3 = B2 + 16384
    V = carve(k, B3, [128, HW], F32)
    VA = [carve(k, B3 + 4160 + i * 2112, [128, 528], F32) for i in range(2)]
    INV = carve(k, B3 + 4160 + 4224, [128, 512], F32)
    SBh = carve(k, B3 + 10432, [128, KC, 16], F32)
    RBh = carve(k, B3 + 11456, [128, 2, KC, 16], F32)
    TMP = [carve(k, B3 + 13504 + i * 2048, [128, 512], F32) for i in range(2)]
    NB = B0
    bH, bA, bD, bPW, bV, bVA, bINV, bT = Buf("H"), [Buf(), Buf()], [Buf(), Buf()], Buf(), Buf(), [Buf(), Buf()], Buf(), [Buf(), Buf()]
    P.dma("pool", PW.rearrange("p g i o -> p (g i) o"),
          k.pool_w[0].rearrange("g (i p) o -> p (g i) o", p=128), getds(k, "pw"), wr=[bPW])
    norm_modulate(k, lambda kc: k.modA[:, L, 1, kc:kc + 1],
                  lambda kc: k.mods[:, L, 48 + kc:48 + kc + 1],
                  lambda kc, half: H[:, kc, 8 + half * 512:8 + (half + 1) * 512], NB, bout=bH)
    cc_in = nc.dram_tensor("plh_in", [128, 256], F32)
    cc_out = nc.dram_tensor("plh_out", [256, 256], F32)
    bSB, bRB, bcin, bcout = Buf(), Buf(), Buf(), Buf()
    P.op("dve", lambda e: e.tensor_copy(out=SBh[:, :, 0:8], in_=H[:, :, 8:16]), rd=[bH], wr=[bSB])
    P.op("dve", lambda e: e.tensor_copy(out=SBh[:, :, 8:16], in_=H[:, :, 1024:1032]), rd=[bH], wr=[bSB])
    P.dma("sp", cc_in.ap(), SBh.rearrange("p a b -> p (a b)"), getds(k, "halo"), rd=[bSB], wr=[bcin])
    P.op("pool", lambda e: e.collective_compute("AllGather", ALU.bypass, replica_groups=PAIRS,
                                                ins=[cc_in.ap().opt()], outs=[cc_out.ap().opt()]),
         rd=[bcin], wr=[bcout])
    P.dma("sp", RBh.rearrange("p r a b -> p r (a b)"), cc_out.ap().rearrange("(r p) n -> p r n", p=128),
          getds(k, "halo"), rd=[bcout], wr=[bRB])
    P.op("dve", lambda e: e.tensor_scalar(out=H[:, :, 0:8], in0=RBh[:, 0, :, 8:16], scalar1=sm(k, "is_odd"),
                                          scalar2=None, op0=ALU.mult), rd=[bRB, k.bsm], wr=[bH])
    P.op("dve", lambda e: e.tensor_scalar(out=H[:, :, 1032:1040], in0=RBh[:, 1, :, 0:8], scalar1=sm(k, "is_even"),
                                          scalar2=None, op0=ALU.mult), rd=[bRB, k.bsm], wr=[bH])
    P.op("dve", lambda e: e.memset(V[:, :], 1.0), wr=[bV])
    P.op("dve", lambda e: e.tensor_copy(out=V[:, 0:8], in_=sm(k, "is_odd").to_broadcast([128, 8])), rd=[k.bsm], wr=[bV])
    P.op("dve", lambda e: e.tensor_copy(out=V[:, 1032:1040], in_=sm(k, "is_even").to_broadcast([128, 8])),
         rd=[k.bsm], wr=[bV])
    u = 0
    for gi in range(4):
        w = 2 << gi
        cs = slice(gi * 4, gi * 4 + 4)
        for half in range(2):
            b0 = half * 512 + 8 - w // 2
            n = 512 + w - 1
            src, bsrc = H[:, cs, b0:b0 + n], bH
            vsrc, bvsrc = V[:, b0:b0 + n], bV
            st = 1
            lvl = 0
            while st < w:
                dst, bdst = A_[lvl % 2], bA[lvl % 2]
                vdst, bvdst = VA[lvl % 2], bVA[lvl % 2]
                nn = n - st
                P.op("dve", (lambda e, dst=dst, src=src, nn=nn, st=st: e.tensor_tensor(
                    out=dst[:, :, 0:nn], in0=src[:, :, 0:nn], in1=src[:, :, st:st + nn], op=ALU.add)),
                    rd=[bsrc], wr=[bdst])
                P.op("dve", (lambda e, vdst=vdst, vsrc=vsrc, nn=nn, st=st: e.tensor_tensor(
                    out=vdst[:, 0:nn], in0=vsrc[:, 0:nn], in1=vsrc[:, st:st + nn], op=ALU.add)),
                    rd=[bvsrc], wr=[bvdst])
                src, bsrc = dst, bdst
                vsrc, bvsrc = vdst, bvdst
                n = nn
                st *= 2
                lvl += 1
            assert n == 512
            P.op("dve", (lambda e, vsrc=vsrc: e.reciprocal(out=INV[:, :], in_=vsrc[:, 0:512])), rd=[bvsrc], wr=[bINV])
            inv3 = INV[:, :].unsqueeze(1).to_broadcast([128, 4, 512])
            P.op("dve", (lambda e, src=src, inv3=inv3: e.tensor_tensor(out=src[:, :, 0:512], in0=src[:, :, 0:512],
                                                                      in1=inv3, op=ALU.mult)),
                 rd=[bINV], wr=[bsrc])
            di = u % 2
            u += 1
            P.op("dve", (lambda e, src=src, di=di, cs=cs, half=half: e.tensor_tensor(
                out=DBF[di][:, :, :], in0=src[:, :, 0:512], in1=H[:, cs, 8 + half * 512:8 + (half + 1) * 512],
                op=ALU.subtract)), rd=[bsrc, bH], wr=[bD[di]])
            hs = slice(half * 512, (half + 1) * 512)
            for oc in range(4):
                ch = gi * 4 + oc
                b = 6 + (u * 4 + oc) % 2
                yps, byps = k.pb[b], k.bpb[b]
                for ic in range(4):
                    P.op("pe", (lambda e, yps=yps, gi=gi, ic=ic, oc=oc, di=di: e.matmul(
                        yps[:, :], lhsT=PW[:, gi, ic, oc * 128:(oc + 1) * 128], rhs=DBF[di][:, ic, :],
                        start=(ic == 0), stop=(ic == 3))),
                        rd=[bPW, bD[di]] if ic in (0, 3) else [],
                        wr=[byps] if ic == 3 else [],
                        waits=([byps.w] + byps.r) if ic == 0 else [], sig=(ic == 3))
                ti = oc % 2
                P.op("dve", (lambda e, yps=yps, ch=ch, ti=ti: e.tensor_scalar(
                    out=TMP[ti][:, :], in0=yps[:, :], scalar1=sm(k, "pb", ch, 1), scalar2=sm(k, "pscale", ch, 1),
                    op0=ALU.add, op1=ALU.mult)), rd=[byps, k.bsm], wr=[bT[ti]])
                P.op("dve", (lambda e, ch=ch, ti=ti, hs=hs: e.scalar_tensor_tensor(
                    out=k.xT[:, ch, hs], in0=TMP[ti][:, :], scalar=k.modG[:, L, 1, ch:ch + 1], in1=k.xT[:, ch, hs],
                    op0=ALU.mult, op1=ALU.add)), rd=[bT[ti], k.bmods], wr=[k.bx])


def epilogue(k, last):
    P = k.P
    P.barrier()
    ov = k.out.rearrange("(kc p) t -> p kc t", p=128)
    if last:
        OUT = carve(k, 0, [128, KC, T], F32)
        bo = Buf("out")
        norm_modulate(k, lambda kc: sm(k, "final_g", kc, 1), None,
                      lambda kc, half: OUT[:, kc, half * 512:(half + 1) * 512], 65536, bout=bo)
        src, bsrc = OUT, bo
    else:
        src, bsrc = k.xT, k.bx
    toks = []
    for q in range(4):
        toks.append(P.dma("sp", ov[:, q * 4:(q + 1) * 4, :], src[:, q * 4:(q + 1) * 4, :], getds(k, "xout"), rd=[bsrc]))
    P.wait_only("sp", toks[-1:])


_CACHE = {}
LAUNCH_GROUPS = [[0, 1, 2, 3]]


def weights_for(inp, grp, mixers):
    w = {}
    g = list(grp)
    for n in ("ada_w", "ffn_w_gate", "ffn_w_up", "ffn_w_down"):
        w[n] = np.ascontiguousarray(np.asarray(inp[n], np.float32)[g])
    convL = [L for L in g if L % 3 == 0 and mixers]
    if convL:
        cj = [L // 3 for L in convL]
        w["conv_w_in"] = np.ascontiguousarray(np.asarray(inp["conv_w_in"], np.float32)[cj])
        w["conv_w_out"] = np.ascontiguousarray(np.asarray(inp["conv_w_out"], np.float32)[cj])
    if 1 in g and mixers:
        for n in ("gla_w_in", "gla_wa1", "gla_wa2", "gla_w_out"):
            w[n] = np.ascontiguousarray(np.asarray(inp[n], np.float32))
    if 2 in g and mixers:
        w["pool_w"] = np.ascontiguousarray(np.asarray(inp["pool_w"], np.float32))
    return w


def run_groups(inp, groups, mixers=True, final=True):
    x = np.asarray(inp["x"], np.float32)
    smalls = [pack_smalls(inp, c) for c in range(NCORES)]
    xT = [np.ascontiguousarray(x[c // 2, (c % 2) * T:(c % 2 + 1) * T, :].T) for c in range(NCORES)]
    for gi, grp in enumerate(groups):
        lastg = final and gi == len(groups) - 1
        key = (tuple(grp), lastg, mixers)
        if key not in _CACHE:
            _CACHE[key] = build_program(list(grp), gi == 0, lastg, mixers)
        nc = _CACHE[key]
        w = weights_for(inp, grp, mixers)
        in_maps = []
        if ADA_SPLIT:
            ada_half = [np.ascontiguousarray(w["ada_w"][:, :, r * 9216:(r + 1) * 9216]) for r in range(2)]
        for c in range(NCORES):
            m = {"xT_in": xT[c], "smalls": smalls[c]}
            m.update(w)
            if ADA_SPLIT:
                m["ada_w"] = ada_half[c % 2]
            in_maps.append(m)
        res = run_bass_kernel_spmd(nc, in_maps, core_ids=list(range(NCORES)))
        xT = [np.ascontiguousarray(res.results[c]["outT"]) for c in range(NCORES)]
    out = np.empty((4, SEQ, D), np.float32)
    for c in range(NCORES):
        out[c // 2, (c % 2) * T:(c % 2 + 1) * T, :] = xT[c].T
    return out


def kernel(**inputs):
    inp = {n: np.asarray(v) for n, v in inputs.items()}
    return run_groups(inp, LAUNCH_GROUPS, True, True)
```
